# Optimizing a Trainium2 kernel written in Bass

```python
import math, functools
import jax, jax.numpy as jnp
from jax import lax
import numpy as np

D_MODEL = 1024
BATCH = 2
SEQ = 8192
DEPTH = 2
DEC_BATCH = 128
DEC_SEQ = 8
PAST_LEN = 8192
PAGE_SIZE = 128

N_META = 16
CONV_WIDTH = 4
CHUNK = 64
BLOCK = 128
FRONT_PAD = BLOCK - N_META

DN_HEADS = 4
DN_DK = 128
DN_DV = 128
DN_QK = DN_HEADS * DN_DK
DN_V = DN_HEADS * DN_DV
DN_CONV = 2 * DN_QK + DN_V
SSM_HEADS = 4
SSM_HEADDIM = 64
SSM_GROUPS = 2
SSM_STATE = 128
SSM_INNER = SSM_HEADS * SSM_HEADDIM
SSM_BC = SSM_GROUPS * SSM_STATE
SSM_CONV = SSM_INNER + 2 * SSM_BC
SWA_Q_HEADS = 4
SWA_KV_HEADS = 2
SWA_HEAD_DIM = 64
SWA_Q = SWA_Q_HEADS * SWA_HEAD_DIM
SWA_KV = SWA_KV_HEADS * SWA_HEAD_DIM
WINDOW = 128
ROPE_THETA = 10000.0

D_MIX = DN_V + SSM_INNER + SWA_Q
D_FF = -(-8 * D_MODEL // (3 * 256)) * 256
IN_WIDTHS = (DN_CONV, DN_V, DN_HEADS, DN_HEADS, SSM_CONV, SSM_INNER, SSM_HEADS, SWA_Q, SWA_KV, SWA_KV)
D_IN = sum(IN_WIDTHS)

kernel_name = "hymba_style_deltanet_ssd_swa_decoder_step"


def rmsnorm(x, g, eps=1e-6):
    xf = x.astype(jnp.float32)
    y = xf * lax.rsqrt(jnp.mean(xf * xf, axis=-1, keepdims=True) + eps)
    return (y * g.astype(jnp.float32)).astype(x.dtype)


def l2norm(x, eps=1e-6):
    xf = x.astype(jnp.float32)
    return (xf * lax.rsqrt(jnp.sum(xf * xf, axis=-1, keepdims=True) + eps)).astype(x.dtype)


def split_cols(a, widths):
    return jnp.split(a, np.cumsum(widths)[:-1].tolist(), axis=-1)


def pad_front(a, n):
    return jnp.pad(a, [(0, 0), (n, 0)] + [(0, 0)] * (a.ndim - 2))


def causal_conv(u, buf, w):
    t = u.shape[1]
    full = jnp.concatenate([buf.astype(u.dtype), u], axis=1)
    out = full[:, 0:t] * w[0]
    for i in range(1, CONV_WIDTH):
        out = out + full[:, i:i + t] * w[i]
    return out, full[:, t:]


def rope(x, pos):
    half = x.shape[-1] // 2
    inv = ROPE_THETA ** (-jnp.arange(half, dtype=jnp.float32) / half)
    ang = pos.astype(jnp.float32)[:, None] * inv[None, :]
    cos = jnp.cos(ang)[None, :, None, :]
    sin = jnp.sin(ang)[None, :, None, :]
    xf = x.astype(jnp.float32)
    x1, x2 = xf[..., :half], xf[..., half:]
    return jnp.concatenate([x1 * cos - x2 * sin, x2 * cos + x1 * sin], axis=-1).astype(x.dtype)


def to_chunks(a, c):
    b, t = a.shape[:2]
    return jnp.swapaxes(a.reshape((b, t // c, c) + a.shape[2:]), 2, 3)


def from_chunks(a):
    a = jnp.swapaxes(a, 2, 3)
    return a.reshape((a.shape[0], a.shape[1] * a.shape[2]) + a.shape[3:])


def segment_decay(gc):
    c = gc.shape[-1]
    causal = jnp.tril(jnp.ones((c, c), bool))
    diff = gc[..., :, None] - gc[..., None, :]
    return jnp.where(causal, jnp.exp(jnp.where(causal, diff, 0.0)), 0.0)


def gated_delta_rule(q, k, v, g, beta, s0, chunk):
    f32 = jnp.float32
    dv = v.shape[-1]
    qc, kc, vc = (to_chunks(a.astype(f32), chunk) for a in (q, k, v))
    gc = jnp.cumsum(to_chunks(g.astype(f32), chunk), axis=-1)
    bc = to_chunks(beta.astype(f32), chunk)
    decay = segment_decay(gc)
    strict = jnp.tril(jnp.ones((chunk, chunk), bool), -1)
    a_mat = jnp.where(strict, bc[..., :, None] * jnp.einsum('bnhid,bnhjd->bnhij', kc, kc) * decay, 0.0)
    m = a_mat + jnp.eye(chunk, dtype=f32)
    rhs = jnp.concatenate([vc * bc[..., None], kc * (bc * jnp.exp(gc))[..., None]], axis=-1)
    sol = lax.linalg.triangular_solve(m, rhs, left_side=True, lower=True, unit_diagonal=True)
    u, w = sol[..., :dv], sol[..., dv:]
    attn = jnp.einsum('bnhid,bnhjd->bnhij', qc, kc) * decay
    q_dec = qc * jnp.exp(gc)[..., None]
    k_dec = kc * jnp.exp(gc[..., -1:] - gc)[..., None]
    g_last = jnp.exp(gc[..., -1])

    def step(s, inp):
        u_c, w_c, attn_c, qd_c, kd_c, gl_c = inp
        v_new = u_c - jnp.einsum('bhcd,bhde->bhce', w_c, s)
        o_c = jnp.einsum('bhcd,bhde->bhce', qd_c, s) + jnp.einsum('bhij,bhje->bhie', attn_c, v_new)
        s = s * gl_c[..., None, None] + jnp.einsum('bhcd,bhce->bhde', kd_c, v_new)
        return s, o_c

    xs = tuple(jnp.moveaxis(a, 1, 0) for a in (u, w, attn, q_dec, k_dec, g_last))
    s_final, o = lax.scan(step, s0.astype(f32), xs)
    return from_chunks(jnp.moveaxis(o, 0, 1)).astype(v.dtype), s_final


def ssd_scan(x, dt, a_neg, b_in, c_in, h0, chunk):
    f32 = jnp.float32
    rep = SSM_HEADS // SSM_GROUPS
    bc = to_chunks(jnp.repeat(b_in.astype(f32), rep, axis=2), chunk)
    cc = to_chunks(jnp.repeat(c_in.astype(f32), rep, axis=2), chunk)
    xdt = to_chunks(x.astype(f32) * dt.astype(f32)[..., None], chunk)
    gc = jnp.cumsum(to_chunks(dt.astype(f32) * a_neg.astype(f32), chunk), axis=-1)
    attn = jnp.einsum('bnhis,bnhjs->bnhij', cc, bc) * segment_decay(gc)
    y_intra = jnp.einsum('bnhij,bnhjp->bnhip', attn, xdt)
    c_dec = cc * jnp.exp(gc)[..., None]
    b_dec = bc * jnp.exp(gc[..., -1:] - gc)[..., None]
    g_last = jnp.exp(gc[..., -1])

    def step(h, inp):
        cd, bd, xd, gl = inp
        y_c = jnp.einsum('bhcs,bhps->bhcp', cd, h)
        h = h * gl[..., None, None] + jnp.einsum('bhcp,bhcs->bhps', xd, bd)
        return h, y_c

    xs = tuple(jnp.moveaxis(a, 1, 0) for a in (c_dec, b_dec, xdt, g_last))
    h_final, y_inter = lax.scan(step, h0.astype(f32), xs)
    y = y_intra + jnp.moveaxis(y_inter, 0, 1)
    return from_chunks(y).astype(x.dtype), h_final


def sink_softmax(scores, mask, sinks):
    s = jnp.where(mask, scores.astype(jnp.float32), -jnp.inf)
    sink = jnp.broadcast_to(sinks.astype(jnp.float32)[:, :, None, None], s.shape[:-1] + (1,))
    p = jax.nn.softmax(jnp.concatenate([s, sink], axis=-1), axis=-1)
    return p[..., :-1]


def swa_prompt(q, k, v, sinks):
    b, L = q.shape[:2]
    grp = SWA_Q_HEADS // SWA_KV_HEADS
    lp = L + FRONT_PAD
    nb = lp // BLOCK
    qb = pad_front(q, FRONT_PAD).reshape(b, nb, BLOCK, SWA_KV_HEADS, grp, SWA_HEAD_DIM)

    def band(a):
        ap = pad_front(a, FRONT_PAD + BLOCK).reshape(b, nb + 1, BLOCK, SWA_KV_HEADS, SWA_HEAD_DIM)
        return jnp.concatenate([ap[:, :-1], ap[:, 1:]], axis=2)

    kb, vb = band(k), band(v)
    scores = jnp.einsum('bnqkgd,bnskd->bnkgqs', qb, kb) * SWA_HEAD_DIM ** -0.5
    qi = jnp.arange(BLOCK)[:, None] + BLOCK
    kj = jnp.arange(2 * BLOCK)[None, :]
    rel = qi - kj
    kpos = (jnp.arange(nb)[:, None, None] - 1) * BLOCK + kj[None]
    mask = (rel >= 0) & (rel < WINDOW) & (kpos >= FRONT_PAD)
    p = sink_softmax(scores, mask[None, :, None, None], sinks.reshape(SWA_KV_HEADS, grp))
    o = jnp.einsum('bnkgqs,bnskd->bnqkgd', p.astype(vb.dtype), vb)
    o = o.reshape(b, lp, SWA_Q_HEADS, SWA_HEAD_DIM)[:, FRONT_PAD:]
    return o, k[:, -WINDOW:], v[:, -WINDOW:]


def swa_sample(q, k, v, sinks, k_buf, v_buf):
    t = q.shape[1]
    grp = SWA_Q_HEADS // SWA_KV_HEADS
    keys = jnp.concatenate([k_buf.astype(k.dtype), k], axis=1)
    vals = jnp.concatenate([v_buf.astype(v.dtype), v], axis=1)
    qg = q.reshape(q.shape[0], t, SWA_KV_HEADS, grp, SWA_HEAD_DIM)
    scores = jnp.einsum('btkgd,bskd->bkgts', qg, keys) * SWA_HEAD_DIM ** -0.5
    rel = (WINDOW + jnp.arange(t))[:, None] - jnp.arange(WINDOW + t)[None, :]
    mask = (rel >= 0) & (rel < WINDOW)
    p = sink_softmax(scores, mask, sinks.reshape(SWA_KV_HEADS, grp))
    o = jnp.einsum('bkgts,bskd->btkgd', p.astype(vals.dtype), vals).reshape(q.shape)
    return o, keys[:, -WINDOW:], vals[:, -WINDOW:]


def hybrid_mixer(h, pos, lw, states, front_pad, chunk, attend):
    dn_s0, dn_buf, ssm_s0, ssm_buf = states
    f32 = jnp.float32
    bsz, t = h.shape[:2]
    (dn_qkv, dn_z, dn_b, dn_a, ssm_xbc, ssm_z, ssm_dt, sw_q, sw_k, sw_v) = split_cols(h @ lw['w_in'], IN_WIDTHS)
    qkv, dn_buf_new = causal_conv(dn_qkv, dn_buf, lw['dn_conv_w'])
    q, k, v = split_cols(jax.nn.silu(qkv), (DN_QK, DN_QK, DN_V))
    q = l2norm(q.reshape(bsz, t, DN_HEADS, DN_DK)) * DN_DK ** -0.5
    k = l2norm(k.reshape(bsz, t, DN_HEADS, DN_DK))
    v = v.reshape(bsz, t, DN_HEADS, DN_DV)
    beta = jax.nn.sigmoid(dn_b.astype(f32))
    g = -jnp.exp(lw['dn_a_log'].astype(f32)) * jax.nn.softplus(dn_a.astype(f32) + lw['dn_dt_bias'].astype(f32))
    o_dn, dn_s = gated_delta_rule(*(pad_front(a, front_pad) for a in (q, k, v, g, beta)), dn_s0, chunk)
    o_dn = rmsnorm(o_dn[:, front_pad:], lw['dn_norm_w']) * jax.nn.silu(dn_z.reshape(bsz, t, DN_HEADS, DN_DV))
    xbc, ssm_buf_new = causal_conv(ssm_xbc, ssm_buf, lw['ssm_conv_w'])
    xs, bs, cs = split_cols(jax.nn.silu(xbc + lw['ssm_conv_b']), (SSM_INNER, SSM_BC, SSM_BC))
    xs = xs.reshape(bsz, t, SSM_HEADS, SSM_HEADDIM)
    bs = bs.reshape(bsz, t, SSM_GROUPS, SSM_STATE)
    cs = cs.reshape(bsz, t, SSM_GROUPS, SSM_STATE)
    dt = jax.nn.softplus(ssm_dt.astype(f32) + lw['ssm_dt_bias'].astype(f32))
    a_neg = -jnp.exp(lw['ssm_a_log'].astype(f32))
    y, ssm_s = ssd_scan(pad_front(xs, front_pad), pad_front(dt, front_pad), a_neg,
                        pad_front(bs, front_pad), pad_front(cs, front_pad), ssm_s0, chunk)
    y = (y[:, front_pad:] + xs * lw['ssm_d'][:, None]) * jax.nn.silu(ssm_z.reshape(bsz, t, SSM_HEADS, SSM_HEADDIM))
    y = rmsnorm(y.reshape(bsz, t, SSM_GROUPS, SSM_INNER // SSM_GROUPS),
                lw['ssm_norm_w'].reshape(SSM_GROUPS, SSM_INNER // SSM_GROUPS))
    qa = rope(sw_q.reshape(bsz, t, SWA_Q_HEADS, SWA_HEAD_DIM), pos)
    ka = rope(sw_k.reshape(bsz, t, SWA_KV_HEADS, SWA_HEAD_DIM), pos)
    va = sw_v.reshape(bsz, t, SWA_KV_HEADS, SWA_HEAD_DIM)
    o_sw, k_new, v_new = attend(qa, ka, va, lw['swa_sinks'])
    mixed = jnp.concatenate([o_dn.reshape(bsz, t, DN_V), y.reshape(bsz, t, SSM_INNER),
                             o_sw.reshape(bsz, t, SWA_Q)], axis=-1)
    return mixed @ lw['w_out'], (dn_s, dn_buf_new, ssm_s, ssm_buf_new, k_new, v_new)


def trunk_layer(x, pos, lw, states, front_pad, chunk, attend):
    m, new_states = hybrid_mixer(rmsnorm(x, lw['g_pre_mix']), pos, lw, states, front_pad, chunk, attend)
    x = x + rmsnorm(m, lw['g_post_mix'])
    gate, up = jnp.split(rmsnorm(x, lw['g_pre_ffn']) @ lw['w_ffn_in'], 2, axis=-1)
    x = x + rmsnorm((jax.nn.silu(gate) * up) @ lw['w_ffn_out'], lw['g_post_ffn'])
    return x, new_states


def setup_inputs(seed: int = 0) -> dict:
    key = jax.random.key(seed)
    ks = jax.random.split(key, 32)
    f32 = jnp.float32

    def normal(kk, shape, scale):
        return jax.random.normal(kk, shape, f32) * scale

    def gain(kk, shape):
        return 1.0 + normal(kk, shape, 0.02)

    def dt_bias(kk, shape):
        dt = jnp.exp(jax.random.uniform(kk, shape, f32, math.log(1e-3), math.log(1e-1)))
        return dt + jnp.log(-jnp.expm1(-dt))

    def a_log(kk, shape):
        return jnp.log(jax.random.uniform(kk, shape, f32, 1.0, 16.0))

    return {
        "x_prompt": normal(ks[0], (BATCH, SEQ, D_MODEL), 1.0),
        "x_sample": normal(ks[1], (DEC_BATCH, DEC_SEQ, D_MODEL), 1.0),
        "state_dn": normal(ks[2], (DEPTH, DEC_BATCH, DN_HEADS, DN_DK, DN_DV), 0.1),
        "state_dn_conv": normal(ks[3], (DEPTH, DEC_BATCH, CONV_WIDTH - 1, DN_CONV), 1.0),
        "state_ssm": normal(ks[4], (DEPTH, DEC_BATCH, SSM_HEADS, SSM_HEADDIM, SSM_STATE), 0.1),
        "state_ssm_conv": normal(ks[5], (DEPTH, DEC_BATCH, CONV_WIDTH - 1, SSM_CONV), 1.0),
        "cache_swa_k": normal(ks[6], (DEPTH, DEC_BATCH, WINDOW, SWA_KV_HEADS, SWA_HEAD_DIM), 1.0),
        "cache_swa_v": normal(ks[7], (DEPTH, DEC_BATCH, WINDOW, SWA_KV_HEADS, SWA_HEAD_DIM), 1.0),
        "meta_tokens": normal(ks[8], (N_META, D_MODEL), 1.0),
        "w_in": normal(ks[9], (DEPTH, D_MODEL, D_IN), D_MODEL ** -0.5),
        "dn_conv_w": normal(ks[10], (DEPTH, CONV_WIDTH, DN_CONV), CONV_WIDTH ** -0.5),
        "dn_a_log": a_log(ks[11], (DEPTH, DN_HEADS)),
        "dn_dt_bias": dt_bias(ks[12], (DEPTH, DN_HEADS)),
        "dn_norm_w": gain(ks[13], (DEPTH, DN_DV)),
        "ssm_conv_w": normal(ks[14], (DEPTH, CONV_WIDTH, SSM_CONV), CONV_WIDTH ** -0.5),
        "ssm_conv_b": normal(ks[15], (DEPTH, SSM_CONV), 0.02),
        "ssm_a_log": a_log(ks[16], (DEPTH, SSM_HEADS)),
        "ssm_dt_bias": dt_bias(ks[17], (DEPTH, SSM_HEADS)),
        "ssm_d": 1.0 + normal(ks[18], (DEPTH, SSM_HEADS), 0.1),
        "ssm_norm_w": gain(ks[19], (DEPTH, SSM_INNER)),
        "swa_sinks": normal(ks[20], (DEPTH, SWA_Q_HEADS), 0.5),
        "w_out": normal(ks[21], (DEPTH, D_MIX, D_MODEL), D_MIX ** -0.5),
        "g_pre_mix": gain(ks[22], (DEPTH, D_MODEL)),
        "g_post_mix": gain(ks[23], (DEPTH, D_MODEL)),
        "g_pre_ffn": gain(ks[24], (DEPTH, D_MODEL)),
        "g_post_ffn": gain(ks[25], (DEPTH, D_MODEL)),
        "w_ffn_in": normal(ks[26], (DEPTH, D_MODEL, 2 * D_FF), D_MODEL ** -0.5),
        "w_ffn_out": normal(ks[27], (DEPTH, D_FF, D_MODEL), D_FF ** -0.5),
    }


def reference(x_prompt, x_sample, state_dn, state_dn_conv, state_ssm, state_ssm_conv, cache_swa_k, cache_swa_v,
              meta_tokens, w_in, dn_conv_w, dn_a_log, dn_dt_bias, dn_norm_w, ssm_conv_w, ssm_conv_b, ssm_a_log,
              ssm_dt_bias, ssm_d, ssm_norm_w, swa_sinks, w_out, g_pre_mix, g_post_mix, g_pre_ffn, g_post_ffn,
              w_ffn_in, w_ffn_out):
    bp = x_prompt.shape[0]
    meta = jnp.broadcast_to(meta_tokens[None].astype(x_prompt.dtype), (bp, N_META, D_MODEL))
    xp = jnp.concatenate([meta, x_prompt], axis=1)
    xs = x_sample
    pos_p = jnp.arange(xp.shape[1], dtype=jnp.int32)
    pos_s = PAST_LEN + jnp.arange(xs.shape[1], dtype=jnp.int32)
    zero_states = (jnp.zeros((bp, DN_HEADS, DN_DK, DN_DV), jnp.float32),
                   jnp.zeros((bp, CONV_WIDTH - 1, DN_CONV), xp.dtype),
                   jnp.zeros((bp, SSM_HEADS, SSM_HEADDIM, SSM_STATE), jnp.float32),
                   jnp.zeros((bp, CONV_WIDTH - 1, SSM_CONV), xp.dtype))
    new_p, new_s = [], []
    for l in range(DEPTH):
        lw = dict(w_in=w_in[l], dn_conv_w=dn_conv_w[l], dn_a_log=dn_a_log[l], dn_dt_bias=dn_dt_bias[l],
                  dn_norm_w=dn_norm_w[l], ssm_conv_w=ssm_conv_w[l], ssm_conv_b=ssm_conv_b[l],
                  ssm_a_log=ssm_a_log[l], ssm_dt_bias=ssm_dt_bias[l], ssm_d=ssm_d[l], ssm_norm_w=ssm_norm_w[l],
                  swa_sinks=swa_sinks[l], w_out=w_out[l], g_pre_mix=g_pre_mix[l], g_post_mix=g_post_mix[l],
                  g_pre_ffn=g_pre_ffn[l], g_post_ffn=g_post_ffn[l], w_ffn_in=w_ffn_in[l], w_ffn_out=w_ffn_out[l])
        xp, st_p = trunk_layer(xp, pos_p, lw, zero_states, FRONT_PAD, CHUNK, swa_prompt)
        xs, st_s = trunk_layer(xs, pos_s, lw, (state_dn[l], state_dn_conv[l], state_ssm[l], state_ssm_conv[l]),
                               0, xs.shape[1],
                               functools.partial(swa_sample, k_buf=cache_swa_k[l], v_buf=cache_swa_v[l]))
        new_p.append(st_p)
        new_s.append(st_s)
    dn_p, dnc_p, ssm_p, ssmc_p, k_p, v_p = (jnp.stack([st[i] for st in new_p]) for i in range(6))
    dn_s, dnc_s, ssm_s, ssmc_s, k_s, v_s = (jnp.stack([st[i] for st in new_s]) for i in range(6))
    return (xp[:, N_META:], xs, dn_p, dnc_p, ssm_p, ssmc_p, k_p, v_p, dn_s, dnc_s, ssm_s, ssmc_s, k_s, v_s)
```

```python
from contextlib import ExitStack
import types
import numpy as np
import concourse.bass as bass
import concourse.mybir as mybir
from concourse.bass_utils import run_bass_kernel_spmd

F32 = mybir.dt.float32
BF16 = mybir.dt.bfloat16
AF = mybir.ActivationFunctionType
ALU = mybir.AluOpType
AX = mybir.AxisListType


def _freeze(fn):
    if fn.__closure__ is None:
        return fn
    cells = []
    for c in fn.__closure__:
        try:
            cells.append(types.CellType(c.cell_contents))
        except ValueError:
            cells.append(c)
    return types.FunctionType(fn.__code__, fn.__globals__, fn.__name__, fn.__defaults__, tuple(cells))


class Sched:
    BLK = {'pe': 'tensor', 'act': 'scalar', 'dve': 'vector', 'pool': 'gpsimd', 'sp': 'sync'}
    NPOOL = 16

    def __init__(self, nc):
        self.nc = nc
        self.ops = {e: [] for e in self.BLK}
        self.cnt = {e: 0 for e in self.BLK}
        self.seen = {e: {} for e in self.BLK}
        self.lastw = {}
        self.readers = {}
        self.dcount = {e: 0 for e in self.BLK}
        self.dtot = {}
        self.nins = 0

    def _deps(self, reads, writes, eng=None):
        deps = []
        for k in reads:
            t = self.lastw.get(k)
            if t is not None:
                deps.append(t)
            if isinstance(k, str) and k.startswith('ps'):
                r = self.readers.get(k)
                if r:
                    deps.extend((sid, v) for sid, v in r.items() if sid != 'c:' + str(eng))
        for k in writes:
            t = self.lastw.get(k)
            if t is not None:
                deps.append(t)
            r = self.readers.get(k)
            if r:
                deps.extend(r.items())
        return deps

    def _waits(self, eng, deps):
        need = {}
        seen = self.seen[eng]
        for sid, v in deps:
            if sid == 'c:pe' and eng == 'pe':
                continue
            if seen.get(sid, 0) >= v:
                continue
            if need.get(sid, 0) < v:
                need[sid] = v
        for sid, v in need.items():
            seen[sid] = v
        return list(need.items())

    def _commit(self, tok, reads, writes):
        for k in reads:
            r = self.readers.setdefault(k, {})
            if r.get(tok[0], 0) < tok[1]:
                r[tok[0]] = tok[1]
        for k in writes:
            self.lastw[k] = tok
            self.readers[k] = {}

    def op(self, eng, fn, reads=(), writes=()):
        waits = self._waits(eng, self._deps(reads, writes, eng))
        self.cnt[eng] += 1
        tok = ('c:' + eng, self.cnt[eng])
        self.ops[eng].append((waits, _freeze(fn), (tok[0], 1)))
        self._commit(tok, reads, writes)
        self.nins += 1

    def dma(self, q, out, in_, reads=(), writes=(), **kw):
        deps = self._deps(reads, writes)
        j = self.dcount[q]
        self.dcount[q] += 1
        sid = 'd:%s:%d' % (q, j % self.NPOOL)
        prev = self.dtot.get(sid, 0)
        if prev:
            deps.append((sid, prev))
        waits = self._waits(q, deps)
        self.dtot[sid] = prev + 16
        tok = (sid, prev + 16)
        self.ops[q].append((waits, lambda e: e.dma_start(out=out, in_=in_, **kw), (sid, 16)))
        self._commit(tok, reads, writes)
        self.nins += 1

    def barrier(self):
        alls = [('c:' + e, c) for e, c in self.cnt.items() if c] + list(self.dtot.items())
        for e in self.BLK:
            w = self._waits(e, alls)
            if w:
                self.ops[e].append((w, None, None))

    def finish(self):
        w = self._waits('sp', list(self.dtot.items()) + [('c:' + e, c) for e, c in self.cnt.items() if c])
        if w:
            self.ops['sp'].append((w, None, None))
        nc = self.nc
        sids = ['c:' + e for e in self.BLK] + sorted(self.dtot)
        with ExitStack() as st:
            semh = {}
            for sid in sids:
                semh[sid] = st.enter_context(nc.semaphore(sid.replace(':', '_')))
            block = st.enter_context(nc.Block())
            for ename, bname in self.BLK.items():
                ops = self.ops[ename]

                def body(e, ops=ops):
                    for waits, fn, inc in ops:
                        for sid, v in waits:
                            e.wait_ge(semh[sid], v)
                        if fn is not None:
                            ins = fn(e)
                            ins.then_inc(semh[inc[0]], inc[1])
                getattr(block, bname)(body)


SUB = 99
INTERLEAVE = True

class Mixers:
    NAMES = ('win', 'pp', 'pb', 'cf', 'psA', 'psT', 'psB', 'mixT', 'negA8', 'negsink', 'identF', 'identB', 'onesB',
             'mUiP', 'mLsP', 'Rm', 'swaMP', 'swaMP0', 'swaMP1', 'validc', 'roped', 'rot', 'NT', 'SMP', 'epsc',
             'dn_p', 'dnc_p', 'ssm_p', 'ssmc_p', 'k_p', 'v_p', 'stage', 'do_sample', 'smp_io', 'wout_off')

    def __init__(self, nc, S, ar, c):
        self.nc, self.S, self.ar = nc, S, ar
        for k in self.NAMES:
            setattr(self, k, c[k])
        a = ar.alloc
        self.onec = self.cf[:, 1026:1027]
        self.rawbuf = a([128, 18, 176])
        self.rawP = self.rawbuf[:, :, 0:131]
        self.rawS = self.rawbuf.rearrange("p c (s u) -> p c s u", u=11)
        self.cv = [a([128, 4, 128]) for _ in range(2)]
        self.qk32 = a([128, 8, 128])
        self.sqb = a([128, 8, 128], BF16)
        self.qnT = a([128, 4, 128], BF16)
        self.knT = a([128, 4, 128], BF16)
        self.QdT = a([128, 4, 128], BF16)
        self.vT = a([128, 4, 128], BF16)
        self.szT = a([128, 4, 128], BF16)
        self.xT32 = a([128, 2, 128])
        self.BT = a([128, 2, 128], BF16)
        self.CT = a([128, 2, 128], BF16)
        self.sw32 = a([128, 4, 128])
        self.sz = a([128, 256], BF16)
        self.sm = a([128, 80])
        self.gcT = [a([1, 128]) for _ in range(2)]
        self.Ui = a([128, 8, 128], BF16)
        self.E = a([128, 8, 128], BF16)
        self.Ls = a([128, 4, 128])
        self.Dt = [a([128, 128]) for _ in range(2)]
        self.N32 = a([128, 4, 128])
        self.Noffb = a([128, 4, 128], BF16)
        self.bd64 = self.cf[:, 2048:2176]
        self.offm = self.cf[:, 2176:2304]
        self.Pk = [a([128, 4, 128], BF16) for _ in range(2)]
        self.Ptk = [a([128, 4, 128], BF16) for _ in range(2)]
        self.Y = a([128, 4, 128])
        self.Yb = [a([128, 4, 128], BF16) for _ in range(2)]
        self.attnT = a([128, 4, 128], BF16)
        self.Kbg = a([128, 4, 128], BF16)
        self.Kd = a([128, 4, 128], BF16)
        self.Vb = a([128, 4, 128], BF16)
        self.negWT = a([128, 4, 128], BF16)
        self.vnew = a([128, 4, 128], BF16)
        self.S32 = a([128, 4, 128])
        self.Sb = a([128, 4, 128], BF16)
        self.sqo = self.sqb[:, 0:4, :]
        self.o1 = self.N32
        self.xtok = a([128, 256])
        self.xdt = a([128, 4, 64], BF16)
        self.Bd = a([128, 4, 128], BF16)
        self.sattnT = a([128, 4, 128], BF16)
        self.CdT = a([128, 4, 128], BF16)
        self.h32 = a([128, 256])
        self.hb = a([128, 256], BF16)
        self.y2 = a([128, 256])
        self.y4 = a([128, 256], BF16)
        self.hout = a([128, 2, 128])
        self.ropet = [a([128, 2, 128]) for _ in range(2)]
        self.r1 = a([128, 3, 128])
        self.r2 = a([128, 3, 128])
        self.qz = a([128, 4, 128], BF16)
        self.kring = a([128, 2, 128], BF16)
        self.vring = a([128, 2, 2, 128], BF16)
        self.v32 = a([128, 128])
        self.e32 = [a([128, 256])]
        self.em = a([128, 256])
        self.pn = a([128, 256], BF16)
        self.pnT = [a([128, 2, 128], BF16) for _ in range(2)]
        if self.do_sample:
            (self.sdn, self.sdnc, self.sssm, self.sssmc, self.ck, self.cvd,
             self.dn_s, self.dnc_s, self.ssm_s, self.ssmc_s, self.k_s, self.v_s) = self.smp_io
            cf = self.cf
            self.mUiS, self.mLsS, self.lastm = cf[:, 1152:1280], cf[:, 1280:1408], cf[:, 1408:1536]
            self.swaMS, self.rowm = cf[:, 1536:1792], cf[:, 1920:1936]
            self.cstg = a([128, 768])
            self.ZW = a([128, 16, 248], BF16)
            self.kdv = a([128, 8])
            save_top = ar.top
            ar.top = self.wout_off
            self.ctmp = a([128, 4, 48])
            self.glS = a([128, 8, 16])
            self.KdM = [a([128, 128], BF16) for _ in range(2)]
            base = ar.top
            self.Sall = a([128, 16, 128])
            self.Sball = a([128, 16, 128], BF16)
            top1 = ar.top
            ar.top = base
            self.hin = a([128, 16, 128])
            self.Hh = a([128, 16, 64])
            self.Hhb = a([128, 16, 64], BF16)
            top2 = ar.top
            ar.top = base
            self.KcT = a([128, 16, 128], BF16)
            self.VA = a([128, 16, 2, 128], BF16)
            self.kst = a([128, 2, 128])
            self.vst = a([128, 2, 128])
            assert max(top1, top2, ar.top) <= self.wout_off + 8 * 1024, "sample overlay exceeds w_out region"
            ar.top = save_top
            S.op('pool', lambda e: e.memset(self.ZW, 0.0), writes=['ZW'])
        z = S.op
        z('pool', lambda e: e.memset(self.rawP[:, :, 0:3], 0.0), writes=['raw%d' % g for g in range(5)])
        z('pool', lambda e: e.memset(self.S32, 0.0), writes=['S32'])
        z('pool', lambda e: e.memset(self.Sb, 0.0), writes=['Sb'])
        z('pool', lambda e: e.memset(self.h32, 0.0), writes=['h32'])
        z('pool', lambda e: e.memset(self.hb, 0.0), writes=['hb'])
        z('pool', lambda e: e.memset(self.kring, 0.0), writes=['kring0', 'kring1'])
        z('pool', lambda e: e.memset(self.qz, 0.0), writes=['qz'])
        z('pool', lambda e: e.memset(self.vring, 0.0), writes=['vring0', 'vring1'])

    def proj_fm(self, chunks, half, hT, hkey):
        S, psA, win = self.S, self.psA, self.win
        for j, c in enumerate(chunks):
            for kc in range(8):
                S.op('pe', lambda e, j=j, c=c, kc=kc: e.matmul(psA[:, half * 512 + j * 128:half * 512 + (j + 1) * 128],
                                                               lhsT=win[:, kc, c * 128:(c + 1) * 128], rhs=hT[:, kc, :],
                                                               start=(kc == 0), stop=(kc == 7)),
                     reads=['win', hkey], writes=['psA%d' % half])

    def tile(self, L, t, hT, hkey):
        S = self.S
        psA, psT, psB = self.psA, self.psT, self.psB
        pp, pb, sm = self.pp, self.pb, self.sm
        rot = self.rot
        last = (t == self.NT)
        kind = 'S' if t == self.SMP else 'P'
        self.kind = kind
        mUi, mLs = (self.mUiS, self.mLsS) if kind == 'S' else (self.mUiP, self.mLsP)
        if kind == 'S':
            self.conv_hist_load(L)
        beta, negbeta, spin, sp, g8, gc8, egc8, begc, glast = (sm[:, 0:4], sm[:, 4:8], sm[:, 8:16], sm[:, 16:24],
                                                               sm[:, 24:32], sm[:, 32:40], sm[:, 40:48], sm[:, 48:52],
                                                               sm[:, 52:60])
        sws = sm[:, 60:64]

        def gen_proj():
            groups = [([0, 1, 2, 3], 0), ([4, 5, 6, 7], 4), ([8, 9, 10, 11], 8), ([16, 17, 18, 19], 12), ([20, 21], 16)]
            for gi, (chunks, ci0) in enumerate(groups):
                n = len(chunks)
                half = rot('psAh')
                self.proj_fm(chunks, half, hT, hkey)
                rk = 'raw%d' % gi
                if kind == 'P':
                    S.op('act', lambda e, half=half, ci0=ci0, n=n: e.copy(
                        out=self.rawP[:, ci0:ci0 + n, 3:131],
                        in_=psA[:, half * 512:half * 512 + n * 128].rearrange("p (a b) -> p a b", a=n)),
                        reads=['psA%d' % half], writes=[rk])
                else:
                    for j in range(n):
                        S.op('act', lambda e, half=half, ci0=ci0, j=j: e.copy(
                            out=self.rawS[:, ci0 + j, :, 3:11],
                            in_=psA[:, half * 512 + j * 128:half * 512 + (j + 1) * 128].rearrange("p (s u) -> p s u", u=8)),
                            reads=['psA%d' % half], writes=[rk])
                cb_ = rot('cv')
                cv = self.cv[cb_]
                ckl = ['cv%d_%d' % (cb_, j) for j in range(4)]
                srcs, dsts = [], []
                for j in range(n):
                    ci = ci0 + j
                    if kind == 'P':
                        srcs.append(lambda tap, ci=ci: self.rawP[:, ci, tap:tap + 128])
                        dsts.append(cv[:, j, :])
                    else:
                        srcs.append(lambda tap, ci=ci: self.rawS[:, ci, :, tap:tap + 8])
                        dsts.append(cv[:, j, :].rearrange("p (s u) -> p s u", u=8))
                for tap in range(4):
                    for j in range(n):
                        ci = ci0 + j
                        src, dstv = srcs[j], dsts[j]
                        if tap == 0:
                            S.op('dve', lambda e, ci=ci, src=src, dstv=dstv: e.tensor_scalar(out=dstv, in0=src(0),
                                                                                             scalar1=pp[:, ci * 4:ci * 4 + 1], scalar2=None,
                                                                                             op0=ALU.mult),
                                 reads=[rk, 'pp'], writes=[ckl[j]])
                        else:
                            S.op('dve', lambda e, ci=ci, tap=tap, src=src, dstv=dstv: e.scalar_tensor_tensor(
                                out=dstv, in0=src(tap), scalar=pp[:, ci * 4 + tap:ci * 4 + tap + 1],
                                in1=dstv, op0=ALU.mult, op1=ALU.add), reads=[rk, 'pp', ckl[j]], writes=[ckl[j]])
                if kind == 'P':
                    S.op('pool', lambda e, ci0=ci0, n=n: e.tensor_copy(out=self.rawP[:, ci0:ci0 + n, 0:3],
                                                                       in_=self.rawP[:, ci0:ci0 + n, 128:131]),
                         reads=[rk] + ckl, writes=[rk])
                if gi == 0:
                    S.op('act', lambda e, cv=cv: e.activation(out=self.qk32[:, 0:4, :], in_=cv, func=AF.Silu), reads=ckl, writes=['q32'])
                elif gi == 1:
                    S.op('act', lambda e, cv=cv: e.activation(out=self.qk32[:, 4:8, :], in_=cv, func=AF.Silu), reads=ckl, writes=['k32'])
                elif gi == 2:
                    S.op('act', lambda e, cv=cv: e.activation(out=self.vT, in_=cv, func=AF.Silu), reads=ckl, writes=['vT'])
                else:
                    dests = ([self.xT32[:, 0, :], self.xT32[:, 1, :], self.BT[:, 0, :], self.BT[:, 1, :]] if gi == 3
                             else [self.CT[:, 0, :], self.CT[:, 1, :]])
                    for j in range(n):
                        ci = ci0 + j
                        S.op('act', lambda e, j=j, ci=ci, cv=cv, dests=dests: e.activation(
                            out=dests[j], in_=cv[:, j, :], func=AF.Silu, bias=pp[:, 72 + ci - 12:73 + ci - 12]),
                            reads=[ckl[j], 'pp'], writes=['ssmxbc%d' % gi])
                yield
            half = rot('psAh')
            self.proj_fm([12, 13, 14, 15], half, hT, hkey)
            S.op('act', lambda e, half=half: e.activation(out=self.szT, in_=psA[:, half * 512:(half + 1) * 512].rearrange(
                "p (a b) -> p a b", a=4), func=AF.Silu), reads=['psA%d' % half], writes=['szT'])
            half = rot('psAh')
            self.proj_fm([24, 25, 26, 27], half, hT, hkey)
            S.op('act', lambda e, half=half: e.copy(out=self.sw32, in_=psA[:, half * 512:(half + 1) * 512].rearrange(
                "p (a b) -> p a b", a=4)), reads=['psA%d' % half], writes=['sw32'])
            yield
            for kc in range(8):
                S.op('pe', lambda e, kc=kc: e.matmul(psB[0][:, 0:256], lhsT=hT[:, kc, :], rhs=self.win[:, kc, 22 * 128:24 * 128],
                                                     start=(kc == 0), stop=(kc == 7)), reads=['win', hkey], writes=['psB0'])
            S.op('act', lambda e: e.activation(out=self.sz, in_=psB[0][:, 0:256], func=AF.Silu), reads=['psB0'], writes=['sz'])
            yield

        def gen_decay():
            for kc in range(8):
                S.op('pe', lambda e, kc=kc: e.matmul(psB[2][:, 0:16], lhsT=hT[:, kc, :], rhs=self.win[:, kc, 3584:3600],
                                                     start=(kc == 0), stop=(kc == 7)), reads=['win', hkey], writes=['psB2'])
            yield
            S.op('act', lambda e: e.activation(out=beta, in_=psB[2][:, 0:4], func=AF.Sigmoid), reads=['psB2'], writes=['beta'])
            S.op('dve', lambda e: e.tensor_tensor(out=spin, in0=psB[2][:, 4:12], in1=pb[:, 8:16], op=ALU.add),
                 reads=['psB2', 'pb'], writes=['spin'])
            S.op('act', lambda e: e.activation(out=spin, in_=spin, func=AF.Exp), reads=['spin'], writes=['spin'])
            S.op('act', lambda e: e.activation(out=sp, in_=spin, func=AF.Ln, bias=self.onec), reads=['spin'], writes=['sp'])
            if t == 0:
                S.op('dve', lambda e: e.tensor_scalar(out=sp, in0=sp, scalar1=self.validc, scalar2=None, op0=ALU.mult),
                     reads=['sp', 'cf'], writes=['sp'])
            S.op('dve', lambda e: e.tensor_tensor(out=g8, in0=sp, in1=self.negA8, op=ALU.mult), reads=['sp', 'negA8'], writes=['g8'])
            S.op('dve', lambda e: e.tensor_scalar(out=negbeta, in0=beta, scalar1=-1.0, scalar2=None, op0=ALU.mult),
                 reads=['beta'], writes=['negbeta'])
            S.op('pe', lambda e: e.matmul(psB[2][:, 16:24], lhsT=mUi, rhs=g8, start=True, stop=True),
                 reads=['g8', 'cf'], writes=['psB2'])
            S.op('dve', lambda e: e.tensor_copy(out=gc8, in_=psB[2][:, 16:24]), reads=['psB2'], writes=['gc8'])
            S.op('act', lambda e: e.activation(out=egc8, in_=gc8, func=AF.Exp), reads=['gc8'], writes=['egc8'])
            S.op('dve', lambda e: e.tensor_tensor(out=begc, in0=beta, in1=egc8[:, 0:4], op=ALU.mult),
                 reads=['beta', 'egc8'], writes=['begc'])
            yield
            for hh in range(8):
                bi = 3 + hh % 2
                bank = psB[bi]
                bk = 'psB%d' % bi
                gb_ = rot('gcT')
                S.op('pe', lambda e, hh=hh: e.transpose(out=psB[2][0:1, 32:160], in_=gc8[:, hh:hh + 1], identity=self.identF),
                     reads=['gc8', 'cf'], writes=['psB2'])
                S.op('act', lambda e, gb_=gb_: e.copy(out=self.gcT[gb_], in_=psB[2][0:1, 32:160]), reads=['psB2'], writes=['gcT%d' % gb_])
                S.op('pe', lambda e, hh=hh, bank=bank, gb_=gb_: e.matmul(bank[:, 0:128], lhsT=self.mUiP[0:1, :],
                                                                         rhs=self.gcT[gb_], start=True, stop=True),
                     reads=['gcT%d' % gb_, 'cf'], writes=[bk])
                db = rot('Dt')
                S.op('dve', lambda e, hh=hh, bank=bank, db=db: e.tensor_scalar(out=self.Dt[db], in0=bank[:, 0:128],
                                                                               scalar1=gc8[:, hh:hh + 1], scalar2=0.0,
                                                                               op0=ALU.subtract, op1=ALU.min),
                     reads=[bk, 'gc8'], writes=['Dt%d' % db])
                S.op('act', lambda e, hh=hh, db=db: e.activation(out=self.Ui[:, hh, :], in_=self.Dt[db], func=AF.Exp),
                     reads=['Dt%d' % db], writes=['Ui%d' % hh])
                S.op('pool', lambda e, hh=hh: e.tensor_tensor(out=self.Ui[:, hh, :], in0=self.Ui[:, hh, :], in1=mUi, op=ALU.mult),
                     reads=['Ui%d' % hh, 'cf'], writes=['Ui%d' % hh])
                S.op('act', lambda e, hh=hh, bank=bank: e.activation(out=self.E[:, hh, :], in_=bank[:, 0:128], func=AF.Exp),
                     reads=[bk], writes=['E%d' % hh])
                if kind == 'P':
                    S.op('act', lambda e, hh=hh, bank=bank: e.activation(out=glast[:, hh:hh + 1], in_=bank[:, 127:128], func=AF.Exp),
                         reads=[bk], writes=['glast'])
                else:
                    S.op('act', lambda e, hh=hh, bank=bank: e.activation(
                        out=self.glS[:, hh, :], in_=bank[:, 0:128].rearrange("p (s u) -> p s u", u=8)[:, :, 7], func=AF.Exp),
                        reads=[bk], writes=['glS'])
                    db = rot('Dt')
                    S.op('dve', lambda e, hh=hh, db=db: e.scalar_tensor_tensor(
                        out=self.Dt[db], in0=self.Ui[:, hh, :], scalar=1.0, in1=self.lastm, op0=ALU.mult, op1=ALU.mult,
                        accum_out=self.kdv[:, hh:hh + 1]), reads=['Ui%d' % hh, 'cf'], writes=['Dt%d' % db, 'kdv'])
                if hh < 4:
                    db = rot('Dt')
                    S.op('dve', lambda e, hh=hh, bank=bank, db=db: e.tensor_scalar(out=self.Dt[db], in0=bank[:, 0:128],
                                                                                   scalar1=gc8[:, hh:hh + 1], scalar2=0.0,
                                                                                   op0=ALU.subtract, op1=ALU.max),
                         reads=[bk, 'gc8'], writes=['Dt%d' % db])
                    S.op('act', lambda e, hh=hh, db=db: e.activation(out=self.Ls[:, hh, :], in_=self.Dt[db], func=AF.Exp, scale=-1.0),
                         reads=['Dt%d' % db], writes=['Ls%d' % hh])
                    S.op('pool', lambda e, hh=hh: e.tensor_tensor(out=self.Ls[:, hh, :], in0=self.Ls[:, hh, :], in1=mLs, op=ALU.mult),
                         reads=['Ls%d' % hh, 'cf'], writes=['Ls%d' % hh])
                yield

        gp, gd = gen_proj(), gen_decay()
        if kind == 'P' and INTERLEAVE:
            pend = [gp, gd]
            while pend:
                for g in list(pend):
                    try:
                        next(g)
                    except StopIteration:
                        pend.remove(g)
        else:
            for g in (gp, gd):
                for _ in g:
                    pass
        self.kdS = self.kdv[:, 4:8] if kind == 'S' else self.Ui[:, 4:8, 127]
        if kind == 'S':
            self.conv_tail_out_S(L)
        gens = [self.deltanet(L, t, last), self.ssd(L, t, last), self.swa(L, t, last)]
        if kind == 'P' and INTERLEAVE:
            while gens:
                for g in list(gens):
                    try:
                        next(g)
                    except StopIteration:
                        gens.remove(g)
        else:
            for g in gens:
                for _ in g:
                    pass

    def deltanet(self, L, t, last):
        S = self.S
        psT, psB, sm = self.psT, self.psB, self.sm
        rot = self.rot
        beta, negbeta, begc, glast = sm[:, 0:4], sm[:, 4:8], sm[:, 48:52], sm[:, 52:60]
        v4 = lambda ap: ap.rearrange("p (a b) -> p a b", a=4)
        isP = (self.kind == 'P')
        S.op('pool', lambda e: e.tensor_tensor(out=self.sqb, in0=self.qk32, in1=self.qk32, op=ALU.mult),
             reads=['q32', 'k32'], writes=['sqb'])
        for j in range(8):
            S.op('pe', lambda e, j=j: e.matmul(psB[j // 4][:, (j % 4) * 128:(j % 4 + 1) * 128], lhsT=self.onesB,
                                               rhs=self.sqb[:, j, :], start=True, stop=True),
                 reads=['sqb', 'cb'], writes=['psB%d' % (j // 4)])
        for hf in range(2):
            S.op('act', lambda e, hf=hf: e.activation(out=self.cv[hf], in_=v4(psB[hf][:, :]),
                                                      func=AF.Sqrt, bias=self.epsc), reads=['psB%d' % hf], writes=['cv%d_%d' % (hf, j) for j in range(4)])
            S.op('dve', lambda e, hf=hf: e.reciprocal(out=self.cv[hf], in_=self.cv[hf]),
                 reads=['cv%d_%d' % (hf, j) for j in range(4)], writes=['cv%d_%d' % (hf, j) for j in range(4)])
        S.op('dve', lambda e: e.scalar_tensor_tensor(out=self.qnT, in0=self.qk32[:, 0:4, :], scalar=128.0 ** -0.5,
                                                     in1=self.cv[0], op0=ALU.mult, op1=ALU.mult),
             reads=['q32'] + ['cv0_%d' % j for j in range(4)], writes=['qnT'])
        S.op('dve', lambda e: e.tensor_tensor(out=self.knT, in0=self.qk32[:, 4:8, :], in1=self.cv[1], op=ALU.mult),
             reads=['k32'] + ['cv1_%d' % j for j in range(4)], writes=['knT'])
        S.op('pool', lambda e: e.tensor_tensor(out=self.QdT, in0=self.qnT, in1=self.E[:, 0:4, :], op=ALU.mult),
             reads=['qnT'] + ['E%d' % h for h in range(4)], writes=['QdT'])
        yield
        if SUB == 1:
            return self._dn_bail()
        for h in range(4):
            bi = h % 2
            S.op('pe', lambda e, h=h, bi=bi: e.matmul(psB[bi][:, 0:128], lhsT=self.knT[:, h, :], rhs=self.knT[:, h, :],
                                                      start=True, stop=True), reads=['knT'], writes=['psB%d' % bi])
            S.op('pe', lambda e, h=h, bi=bi: e.matmul(psB[bi][:, 128:256], lhsT=self.knT[:, h, :], rhs=self.qnT[:, h, :],
                                                      start=True, stop=True), reads=['knT', 'qnT'], writes=['psB%d' % bi])
            S.op('dve', lambda e, h=h, bi=bi: e.scalar_tensor_tensor(out=self.N32[:, h, :], in0=psB[bi][:, 0:128],
                                                                     scalar=negbeta[:, h:h + 1], in1=self.Ls[:, h, :],
                                                                     op0=ALU.mult, op1=ALU.mult),
                 reads=['psB%d' % bi, 'negbeta', 'Ls%d' % h], writes=['N32'])
            S.op('dve', lambda e, h=h, bi=bi: e.tensor_tensor(out=self.attnT[:, h, :], in0=psB[bi][:, 128:256],
                                                              in1=self.Ui[:, h, :], op=ALU.mult),
                 reads=['psB%d' % bi, 'Ui%d' % h], writes=['attnT'])
            yield
        if SUB == 2:
            return self._dn_bail()
        bd64b = self.bd64.unsqueeze(1).broadcast_to([128, 4, 128])
        offmb = self.offm.unsqueeze(1).broadcast_to([128, 4, 128])
        S.op('pool', lambda e: e.tensor_tensor(out=self.Noffb, in0=self.N32, in1=offmb, op=ALU.mult),
             reads=['N32', 'cf'], writes=['Noffb'])
        S.op('pool', lambda e: e.tensor_tensor(out=self.N32, in0=self.N32, in1=bd64b, op=ALU.mult),
             reads=['N32', 'cf'], writes=['N32'])
        pi = rot('Pk')
        S.op('pool', lambda e: e.tensor_copy(out=self.Pk[pi], in_=self.N32), reads=['N32'], writes=['Pk%d' % pi])
        for h in range(4):
            S.op('pe', lambda e, h=h: e.transpose(out=psB[2][:, h * 128:(h + 1) * 128], in_=self.N32[:, h, :], identity=self.identF),
                 reads=['N32', 'cf'], writes=['psB2'])
        ti = rot('Ptk')
        S.op('act', lambda e: e.copy(out=self.Ptk[ti], in_=v4(psB[2][:, :])), reads=['psB2'], writes=['Ptk%d' % ti])
        for h in range(4):
            S.op('dve', lambda e, h=h: e.tensor_tensor(out=self.Y[:, h, :], in0=psB[2][:, h * 128:(h + 1) * 128],
                                                       in1=self.identF, op=ALU.add), reads=['psB2', 'cf'], writes=['Y'])
        yi = rot('Yb')
        S.op('pool', lambda e: e.tensor_copy(out=self.Yb[yi], in_=self.Y), reads=['Y'], writes=['Yb%d' % yi])
        yield
        K = 5
        for k in range(1, K + 1):
            pn_, tn_, yn_ = rot('Pk'), None, rot('Yb')
            for h in range(4):
                S.op('pe', lambda e, h=h: e.matmul(psB[0][:, h * 128:(h + 1) * 128], lhsT=self.Ptk[ti][:, h, :],
                                                   rhs=self.Pk[pi][:, h, :], start=True, stop=True),
                     reads=['Ptk%d' % ti, 'Pk%d' % pi], writes=['psB0'])
            S.op('act', lambda e, pn_=pn_: e.copy(out=self.Pk[pn_], in_=v4(psB[0][:, :])), reads=['psB0'], writes=['Pk%d' % pn_])
            yield
            if k < K:
                tn_ = rot('Ptk')
                for h in range(4):
                    S.op('pe', lambda e, h=h: e.matmul(psB[1][:, h * 128:(h + 1) * 128], lhsT=self.Pk[pi][:, h, :],
                                                       rhs=self.Ptk[ti][:, h, :], start=True, stop=True),
                         reads=['Ptk%d' % ti, 'Pk%d' % pi], writes=['psB1'])
                S.op('dve', lambda e, tn_=tn_: e.tensor_copy(out=self.Ptk[tn_], in_=v4(psB[1][:, :])),
                     reads=['psB1'], writes=['Ptk%d' % tn_])
            for h in range(4):
                S.op('pe', lambda e, h=h, pn_=pn_: e.matmul(psB[2][:, h * 128:(h + 1) * 128], lhsT=self.Pk[pn_][:, h, :],
                                                            rhs=self.Yb[yi][:, h, :], start=True, stop=True),
                     reads=['Pk%d' % pn_, 'Yb%d' % yi], writes=['psB2'])
            S.op('dve', lambda e: e.tensor_tensor(out=self.Y, in0=v4(psB[2][:, :]), in1=self.Y, op=ALU.add),
                 reads=['psB2', 'Y'], writes=['Y'])
            S.op('pool', lambda e, yn_=yn_: e.tensor_copy(out=self.Yb[yn_], in_=self.Y), reads=['Y'], writes=['Yb%d' % yn_])
            yield
            pi, yi = pn_, yn_
            if tn_ is not None:
                ti = tn_
        tb = self.Pk[1 - pi]
        m1 = self.Ptk[ti]
        for h in range(4):
            S.op('pe', lambda e, h=h: e.transpose(out=psT[:, h * 128:(h + 1) * 128], in_=self.Yb[yi][:, h, :], identity=self.identB),
                 reads=['Yb%d' % yi, 'cb'], writes=['psT'])
        S.op('act', lambda e: e.copy(out=tb, in_=v4(psT[:, 0:512])), reads=['psT'], writes=['Pk%d' % (1 - pi)])
        yield
        for h in range(4):
            S.op('pe', lambda e, h=h: e.matmul(psB[0][:, h * 128:(h + 1) * 128], lhsT=self.Noffb[:, h, :], rhs=self.Yb[yi][:, h, :],
                                               start=True, stop=True), reads=['Noffb', 'Yb%d' % yi], writes=['psB0'])
        S.op('dve', lambda e: e.tensor_copy(out=m1, in_=v4(psB[0][:, :])), reads=['psB0'], writes=['Ptk%d' % ti])
        yield
        for h in range(4):
            S.op('pe', lambda e, h=h: e.matmul(psB[1][:, h * 128:(h + 1) * 128], lhsT=tb[:, h, :], rhs=m1[:, h, :],
                                               start=True, stop=True), reads=['Pk%d' % (1 - pi), 'Ptk%d' % ti], writes=['psB1'])
        S.op('dve', lambda e: e.tensor_tensor(out=self.Y, in0=v4(psB[1][:, :]), in1=self.Y, op=ALU.add),
             reads=['psB1', 'Y'], writes=['Y'])
        yn_ = rot('Yb')
        S.op('pool', lambda e: e.tensor_copy(out=self.Yb[yn_], in_=self.Y), reads=['Y'], writes=['Yb%d' % yn_])
        yi = yn_
        Yb = self.Yb[yi]
        ykey = 'Yb%d' % yi
        yield
        for h in range(4):
            S.op('pe', lambda e, h=h: e.transpose(out=psT[:, h * 128:(h + 1) * 128], in_=self.knT[:, h, :], identity=self.identB),
                 reads=['knT', 'cb'], writes=['psT'])
        for h in range(4):
            S.op('act', lambda e, h=h: e.mul(out=self.Kbg[:, h, :], in_=psT[:, h * 128:(h + 1) * 128], mul=begc[:, h:h + 1]),
                 reads=['psT', 'begc'], writes=['Kbg'])
            kdc = self.kdv[:, h:h + 1] if self.kind == 'S' else self.Ui[:, h, 127:128]
            S.op('dve', lambda e, h=h, kdc=kdc: e.tensor_scalar(out=self.Kd[:, h, :], in0=psT[:, h * 128:(h + 1) * 128],
                                                                scalar1=kdc, scalar2=None, op0=ALU.mult),
                 reads=['psT', 'Ui%d' % h, 'kdv'], writes=['Kd'])
        yield
        for h in range(4):
            S.op('pe', lambda e, h=h: e.transpose(out=psT[:, h * 128:(h + 1) * 128], in_=self.vT[:, h, :],
                                                  identity=self.identB), reads=['vT', 'cb'], writes=['psT'])
        for h in range(4):
            S.op('act', lambda e, h=h: e.mul(out=self.Vb[:, h, :], in_=psT[:, h * 128:(h + 1) * 128],
                                             mul=beta[:, h:h + 1]), reads=['psT', 'beta'], writes=['Vb'])
        yield
        if SUB == 4:
            return self._dn_bail()
        if self.kind == 'S':
            self.dn_state_S(L, Yb, ykey)
        else:
            for h in range(4):
                S.op('pe', lambda e, h=h: e.matmul(psB[0][:, h * 128:(h + 1) * 128], lhsT=self.Kbg[:, h, :], rhs=Yb[:, h, :],
                                                   start=True, stop=True), reads=['Kbg', ykey], writes=['psB0'])
            yield
            S.op('act', lambda e: e.mul(out=self.negWT, in_=v4(psB[0][:, :]), mul=-1.0), reads=['psB0'], writes=['negWT'])
            for h in range(4):
                S.op('pe', lambda e, h=h: e.matmul(psB[1][:, h * 128:(h + 1) * 128], lhsT=Yb[:, h, :], rhs=self.Vb[:, h, :],
                                                   start=True, stop=False), reads=[ykey, 'Vb'], writes=['psB1'])
                S.op('pe', lambda e, h=h: e.matmul(psB[1][:, h * 128:(h + 1) * 128], lhsT=self.negWT[:, h, :], rhs=self.Sb[:, h, :],
                                                   start=False, stop=True), reads=['negWT', 'Sb'], writes=['psB1'])
            yield
            S.op('act', lambda e: e.copy(out=self.vnew, in_=v4(psB[1][:, :])), reads=['psB1'], writes=['vnew'])
            for h in range(4):
                S.op('pe', lambda e, h=h: e.matmul(psB[2][:, h * 128:(h + 1) * 128], lhsT=self.Sb[:, h, :], rhs=self.QdT[:, h, :],
                                                   start=True, stop=False), reads=['Sb', 'QdT'], writes=['psB2'])
                S.op('pe', lambda e, h=h: e.matmul(psB[2][:, h * 128:(h + 1) * 128], lhsT=self.vnew[:, h, :], rhs=self.attnT[:, h, :],
                                                   start=False, stop=True), reads=['vnew', 'attnT'], writes=['psB2'])
            for h in range(4):
                S.op('pe', lambda e, h=h: e.matmul(psB[0][:, h * 128:(h + 1) * 128], lhsT=self.Kd[:, h, :], rhs=self.vnew[:, h, :],
                                                   start=True, stop=True), reads=['Kd', 'vnew'], writes=['psB0'])
            for h in range(4):
                S.op('dve', lambda e, h=h: e.scalar_tensor_tensor(out=self.S32[:, h, :], in0=self.S32[:, h, :],
                                                                  scalar=glast[:, h:h + 1], in1=psB[0][:, h * 128:(h + 1) * 128],
                                                                  op0=ALU.mult, op1=ALU.add),
                     reads=['S32', 'glast', 'psB0'], writes=['S32'])
            yield
            S.op('pool', lambda e: e.tensor_copy(out=self.Sb, in_=self.S32), reads=['S32'], writes=['Sb'])
            if SUB == 5:
                return self._dn_bail()
        ob, nb = (2, 1) if isP else (0, 2)
        yield
        S.op('act', lambda e: e.activation(out=self.sqo, in_=v4(psB[ob][:, :]), func=AF.Square), reads=['psB%d' % ob], writes=['sqb'])
        for h in range(4):
            S.op('pe', lambda e, h=h: e.matmul(psB[nb][:, h * 128:(h + 1) * 128], lhsT=self.onesB, rhs=self.sqo[:, h, :],
                                               start=True, stop=True), reads=['sqb', 'cb'], writes=['psB%d' % nb])
        S.op('act', lambda e: e.activation(out=self.o1, in_=v4(psB[nb][:, :]), func=AF.Sqrt, scale=1.0 / 128, bias=self.epsc),
             reads=['psB%d' % nb], writes=['N32'])
        yield
        S.op('dve', lambda e: e.reciprocal(out=self.o1, in_=self.o1), reads=['N32'], writes=['N32'])
        S.op('dve', lambda e: e.tensor_tensor(out=self.o1, in0=v4(psB[ob][:, :]), in1=self.o1, op=ALU.mult),
             reads=['psB%d' % ob, 'N32'], writes=['N32'])
        S.op('dve', lambda e: e.scalar_tensor_tensor(out=self.mixT[:, 0:4, :], in0=self.o1, scalar=self.pp[:, 78:79],
                                                     in1=self.szT, op0=ALU.mult, op1=ALU.mult),
             reads=['N32', 'pp', 'szT'], writes=['mixT'])
        if SUB == 6:
            return self._dn_bail()
        if last and self.kind == 'P':
            S.dma('sp', self.dn_p[L].rearrange("h k v -> k h v"), self.S32, reads=['S32'], writes=[('dn_p', L)])
            self.conv_tail_out(L, 0, 12, self.dnc_p)

    def _dn_bail(self):
        self.S.op('pool', lambda e: e.memset(self.mixT[:, 0:4, :], 0.0), writes=['mixT'])

    def conv_tail_out(self, L, ci0, n, dst):
        S = self.S
        for j in range(n):
            gi = {0: 0, 4: 1, 8: 2}.get(((ci0 + j) // 4) * 4, None) if ci0 + j < 12 else (3 if ci0 + j < 16 else 4)
            S.dma('sp', dst[L][:, j * 128:(j + 1) * 128].rearrange("r p -> p r"), self.rawP[:, ci0 + j, 0:3],
                  reads=['raw%d' % gi], writes=[('ctail', L, ci0 + j)], allow_slow_non_contiguous=True)

    def ssd(self, L, t, last):
        S = self.S
        psT, psB, sm, pb = self.psT, self.psB, self.sm, self.pb
        dt, glast = sm[:, 20:24], sm[:, 52:60]
        ss2, rstd = sm[:, 64:66], sm[:, 66:68]
        isP = (self.kind == 'P')
        yb = 3 if isP else 0
        for c in range(2):
            S.op('pe', lambda e, c=c: e.transpose(out=psB[3][:, c * 128:(c + 1) * 128], in_=self.xT32[:, c, :], identity=self.identF),
                 reads=['ssmxbc3', 'cf'], writes=['psB3'])
        S.op('act', lambda e: e.copy(out=self.xtok, in_=psB[3][:, 0:256]), reads=['psB3'], writes=['xtok'])
        S.op('dve', lambda e: e.tensor_tensor(out=self.xdt, in0=self.xtok.rearrange("p (a b) -> p a b", a=4),
                                              in1=dt.unsqueeze(2).broadcast_to([128, 4, 64]), op=ALU.mult),
             reads=['xtok', 'sp'], writes=['xdt'])
        for g in range(2):
            S.op('pe', lambda e, g=g: e.transpose(out=psT[:, 512 + g * 128:512 + (g + 1) * 128], in_=self.BT[:, g, :], identity=self.identB),
                 reads=['ssmxbc3', 'cb'], writes=['psT'])
        yield
        for h in range(4):
            g = h // 2
            kdc = self.kdS[:, h:h + 1]
            S.op('dve', lambda e, h=h, g=g, kdc=kdc: e.tensor_scalar(out=self.Bd[:, h, :], in0=psT[:, 512 + g * 128:512 + (g + 1) * 128],
                                                                     scalar1=kdc, scalar2=None, op0=ALU.mult),
                 reads=['psT', 'kdv'] + ['Ui%d' % (4 + h)], writes=['Bd'])
        for g in range(2):
            S.op('pe', lambda e, g=g: e.matmul(psB[3][:, 256 + g * 128:256 + (g + 1) * 128], lhsT=self.BT[:, g, :], rhs=self.CT[:, g, :],
                                               start=True, stop=True), reads=['ssmxbc3', 'ssmxbc4'], writes=['psB3'])
        yield
        for h in range(4):
            g = h // 2
            S.op('dve', lambda e, h=h, g=g: e.tensor_tensor(out=self.sattnT[:, h, :], in0=psB[3][:, 256 + g * 128:256 + (g + 1) * 128],
                                                            in1=self.Ui[:, 4 + h, :], op=ALU.mult),
                 reads=['psB3', 'Ui%d' % (4 + h)], writes=['sattnT'])
            S.op('pool', lambda e, h=h, g=g: e.tensor_tensor(out=self.CdT[:, h, :], in0=self.CT[:, g, :], in1=self.E[:, 4 + h, :],
                                                             op=ALU.mult), reads=['ssmxbc4', 'E%d' % (4 + h)], writes=['CdT'])
        yield
        if self.kind == 'S':
            self.ssd_core_S(L)
        else:
            yield from self.ssd_core(L, t)
        yield
        for h in range(4):
            S.op('dve', lambda e, h=h: e.scalar_tensor_tensor(out=self.y2[:, h * 64:(h + 1) * 64], in0=self.xtok[:, h * 64:(h + 1) * 64],
                                                              scalar=pb[:, 16 + h:17 + h], in1=psB[yb][:, h * 64:(h + 1) * 64],
                                                              op0=ALU.mult, op1=ALU.add),
                 reads=['xtok', 'pb', 'psB%d' % yb], writes=['y2'])
        S.op('pool', lambda e: e.tensor_tensor(out=self.y2, in0=self.y2, in1=self.sz, op=ALU.mult), reads=['y2', 'sz'], writes=['y2'])
        yield
        for g in range(2):
            S.op('act', lambda e, g=g: e.activation(out=self.y4[:, g * 128:(g + 1) * 128], in_=self.y2[:, g * 128:(g + 1) * 128],
                                                    func=AF.Square, accum_out=ss2[:, g:g + 1]), reads=['y2'], writes=['y4', 'ss2'])
        S.op('act', lambda e: e.activation(out=rstd, in_=ss2, func=AF.Sqrt, scale=1.0 / 128, bias=self.epsc),
             reads=['ss2'], writes=['rstd2'])
        S.op('dve', lambda e: e.reciprocal(out=rstd, in_=rstd), reads=['rstd2'], writes=['rstd2'])
        yield
        for g in range(2):
            S.op('dve', lambda e, g=g: e.scalar_tensor_tensor(out=self.y4[:, g * 128:(g + 1) * 128], in0=self.y2[:, g * 128:(g + 1) * 128],
                                                              scalar=rstd[:, g:g + 1], in1=pb[:, 32 + g * 128:32 + (g + 1) * 128],
                                                              op0=ALU.mult, op1=ALU.mult),
                 reads=['y2', 'rstd2', 'pb', 'y4'], writes=['y4'])
        for g in range(2):
            S.op('pe', lambda e, g=g: e.transpose(out=psT[:, 512 + g * 128:512 + (g + 1) * 128], in_=self.y4[:, g * 128:(g + 1) * 128],
                                                  identity=self.identB), reads=['y4', 'cb'], writes=['psT'])
        S.op('act', lambda e: e.copy(out=self.mixT[:, 4:6, :], in_=psT[:, 512:768].rearrange("p (a b) -> p a b", a=2)),
             reads=['psT'], writes=['mixT'])
        if last:
            self.ssm_state_out(self.ssm_p[L], self.h32, 'h32')
            self.conv_tail_out(L, 12, 6, self.ssmc_p)

    def ssm_state_out(self, dst, h32, hkey):
        S, psB = self.S, self.psB
        for c in range(2):
            S.op('pe', lambda e, c=c: e.transpose(out=psB[3][:, c * 128:(c + 1) * 128], in_=h32[:, c * 128:(c + 1) * 128],
                                                  identity=self.identF), reads=[hkey, 'cf'], writes=['psB3'])
        S.op('act', lambda e: e.copy(out=self.hout, in_=psB[3][:, 0:256].rearrange("p (a b) -> p a b", a=2)),
             reads=['psB3'], writes=['hout'])
        S.dma('sp', dst.rearrange("(c h2) p n -> (h2 p) c n", c=2), self.hout, reads=['hout'], writes=[('ssmout', id(dst))])

    def ssd_core(self, L, t):
        S, psB, sm = self.S, self.psB, self.sm
        glast = sm[:, 52:60]
        for h in range(4):
            S.op('pe', lambda e, h=h: e.matmul(psB[3][:, h * 64:(h + 1) * 64], lhsT=self.sattnT[:, h, :], rhs=self.xdt[:, h, :],
                                               start=True, stop=False), reads=['sattnT', 'xdt'], writes=['psB3'])
            S.op('pe', lambda e, h=h: e.matmul(psB[3][:, h * 64:(h + 1) * 64], lhsT=self.CdT[:, h, :], rhs=self.hb[:, h * 64:(h + 1) * 64],
                                               start=False, stop=True), reads=['CdT', 'hb'], writes=['psB3'])
        yield
        for h in range(4):
            S.op('pe', lambda e, h=h: e.matmul(psB[3][:, 256 + h * 64:256 + (h + 1) * 64], lhsT=self.Bd[:, h, :], rhs=self.xdt[:, h, :],
                                               start=True, stop=True), reads=['Bd', 'xdt'], writes=['psB3'])
        for h in range(4):
            S.op('dve', lambda e, h=h: e.scalar_tensor_tensor(out=self.h32[:, h * 64:(h + 1) * 64], in0=self.h32[:, h * 64:(h + 1) * 64],
                                                              scalar=glast[:, 4 + h:5 + h], in1=psB[3][:, 256 + h * 64:256 + (h + 1) * 64],
                                                              op0=ALU.mult, op1=ALU.add),
                 reads=['h32', 'glast', 'psB3'], writes=['h32'])
        yield
        S.op('pool', lambda e: e.tensor_copy(out=self.hb, in_=self.h32), reads=['h32'], writes=['hb'])

    def swa(self, L, t, last):
        S = self.S
        psT, psB, sm, pb = self.psT, self.psB, self.sm, self.pb
        rot = self.rot
        negm, rsum, es, rden = sm[:, 60:61], sm[:, 61:62], sm[:, 62:63], sm[:, 63:64]
        cur, prev = t % 2, 1 - (t % 2)
        kind = self.kind
        mask = self.swaMP0 if t == 0 else (self.swaMP1 if t == 1 else self.swaMP)
        if kind == 'S':
            mask = self.swaMS
        isP = (kind == 'P')
        psA = self.psA
        rotb, rotk = (psA[:, 0:384], 'psA0') if isP else (psB[3][:, 0:384], 'psB3')
        vtb, vtk = (psA[:, 384:512], 'psA0') if isP else (psB[4][:, 0:128], 'psB4')
        rb = rot('ropet')
        rt = self.ropet[rb]
        S.dma('pool', rt, self.roped[:, :, t * 128:(t + 1) * 128].rearrange("a p n -> p a n"), writes=['ropet%d' % rb])
        for j in range(3):
            S.op('pe', lambda e, j=j: e.matmul(rotb[:, j * 128:(j + 1) * 128], lhsT=self.Rm, rhs=self.sw32[:, j, :],
                                               start=True, stop=True), reads=['sw32', 'cf'], writes=[rotk])
        S.op('dve', lambda e: e.tensor_tensor(out=self.r1, in0=self.sw32[:, 0:3, :],
                                              in1=rt[:, 0, :].unsqueeze(1).broadcast_to([128, 3, 128]), op=ALU.mult),
             reads=['sw32', 'ropet%d' % rb], writes=['r1'])
        yield
        S.op('dve', lambda e: e.tensor_tensor(out=self.r2, in0=rotb.rearrange("p (a b) -> p a b", a=3),
                                              in1=rt[:, 1, :].unsqueeze(1).broadcast_to([128, 3, 128]), op=ALU.mult),
             reads=[rotk, 'ropet%d' % rb], writes=['r2'])
        S.op('pool', lambda e: e.tensor_tensor(out=self.r1, in0=self.r1, in1=self.r2, op=ALU.add), reads=['r1', 'r2'], writes=['r1'])
        for j in range(2):
            for r in range(2):
                S.op('act', lambda e, j=j, r=r: e.copy(out=self.qz[r * 64:(r + 1) * 64, r * 2 + j, :], in_=self.r1[r * 64:(r + 1) * 64, j, :]),
                     reads=['r1'], writes=['qz'])
        S.op('pool', lambda e: e.tensor_copy(out=self.kring[:, cur, :], in_=self.r1[:, 2, :]), reads=['r1'], writes=['kring%d' % cur])
        yield
        S.op('pe', lambda e: e.transpose(out=vtb, in_=self.sw32[:, 3, :], identity=self.identF),
             reads=['sw32', 'cf'], writes=[vtk])
        S.op('act', lambda e: e.copy(out=self.v32, in_=vtb), reads=[vtk], writes=['v32'])
        S.op('dve', lambda e: e.tensor_copy(out=self.vring[:, cur, 0, 0:64], in_=vtb[:, 0:64]), reads=[vtk], writes=['vring%d' % cur])
        S.op('dve', lambda e: e.tensor_copy(out=self.vring[:, cur, 1, 64:128], in_=vtb[:, 64:128]), reads=[vtk], writes=['vring%d' % cur])
        yield
        if kind == 'S':
            self.swa_cache_load(L)
        yield from self.swa_attend(L, t, mask, lambda r: [(self.kring[:, prev, :], 'kring%d' % prev)],
                                   lambda r: [(self.vring[:, prev, r, :], 'vring%d' % prev)], cur)
        if kind == 'S':
            S.op('pe', lambda e: e.transpose(out=psB[4][:, 128:256], in_=self.r1[:, 2, :], identity=self.identF),
                 reads=['r1', 'cf'], writes=['psB4'])
            S.op('act', lambda e: e.copy(out=self.r2[:, 0, :], in_=psB[4][:, 128:256]), reads=['psB4'], writes=['r2'])
            for s in range(16):
                S.dma('sp', self.k_s[L][s, 120:128, :], self.r2[8 * s:8 * s + 8, 0, :], reads=['r2'], writes=[('k_s', L, 1, s)])
                S.dma('pool', self.v_s[L][s, 120:128, :], self.v32[8 * s:8 * s + 8, :], reads=['v32'], writes=[('v_s', L, 1, s)])
            S.dma('pool', self.k_s[L][:, 0:120, :], self.ck[L][:, 8:128, :], writes=[('k_s', L, 0)])
            S.dma('pool', self.v_s[L][:, 0:120, :], self.cvd[L][:, 8:128, :], writes=[('v_s', L, 0)])
        if last and kind == 'P':
            S.op('pe', lambda e: e.transpose(out=vtb, in_=self.r1[:, 2, :], identity=self.identF),
                 reads=['r1', 'cf'], writes=[vtk])
            S.op('act', lambda e: e.copy(out=self.r2[:, 0, :], in_=vtb), reads=[vtk], writes=['r2'])
            S.dma('sp', self.k_p[L], self.r2[:, 0, :], reads=['r2'], writes=[('k_p', L)])
            S.dma('sp', self.v_p[L], self.v32, reads=['v32'], writes=[('v_p', L)])

    def swa_attend(self, L, t, mask, kprev, vprev, cur):
        S = self.S
        psT, psB, sm, pb = self.psT, self.psB, self.sm, self.pb
        rot = self.rot
        negm, rsum, es, rden = sm[:, 60:61], sm[:, 61:62], sm[:, 62:63], sm[:, 63:64]
        e32 = self.e32[0]
        isP = (self.kind == 'P')
        pvb, pvk = (psB[4], 'psB4') if isP else (psB[2], 'psB2')
        for j in range(2):
            for r in range(2):
                hq = r * 2 + j
                bi = hq % 2
                bank, bk = (self.psA[:, 512:1024], 'psA1') if isP else (psB[bi], 'psB%d' % bi)
                (kp, kpk), = kprev(r)
                if self.kind == 'S':
                    S.op('pool', lambda e, hq=hq: e.tensor_copy(out=self.ZW[:, :, 120:128],
                                                                in_=self.qz[:, hq, :].rearrange("p (s u) -> p s u", u=8)),
                         reads=['qz'], writes=['ZW'])
                    for s in range(16):
                        S.op('pe', lambda e, s=s, bank=bank: e.matmul(bank[:, 0:128], lhsT=self.ZW[:, s, 120 - 8 * s:248 - 8 * s],
                                                                      rhs=self.KcT[:, s, :], start=(s == 0), stop=(s == 15)),
                             reads=['ZW', 'KcT'], writes=[bk])
                else:
                    S.op('pe', lambda e, hq=hq, bank=bank, kp=kp: e.matmul(bank[:, 0:128], lhsT=self.qz[:, hq, :],
                                                                           rhs=kp, start=True, stop=True),
                         reads=['qz', kpk], writes=[bk])
                S.op('pe', lambda e, hq=hq, bank=bank: e.matmul(bank[:, 128:256], lhsT=self.qz[:, hq, :],
                                                                rhs=self.kring[:, cur, :], start=True, stop=True),
                     reads=['qz', 'kring%d' % cur], writes=[bk])
                yield
                S.op('dve', lambda e, bank=bank: e.reduce_max(out=rden, in_=bank[:, 0:256], axis=AX.X), reads=[bk], writes=['rden'])
                S.op('dve', lambda e, hq=hq: e.tensor_scalar(out=negm, in0=rden, scalar1=-0.125, scalar2=self.negsink[:, hq:hq + 1],
                                                             op0=ALU.mult, op1=ALU.min), reads=['rden', 'negsink'], writes=['negm'])
                S.op('act', lambda e, bank=bank: e.activation(out=e32, in_=bank[:, 0:256], func=AF.Exp, bias=negm, scale=0.125),
                     reads=[bk, 'negm'], writes=['e32'])
                S.op('dve', lambda e: e.scalar_tensor_tensor(out=self.em, in0=e32, scalar=1.0, in1=mask, op0=ALU.mult, op1=ALU.mult,
                                                             accum_out=rsum), reads=['e32', 'cf'], writes=['em', 'rsum'])
                yield
                S.op('act', lambda e, hq=hq: e.activation(out=es, in_=negm, func=AF.Exp, bias=pb[:, 20 + hq:21 + hq]),
                     reads=['negm', 'pb'], writes=['es'])
                S.op('dve', lambda e: e.tensor_tensor(out=rden, in0=rsum, in1=es, op=ALU.add), reads=['rsum', 'es'], writes=['rden'])
                S.op('dve', lambda e: e.reciprocal(out=rden, in_=rden), reads=['rden'], writes=['rden'])
                S.op('dve', lambda e: e.tensor_scalar(out=self.pn, in0=self.em, scalar1=rden, scalar2=None, op0=ALU.mult),
                     reads=['em', 'rden'], writes=['pn'])
                yield
                for q in range(2):
                    S.op('pe', lambda e, q=q: e.transpose(out=psT[:, 768 + q * 128:768 + (q + 1) * 128], in_=self.pn[:, q * 128:(q + 1) * 128],
                                                          identity=self.identB), reads=['pn', 'cb'], writes=['psT'])
                pb_ = rot('pnT')
                pnT = self.pnT[pb_]
                S.op('act', lambda e, pnT=pnT: e.copy(out=pnT, in_=psT[:, 768:1024].rearrange("p (a b) -> p a b", a=2)),
                     reads=['psT'], writes=['pnT%d' % pb_])
                yield
                out = pvb[:, j * 128:(j + 1) * 128]
                (vp, vpk), = vprev(r)
                if self.kind == 'S':
                    S.op('pe', lambda e, out=out, pnT=pnT, r=r: e.matmul(out, lhsT=self.vring[:, cur, r, :], rhs=pnT[:, 1, :],
                                                                         start=(r == 0), stop=False),
                         reads=['vring%d' % cur, 'pnT%d' % pb_], writes=[pvk])
                    for s in range(16):
                        S.op('pe', lambda e, s=s, j=j, pnT=pnT, r=r: e.matmul(
                            pvb[:, j * 128 + 8 * s:j * 128 + 8 * s + 8], lhsT=self.VA[:, s, r, :], rhs=pnT[:, 0, 8 * s:8 * s + 8],
                            start=False, stop=(r == 1 and s == 15)), reads=['VA', 'pnT%d' % pb_], writes=[pvk])
                else:
                    S.op('pe', lambda e, out=out, vp=vp, pnT=pnT, r=r: e.matmul(out, lhsT=vp, rhs=pnT[:, 0, :], start=(r == 0), stop=False),
                         reads=[vpk, 'pnT%d' % pb_], writes=[pvk])
                    S.op('pe', lambda e, out=out, pnT=pnT, r=r: e.matmul(out, lhsT=self.vring[:, cur, r, :], rhs=pnT[:, 1, :],
                                                                         start=False, stop=(r == 1)),
                         reads=['vring%d' % cur, 'pnT%d' % pb_], writes=[pvk])
            S.op('act', lambda e, j=j: e.copy(out=self.mixT[:, 6 + j, :], in_=pvb[:, j * 128:(j + 1) * 128]),
                 reads=[pvk], writes=['mixT'])
            yield

    def finish(self, L):
        pass

    RAWK = {0: 'raw0', 1: 'raw1', 2: 'raw2', 3: 'raw3', 4: 'raw4'}

    def _rawkeys(self, ci0, n):
        ks = []
        for ci in range(ci0, ci0 + n):
            k = 'raw%d' % (ci // 4 if ci < 16 else 4)
            if k not in ks:
                ks.append(k)
        return ks

    def conv_hist_load(self, L):
        S, psB = self.S, self.psB
        for (src, c0, ci0) in [(self.sdnc[L], 0, 0), (self.sdnc[L], 768, 6), (self.sssmc[L], 0, 12)]:
            S.dma('sp', self.cstg[0:48, :], src[:, :, c0:c0 + 768].rearrange("s r c -> (s r) c"), writes=['cstg'])
            for k in range(6):
                S.op('pe', lambda e, k=k: e.transpose(out=psB[2][:, k * 48:(k + 1) * 48], in_=self.cstg[0:48, k * 128:(k + 1) * 128],
                                                      identity=self.identF[0:48, 0:48]), reads=['cstg', 'cf'], writes=['psB2'])
            for k in range(6):
                S.op('act', lambda e, k=k, ci0=ci0: e.copy(out=self.rawS[:, ci0 + k, :, 0:3],
                                                           in_=psB[2][:, k * 48:(k + 1) * 48].rearrange("p (s r) -> p s r", r=3)),
                     reads=['psB2'], writes=self._rawkeys(ci0 + k, 1))

    def conv_tail_out_S(self, L):
        S, psB = self.S, self.psB
        for (dst, col0, ci0, n) in [(self.dnc_s[L], 0, 0, 4), (self.dnc_s[L], 512, 4, 4), (self.dnc_s[L], 1024, 8, 4),
                                    (self.ssmc_s[L], 0, 12, 4), (self.ssmc_s[L], 512, 16, 2)]:
            for k in range(n):
                S.op('pool', lambda e, k=k, ci0=ci0: e.tensor_copy(out=self.ctmp[:, k, :].rearrange("p (s r) -> p s r", r=3),
                                                                   in_=self.rawS[:, ci0 + k, :, 8:11]),
                     reads=self._rawkeys(ci0 + k, 1), writes=['ctmp'])
            for k in range(n):
                S.op('pe', lambda e, k=k: e.transpose(out=psB[2][0:48, k * 128:(k + 1) * 128], in_=self.ctmp[:, k, :],
                                                      identity=self.identF), reads=['ctmp', 'cf'], writes=['psB2'])
            S.op('act', lambda e, n=n: e.copy(out=self.cstg[0:48, 0:n * 128], in_=psB[2][0:48, 0:n * 128]),
                 reads=['psB2'], writes=['cstg'])
            S.dma('sp', dst[:, :, col0:col0 + n * 128].rearrange("s r c -> (s r) c"), self.cstg[0:48, 0:n * 128],
                  reads=['cstg'], writes=[('ctailS', L, ci0)])

    def dn_state_S(self, L, Yb, ykey):
        S, psB, rot = self.S, self.psB, self.rot
        for h in range(4):
            S.dma('sp', self.Sall, self.sdn[L][:, h].rearrange("s k v -> k s v"), writes=['Sall'])
            S.op('pool', lambda e: e.tensor_copy(out=self.Sball, in_=self.Sall), reads=['Sall'], writes=['Sball'])
            S.op('pe', lambda e, h=h: e.matmul(psB[3][:, 0:128], lhsT=self.Kbg[:, h, :], rhs=Yb[:, h, :], start=True, stop=True),
                 reads=['Kbg', ykey], writes=['psB3'])
            S.op('act', lambda e: e.mul(out=self.ZW[:, :, 120:128], in_=psB[3][:, 0:128].rearrange("p (s u) -> p s u", u=8), mul=-1.0),
                 reads=['psB3'], writes=['ZW'])
            S.op('pe', lambda e, h=h: e.matmul(psB[4][:, 0:128], lhsT=Yb[:, h, :], rhs=self.Vb[:, h, :], start=True, stop=False),
                 reads=[ykey, 'Vb'], writes=['psB4'])
            for s in range(16):
                S.op('pe', lambda e, s=s: e.matmul(psB[4][:, 0:128], lhsT=self.ZW[:, s, 120 - 8 * s:248 - 8 * s], rhs=self.Sball[:, s, :],
                                                   start=False, stop=(s == 15)), reads=['ZW', 'Sball'], writes=['psB4'])
            S.op('act', lambda e, h=h: e.copy(out=self.vnew[:, h, :], in_=psB[4][:, 0:128]), reads=['psB4'], writes=['vnew'])
            S.op('pe', lambda e, h=h: e.matmul(psB[0][:, h * 128:(h + 1) * 128], lhsT=self.vnew[:, h, :], rhs=self.attnT[:, h, :],
                                               start=True, stop=False), reads=['vnew', 'attnT'], writes=['psB0'])
            for s in range(16):
                S.op('pe', lambda e, h=h, s=s: e.matmul(psB[0][:, h * 128 + 8 * s:h * 128 + 8 * s + 8], lhsT=self.Sball[:, s, :],
                                                        rhs=self.QdT[:, h, 8 * s:8 * s + 8], start=False, stop=(s == 15)),
                     reads=['Sball', 'QdT'], writes=['psB0'])
            for s in range(16):
                kb = rot('KdM')
                bi = 1 if (s // 4) % 2 == 0 else 3
                col = (s % 4) * 128
                S.op('dve', lambda e, h=h, s=s, kb=kb: e.tensor_scalar(out=self.KdM[kb], in0=self.Kd[:, h, :], scalar1=self.rowm[:, s:s + 1],
                                                                       scalar2=None, op0=ALU.mult), reads=['Kd', 'cf'], writes=['KdM%d' % kb])
                S.op('pe', lambda e, h=h, kb=kb, bi=bi, col=col: e.matmul(psB[bi][:, col:col + 128], lhsT=self.KdM[kb], rhs=self.vnew[:, h, :],
                                                                          start=True, stop=True),
                     reads=['KdM%d' % kb, 'vnew'], writes=['psB%d' % bi])
                S.op('dve', lambda e, h=h, s=s, bi=bi, col=col: e.scalar_tensor_tensor(
                    out=self.Sall[:, s, :], in0=self.Sall[:, s, :], scalar=self.glS[:, h, s:s + 1], in1=psB[bi][:, col:col + 128],
                    op0=ALU.mult, op1=ALU.add), reads=['Sall', 'glS', 'psB%d' % bi], writes=['Sall'])
            S.dma('sp', self.dn_s[L][:, h].rearrange("s k v -> k s v"), self.Sall, reads=['Sall'], writes=[('dn_s', L, h)])

    def ssd_core_S(self, L):
        S, psA, psB, rot = self.S, self.psA, self.psB, self.rot
        for h in range(4):
            S.dma('sp', self.hin[0:64, :, :], self.sssm[L][:, h].rearrange("s p n -> p s n"), writes=['hin'])
            for s in range(16):
                S.op('pe', lambda e, s=s: e.transpose(out=psA[:, s * 64:(s + 1) * 64], in_=self.hin[0:64, s, :],
                                                      identity=self.identF[0:64, 0:64]), reads=['hin', 'cf'], writes=['psA0', 'psA1'])
            S.op('act', lambda e: e.copy(out=self.Hh, in_=psA[:, :].rearrange("p (s q) -> p s q", q=64)),
                 reads=['psA0', 'psA1'], writes=['Hh'])
            S.op('pool', lambda e: e.tensor_copy(out=self.Hhb, in_=self.Hh), reads=['Hh'], writes=['Hhb'])
            S.op('pool', lambda e, h=h: e.tensor_copy(out=self.ZW[:, :, 120:128], in_=self.CdT[:, h, :].rearrange("p (s u) -> p s u", u=8)),
                 reads=['CdT'], writes=['ZW'])
            S.op('pe', lambda e, h=h: e.matmul(psB[0][:, h * 64:(h + 1) * 64], lhsT=self.sattnT[:, h, :], rhs=self.xdt[:, h, :],
                                               start=True, stop=False), reads=['sattnT', 'xdt'], writes=['psB0'])
            for s in range(16):
                S.op('pe', lambda e, h=h, s=s: e.matmul(psB[0][:, h * 64:(h + 1) * 64], lhsT=self.ZW[:, s, 120 - 8 * s:248 - 8 * s],
                                                        rhs=self.Hhb[:, s, :], start=False, stop=(s == 15)),
                     reads=['ZW', 'Hhb'], writes=['psB0'])
            for s in range(16):
                kb = rot('KdM')
                bi = 1 if (s // 8) % 2 == 0 else 4
                col = (s % 8) * 64
                S.op('dve', lambda e, h=h, s=s, kb=kb: e.tensor_scalar(out=self.KdM[kb], in0=self.Bd[:, h, :], scalar1=self.rowm[:, s:s + 1],
                                                                       scalar2=None, op0=ALU.mult), reads=['Bd', 'cf'], writes=['KdM%d' % kb])
                S.op('pe', lambda e, h=h, kb=kb, bi=bi, col=col: e.matmul(psB[bi][:, col:col + 64], lhsT=self.KdM[kb], rhs=self.xdt[:, h, :],
                                                                          start=True, stop=True),
                     reads=['KdM%d' % kb, 'xdt'], writes=['psB%d' % bi])
                S.op('dve', lambda e, h=h, s=s, bi=bi, col=col: e.scalar_tensor_tensor(
                    out=self.Hh[:, s, :], in0=self.Hh[:, s, :], scalar=self.glS[:, 4 + h, s:s + 1], in1=psB[bi][:, col:col + 64],
                    op0=ALU.mult, op1=ALU.add), reads=['Hh', 'glS', 'psB%d' % bi], writes=['Hh'])
            for half in range(2):
                for s8 in range(8):
                    S.op('pe', lambda e, half=half, s8=s8: e.transpose(out=psA[0:64, s8 * 128:(s8 + 1) * 128],
                                                                       in_=self.Hh[:, half * 8 + s8, :], identity=self.identF),
                         reads=['Hh', 'cf'], writes=['psA0', 'psA1'])
                S.op('act', lambda e, half=half: e.copy(out=self.hin[0:64, half * 8:(half + 1) * 8, :],
                                                        in_=psA[0:64, :].rearrange("p (s n) -> p s n", n=128)),
                     reads=['psA0', 'psA1'], writes=['hin'])
            S.dma('sp', self.ssm_s[L][:, h].rearrange("s p n -> p s n"), self.hin[0:64, :, :], reads=['hin'], writes=[('ssm_s', L, h)])

    def swa_cache_load(self, L):
        S, psB, rot = self.S, self.psB, self.rot
        S.op('pool', lambda e: e.memset(self.VA, 0.0), writes=['VA'])
        for s in range(16):
            kb = rot('kst')
            S.dma('sp', self.kst[:, kb, :], self.ck[L][s], writes=['kst%d' % kb])
            S.op('pe', lambda e, s=s, kb=kb: e.transpose(out=psB[3][:, (s % 4) * 128:(s % 4 + 1) * 128], in_=self.kst[:, kb, :],
                                                         identity=self.identF), reads=['kst%d' % kb, 'cf'], writes=['psB3'])
            if s % 4 == 3:
                S.op('act', lambda e, s=s: e.copy(out=self.KcT[:, s - 3:s + 1, :], in_=psB[3][:, :].rearrange("p (a b) -> p a b", a=4)),
                     reads=['psB3'], writes=['KcT'])
            vb = rot('vst')
            S.dma('pool', self.vst[:, vb, :], self.cvd[L][s], writes=['vst%d' % vb])
            S.op('dve', lambda e, s=s, vb=vb: e.tensor_copy(out=self.VA[:, s, 0, 0:64], in_=self.vst[:, vb, 0:64]),
                 reads=['vst%d' % vb], writes=['VA'])
            S.op('dve', lambda e, s=s, vb=vb: e.tensor_copy(out=self.VA[:, s, 1, 64:128], in_=self.vst[:, vb, 64:128]),
                 reads=['vst%d' % vb], writes=['VA'])

D = 1024
DFF = 2816
NCI = 3600
EPS = 1e-6
PAST_LEN = 8192
NSEQ = 16


class Arena:
    def __init__(self, nc, nbytes):
        self.t = nc.alloc_sbuf_tensor('arena', [128, nbytes // 2], BF16)
        self.n = nbytes // 2
        self.top = 0
        self.marks = []

    def alloc(self, shape, dt=F32):
        per = 1
        for s in shape[1:]:
            per *= s
        ne = per * (2 if dt == F32 else 1)
        ne = (ne + 15) // 16 * 16
        assert self.top + ne <= self.n, "SBUF arena overflow %d" % ((self.top + ne) * 2)
        v = self.t[0:shape[0], self.top:self.top + (per * (2 if dt == F32 else 1))]
        self.top += ne
        if dt == F32:
            v = v.bitcast(F32)
        if len(shape) == 3:
            v = v.rearrange("p (a b) -> p a b", a=shape[1])
        elif len(shape) == 4:
            v = v.rearrange("p (a b c) -> p a b c", a=shape[1], b=shape[2])
        return v

    def mark(self):
        self.marks.append(self.top)

    def release(self):
        self.top = self.marks.pop()


DBG = []


def build_program(NT, do_sample=True, stage=99):
    nc = bass.Bass("TRN2", target_bir_lowering=False)
    S = Sched(nc)
    NTILE = NT + 2
    SMP = NT + 1

    def din(name, shape, dt=F32):
        return nc.dram_tensor(name, list(shape), dt, kind="ExternalInput").ap()

    def dout(name, shape):
        return nc.dram_tensor(name, list(shape), F32, kind="ExternalOutput").ap()

    def dscr(name, shape):
        return nc.dram_tensor(name, list(shape), F32, kind="Internal").ap()

    xin = din("xin", [NTILE * 128, D])
    w_in = din("w_in", [2, D, NCI])
    w_out = din("w_out", [2, D, D])
    w_fi = din("w_fi", [2, D, 2 * DFF])
    w_fo = din("w_fo", [2, DFF, D])
    gbd = din("gb", [2, 4, 128, D])
    ppd = din("pp", [2, 128, 80])
    pbd = din("pb", [2, 128, 288])
    cfd = din("cf", [128, 2560])
    cbd = din("cb", [128, 256], BF16)
    roped = din("rope", [2, 128, NTILE * 128])
    y_o = dout("y", [(NT + 1) * 128, D])
    dn_p = dout("dn_p", [2, 4, 128, 128])
    dnc_p = dout("dnc_p", [2, 3, 1536])
    ssm_p = dout("ssm_p", [2, 4, 64, 128])
    ssmc_p = dout("ssmc_p", [2, 3, 768])
    k_p = dout("k_p", [2, 128, 128])
    v_p = dout("v_p", [2, 128, 128])
    smp_io = None
    if do_sample:
        smp_io = (din("sdn", [2, 16, 4, 128, 128]), din("sdnc", [2, 16, 3, 1536]), din("sssm", [2, 16, 4, 64, 128]),
                  din("sssmc", [2, 16, 3, 768]), din("ck", [2, 16, 128, 128]), din("cv", [2, 16, 128, 128]),
                  dout("dn_s", [2, 16, 4, 128, 128]), dout("dnc_s", [2, 16, 3, 1536]), dout("ssm_s", [2, 16, 4, 64, 128]),
                  dout("ssmc_s", [2, 16, 3, 768]), dout("k_s", [2, 16, 128, 128]), dout("v_s", [2, 16, 128, 128]))
    xa = dscr("xa", [NTILE * 128, D])
    xb = dscr("xb", [NTILE * 128, D])

    psA = nc.alloc_psum_tensor("psA", [128, 1024], F32)
    psT = nc.alloc_psum_tensor("psT", [128, 1024], BF16)
    psB = [nc.alloc_psum_tensor("psB%d" % i, [128, 512], F32) for i in range(5)]

    ar = Arena(nc, 208000)
    cf = ar.alloc([128, 2560])
    cb = ar.alloc([128, 256], BF16)
    S.dma('sp', cf, cfd, writes=['cf'])
    S.dma('sp', cb, cbd, writes=['cb'])
    identF = cf[:, 0:128]
    mUiP = cf[:, 128:256]
    mLsP = cf[:, 256:384]
    Rm = cf[:, 384:512]
    swaMP = cf[:, 512:768]
    swaMP0 = cf[:, 768:1024]
    swaMP1 = cf[:, 2304:2560]
    validc = cf[:, 1024:1025]
    identB = cb[:, 0:128]
    onesB = cb[:, 128:256]

    dbg_list = DBG
    def dbg_out(name, ap, keys, dt=F32):
        if name not in dbg_list:
            return
        shp = list(ap.shape)
        d = nc.dram_tensor("dbg_" + name, shp, dt, kind="ExternalOutput").ap()
        S.dma('sp', d, ap, reads=keys, writes=[('dbg', name)])

    rr = {}

    def rot(name, n=2):
        i = rr.get(name, 0)
        rr[name] = i + 1
        return i % n

    def engcycle(name, engs):
        return engs[rot('ec_' + name, len(engs))]

    def load_weights_bf16(dst, src, nrow_chunks, ncols, stg, tag, colblk):
        i = 0
        for rc in range(nrow_chunks):
            for c0 in range(0, ncols, colblk):
                cw = min(colblk, ncols - c0)
                b = rot('stg')
                S.dma('sp' if i % 2 == 0 else 'pool', stg[b][:, 0:cw], src[rc * 128:(rc + 1) * 128, c0:c0 + cw],
                      writes=['stg%d' % b])
                eng = ('act', 'dve', 'pool')[i % 3]
                if eng == 'act':
                    S.op('act', lambda e, b=b, rc=rc, c0=c0, cw=cw: e.copy(out=dst[:, rc, c0:c0 + cw], in_=stg[b][:, 0:cw]),
                         reads=['stg%d' % b], writes=[tag])
                else:
                    S.op(eng, lambda e, b=b, rc=rc, c0=c0, cw=cw: e.tensor_copy(out=dst[:, rc, c0:c0 + cw], in_=stg[b][:, 0:cw]),
                         reads=['stg%d' % b], writes=[tag])
                i += 1

    def rms_rstd(src_ap, src_keys, n, junk, ss, rs, tag):
        S.op('act', lambda e: e.activation(out=junk, in_=src_ap, func=AF.Square, accum_out=ss),
             reads=src_keys, writes=['xn', tag + 'ss'])
        S.op('act', lambda e: e.activation(out=rs, in_=ss, func=AF.Sqrt, scale=1.0 / n, bias=epsc),
             reads=[tag + 'ss'], writes=[tag + 'rs'])
        S.op('dve', lambda e: e.reciprocal(out=rs, in_=rs), reads=[tag + 'rs'], writes=[tag + 'rs'])

    def norm_to_T(xt, xkey, gB, hTdst, hkey, junk, xn, ss, rs):
        rms_rstd(xt, [xkey], D, junk, ss, rs, 'nt')
        S.op('dve', lambda e: e.scalar_tensor_tensor(out=xn, in0=xt, scalar=rs, in1=gB, op0=ALU.mult, op1=ALU.mult),
             reads=[xkey, 'ntrs', 'gB'], writes=['xn'])
        for kc in range(8):
            S.op('pe', lambda e, kc=kc: e.transpose(out=psT[:, kc * 128:(kc + 1) * 128], in_=xn[:, kc * 128:(kc + 1) * 128],
                                                     identity=identB),
                 reads=['xn', 'cb'], writes=['psT'])
        S.op('act', lambda e: e.copy(out=hTdst, in_=psT[:, :].rearrange("p (a b) -> p a b", a=8)),
             reads=['psT'], writes=[hkey])

    epsc = cf[:, 1025:1026]

    for L in range(2):
        src_x = xin if L == 0 else xb
        ar.mark()
        win = ar.alloc([128, 8, NCI], BF16)
        wout_off = ar.top
        wout = ar.alloc([128, 8, D], BF16)
        gB = ar.alloc([128, 2, D])
        pp = ar.alloc([128, 80])
        pb = ar.alloc([128, 288])
        ar.mark()
        stg = [ar.alloc([128, 900]) for _ in range(2)]
        S.dma('sp', gB, gbd[L, 0:2].rearrange("a p d -> p a d"), writes=['gB'])
        S.dma('sp', pp, ppd[L], writes=['pp'])
        S.dma('sp', pb, pbd[L], writes=['pb'])
        load_weights_bf16(win, w_in[L], 8, NCI, stg, 'win', 900)
        load_weights_bf16(wout, w_out[L], 8, D, stg, 'wout', 512)
        S.barrier()
        ar.release()
        X = [ar.alloc([128, D]) for _ in range(2)]
        xn = ar.alloc([128, D], BF16)
        junk = xn
        hT = [ar.alloc([128, 8, 128], BF16) for _ in range(2)]
        ss = ar.alloc([128, 8])
        rs = ar.alloc([128, 8])
        mixT = ar.alloc([128, 8, 128], BF16)
        negA8 = ar.alloc([128, 8])
        S.op('act', lambda e: e.activation(out=negA8, in_=pb[:, 0:8], func=AF.Exp), reads=['pb'], writes=['negA8'])
        S.op('dve', lambda e: e.tensor_scalar(out=negA8, in0=negA8, scalar1=-1.0, scalar2=None, op0=ALU.mult),
             reads=['negA8'], writes=['negA8'])
        negsink = ar.alloc([128, 4])
        S.op('dve', lambda e: e.tensor_scalar(out=negsink, in0=pb[:, 20:24], scalar1=-1.0, scalar2=None, op0=ALU.mult),
             reads=['pb'], writes=['negsink'])

        mix = Mixers(nc, S, ar, locals())
        tiles = list(range(0, NT + 1)) + ([SMP] if do_sample else [])
        for t in tiles:
            b = rot('X')
            xt = X[b]
            S.dma('sp', xt, src_x[t * 128:(t + 1) * 128, :], reads=[('x%d' % L, t)], writes=['X%d' % b])
            hb = rot('hT')
            norm_to_T(xt, 'X%d' % b, gB[:, 0, :], hT[hb], 'hT%d' % hb, junk, xn, ss[:, 0:1], rs[:, 0:1])
            if t == SMP:
                S.barrier()
            mix.tile(L, t, hT[hb], 'hT%d' % hb)
            if t == SMP:
                S.barrier()
                for rc in range(8):
                    for c0 in (0, 512):
                        S.dma('sp', mix.cstg[:, 0:512], w_out[L][rc * 128:(rc + 1) * 128, c0:c0 + 512], writes=['cstg'])
                        S.op('act', lambda e, rc=rc, c0=c0: e.copy(out=wout[:, rc, c0:c0 + 512], in_=mix.cstg[:, 0:512]),
                             reads=['cstg'], writes=['wout'])
            for half in range(2):
                for mc in range(8):
                    S.op('pe', lambda e, half=half, mc=mc: e.matmul(psA[:, half * 512:(half + 1) * 512], lhsT=mixT[:, mc, :],
                                                                    rhs=wout[:, mc, half * 512:(half + 1) * 512],
                                                                    start=(mc == 0), stop=(mc == 7)),
                         reads=['mixT', 'wout'], writes=['psA%d' % half])
            rms_rstd(psA[:, :], ['psA0', 'psA1'], D, junk, ss[:, 1:2], rs[:, 1:2], 'po')
            S.op('dve', lambda e: e.tensor_tensor(out=psA[:, :], in0=psA[:, :], in1=gB[:, 1, :], op=ALU.mult),
                 reads=['psA0', 'psA1', 'gB'], writes=['psA0', 'psA1'])
            S.op('dve', lambda e, xt=xt: e.scalar_tensor_tensor(out=xt, in0=psA[:, :], scalar=rs[:, 1:2], in1=xt,
                                                                op0=ALU.mult, op1=ALU.add),
                 reads=['psA0', 'psA1', 'pors', 'X%d' % b], writes=['X%d' % b])
            S.dma('sp', xa[t * 128:(t + 1) * 128, :], xt, reads=['X%d' % b], writes=[('xa%d' % L, t)])
        mix.finish(L)
        S.barrier()
        ar.release()

        ar.mark()
        wfi = ar.alloc([128, 8, 2 * DFF], BF16)
        wfo = ar.alloc([128, 22, D], BF16)
        gB = ar.alloc([128, 2, D])
        S.dma('sp', gB, gbd[L, 2:4].rearrange("a p d -> p a d"), writes=['gB'])
        ar.mark()
        stg = [ar.alloc([128, 1408]) for _ in range(2)]
        load_weights_bf16(wfi, w_fi[L], 8, 2 * DFF, stg, 'wfi', 1408)
        load_weights_bf16(wfo, w_fo[L], 22, D, stg, 'wfo', 1024)
        S.barrier()
        ar.release()
        NSUB = 2
        XS = [ar.alloc([128, D]) for _ in range(NSUB)]
        xn = ar.alloc([128, D], BF16)
        junk = xn
        hTm = [ar.alloc([128, 8, NSUB * 128], BF16) for _ in range(2)]
        h2T = ar.alloc([128, 22, NSUB * 128], BF16)
        sg = [ar.alloc([128, NSUB * 128]) for _ in range(2)]
        ss = ar.alloc([128, 8])
        rs = ar.alloc([128, 8])
        groups = [[0]]
        t = 1
        while t <= NT:
            groups.append(list(range(t, min(t + NSUB, NT + 1))))
            t += NSUB
        if do_sample:
            groups.append([SMP])
        if L == 1:
            groups = groups[1:]
        for grp in groups:
            n = len(grp) * 128
            hb = rot('hTm')
            for si, t in enumerate(grp):
                S.dma('sp', XS[si], xa[t * 128:(t + 1) * 128, :], reads=[('xa%d' % L, t)], writes=['XS%d' % si])
                norm_to_T(XS[si], 'XS%d' % si, gB[:, 0, :], hTm[hb][:, :, si * 128:(si + 1) * 128], 'hTm%d' % hb,
                          junk, xn, ss[:, 0:1], rs[:, 0:1])
            if L == 0 and grp[0] == 1:
                dbg_out('B_xs0', XS[0], ['XS0'])
                dbg_out('B_hT', hTm[hb], ['hTm%d' % hb], BF16)
            for fc in range(22):
                pg = rot('ffn_ps', 2)
                for which in range(2):
                    bank = psB[pg * 2 + which]
                    c0 = which * DFF + fc * 128
                    for kc in range(8):
                        S.op('pe', lambda e, bank=bank, c0=c0, kc=kc: e.matmul(bank[:, 0:n], lhsT=wfi[:, kc, c0:c0 + 128],
                                                                               rhs=hTm[hb][:, kc, 0:n],
                                                                               start=(kc == 0), stop=(kc == 7)),
                             reads=['wfi', 'hTm%d' % hb], writes=['psB%d' % (pg * 2 + which)])
                S.op('act', lambda e, pg=pg: e.activation(out=sg[pg][:, 0:n], in_=psB[pg * 2][:, 0:n], func=AF.Silu),
                     reads=['psB%d' % (pg * 2)], writes=['sg%d' % pg])
                S.op('dve', lambda e, pg=pg, fc=fc: e.tensor_tensor(out=h2T[:, fc, 0:n], in0=psB[pg * 2 + 1][:, 0:n],
                                                                     in1=sg[pg][:, 0:n], op=ALU.mult),
                     reads=['psB%d' % (pg * 2 + 1), 'sg%d' % pg], writes=['h2T'])
            for si, t in enumerate(grp):
                for half in range(2):
                    for fc in range(22):
                        S.op('pe', lambda e, half=half, fc=fc, si=si: e.matmul(
                            psA[:, half * 512:(half + 1) * 512], lhsT=h2T[:, fc, si * 128:(si + 1) * 128],
                            rhs=wfo[:, fc, half * 512:(half + 1) * 512], start=(fc == 0), stop=(fc == 21)),
                            reads=['h2T', 'wfo'], writes=['psA%d' % half])
                rms_rstd(psA[:, :], ['psA0', 'psA1'], D, junk, ss[:, 1:2], rs[:, 1:2], 'po')
                S.op('dve', lambda e: e.tensor_tensor(out=psA[:, :], in0=psA[:, :], in1=gB[:, 1, :], op=ALU.mult),
                     reads=['psA0', 'psA1', 'gB'], writes=['psA0', 'psA1'])
                S.op('dve', lambda e, si=si: e.scalar_tensor_tensor(out=XS[si], in0=psA[:, :], scalar=rs[:, 1:2], in1=XS[si],
                                                                    op0=ALU.mult, op1=ALU.add),
                     reads=['psA0', 'psA1', 'pors', 'XS%d' % si], writes=['XS%d' % si])
                if L == 0:
                    S.dma('sp', xb[t * 128:(t + 1) * 128, :], XS[si], reads=['XS%d' % si], writes=[('x1', t)])
                else:
                    S.dma('sp', y_o[(t - 1) * 128:t * 128, :], XS[si], reads=['XS%d' % si], writes=[('y', t)])
        S.barrier()
        ar.release()
    S.finish()
    return nc

def _bf16():
    import ml_dtypes
    return ml_dtypes.bfloat16


def _consts(NT):
    NTILE = NT + 2
    cf = np.zeros((128, 2560), np.float32)
    p = np.arange(128)[:, None]
    f = np.arange(128)[None, :]
    cf[:, 0:128] = np.eye(128)
    cf[:, 128:256] = (f >= p)
    cf[:, 256:384] = (f < p)
    Rm = np.zeros((128, 128), np.float32)
    for hb in (0, 64):
        for j in range(32):
            Rm[hb + 32 + j, hb + j] = -1.0
            Rm[hb + j, hb + 32 + j] = 1.0
    cf[:, 384:512] = Rm
    f2 = np.arange(256)[None, :]
    band = (f2 > p) & (f2 <= p + 128)
    cf[:, 512:768] = band
    cf[:, 768:1024] = band & (f2 >= 240)
    cf[:, 2304:2560] = band & (f2 >= 112)
    cf[:, 1024] = (np.arange(128) >= 112)
    cf[:, 1025] = 1e-6
    cf[:, 1026] = 1.0
    sb_ = (p // 8) == (f // 8)
    cf[:, 1152:1280] = (f >= p) & sb_
    cf[:, 1280:1408] = (f < p) & sb_
    cf[:, 1408:1536] = (f == (p // 8) * 8 + 7)
    m = np.zeros((128, 256), np.float32)
    m[:, 0:128] = (f > (p % 8))
    m[:, 128:256] = (f <= p) & sb_
    cf[:, 1536:1792] = m
    cf[:, 1792:1920] = sb_
    for s in range(16):
        cf[:, 1920 + s] = (np.arange(128) // 8 == s)
    cf[:, 2048:2176] = (p // 64) == (f // 64)
    cf[:, 2176:2304] = (p >= 64) & (f < 64)
    bf = _bf16()
    cb = np.zeros((128, 256), np.float32)
    cb[:, 0:128] = np.eye(128)
    cb[:, 128:256] = 1.0
    cb = cb.astype(bf)
    inv = (10000.0 ** (-np.arange(32, dtype=np.float32) / np.float32(32))).astype(np.float32)
    pos = np.zeros(NTILE * 128, np.float32)
    for t in range(NT + 1):
        pos[t * 128:(t + 1) * 128] = t * 128 + np.arange(128) - 112
    pos[(NT + 1) * 128:] = PAST_LEN + (np.arange(128) % 8)
    ang = (pos[None, :] * inv[:, None]).astype(np.float32)
    ang = np.tile(ang, (4, 1)).astype(np.float64)
    rope = np.stack([np.cos(ang), np.sin(ang)]).astype(np.float32)
    return cf, cb, rope


def _prep_shared(inp, NT):
    f32 = np.float32
    idx = list(range(0, 2048)) + list(range(2056, 3080))
    idx += [3084 + i for i in list(range(0, 64)) + list(range(128, 192)) + list(range(64, 128)) + list(range(192, 256))]
    idx += list(range(3340, 3596)) + list(range(2048, 2056)) + list(range(3080, 3084))
    w_in = np.zeros((2, D, NCI), f32)
    w_in[:, :, :3596] = inp["w_in"][:, :, idx]
    ridx = list(range(0, 768)) + [768 + i for i in list(range(0, 64)) + list(range(128, 192)) + list(range(64, 128)) + list(range(192, 256))]
    w_out = np.ascontiguousarray(inp["w_out"][:, ridx, :])
    gb = np.stack([inp["g_pre_mix"], inp["g_post_mix"], inp["g_pre_ffn"], inp["g_post_ffn"]], axis=1)
    gb = np.ascontiguousarray(np.broadcast_to(gb[:, :, None, :], (2, 4, 128, D))).astype(f32)
    pp = np.zeros((2, 128, 80), f32)
    for l in range(2):
        for ci in range(12):
            pp[l, :, ci * 4:(ci + 1) * 4] = inp["dn_conv_w"][l][:, ci * 128:(ci + 1) * 128].T
        for ci in range(12, 18):
            pp[l, :, ci * 4:(ci + 1) * 4] = inp["ssm_conv_w"][l][:, (ci - 12) * 128:(ci - 11) * 128].T
        for j in range(6):
            pp[l, :, 72 + j] = inp["ssm_conv_b"][l][j * 128:(j + 1) * 128]
        pp[l, :, 78] = inp["dn_norm_w"][l]
    pb = np.zeros((2, 128, 288), f32)
    for l in range(2):
        row = np.zeros(288, f32)
        row[0:4] = inp["dn_a_log"][l]
        row[4:8] = inp["ssm_a_log"][l]
        row[8:12] = inp["dn_dt_bias"][l]
        row[12:16] = inp["ssm_dt_bias"][l]
        row[16:20] = inp["ssm_d"][l]
        row[20:24] = inp["swa_sinks"][l]
        row[32:288] = inp["ssm_norm_w"][l]
        pb[l] = row[None, :]
    cf, cb, rope = _consts(NT)
    return dict(w_in=w_in, w_out=w_out, w_fi=np.ascontiguousarray(inp["w_ffn_in"], f32),
                w_fo=np.ascontiguousarray(inp["w_ffn_out"], f32), gb=gb, pp=pp, pb=pb, cf=cf, cb=cb, rope=rope)


_CACHE = {}
LAST_RES = None
STAGE = 99
DO_SAMPLE = True


def kernel(**inputs):
    inp = {k: np.asarray(v) for k, v in inputs.items()}
    NT = inp["x_prompt"].shape[1] // 128
    shared = _prep_shared(inp, NT)
    key = (NT, STAGE, DO_SAMPLE)
    if key not in _CACHE:
        _CACHE[key] = build_program(NT, do_sample=DO_SAMPLE, stage=STAGE)
    nc = _CACHE[key]
    in_maps = []
    for c in range(8):
        b = c // 4
        xin = np.zeros(((NT + 2) * 128, D), np.float32)
        xin[112:128] = inp["meta_tokens"]
        xin[128:(NT + 1) * 128] = inp["x_prompt"][b]
        xin[(NT + 1) * 128:] = inp["x_sample"][16 * c:16 * c + 16].reshape(128, D)
        m = dict(shared)
        m["xin"] = xin
        if DO_SAMPLE:
            sl = slice(16 * c, 16 * c + 16)
            m["sdn"] = np.ascontiguousarray(inp["state_dn"][:, sl])
            m["sdnc"] = np.ascontiguousarray(inp["state_dn_conv"][:, sl])
            m["sssm"] = np.ascontiguousarray(inp["state_ssm"][:, sl])
            m["sssmc"] = np.ascontiguousarray(inp["state_ssm_conv"][:, sl])
            m["ck"] = np.ascontiguousarray(inp["cache_swa_k"][:, sl].reshape(2, 16, 128, 128))
            m["cv"] = np.ascontiguousarray(inp["cache_swa_v"][:, sl].reshape(2, 16, 128, 128))
        in_maps.append(m)
    res = run_bass_kernel_spmd(nc, in_maps, core_ids=list(range(8))).results
    global LAST_RES
    LAST_RES = res
    f32 = np.float32
    y_p = np.stack([res[4 * b]["y"][:NT * 128] for b in range(2)]).astype(f32)
    y_s = np.concatenate([res[c]["y"][NT * 128:].reshape(16, 8, D) for c in range(8)]).astype(f32)

    def pst(name, shape):
        return np.stack([res[4 * b][name] for b in range(2)], axis=1).reshape(shape).astype(f32)

    dn_p = pst("dn_p", (2, 2, 4, 128, 128))
    dnc_p = pst("dnc_p", (2, 2, 3, 1536))
    ssm_p = pst("ssm_p", (2, 2, 4, 64, 128))
    ssmc_p = pst("ssmc_p", (2, 2, 3, 768))
    k_p = pst("k_p", (2, 2, 128, 2, 64))
    v_p = pst("v_p", (2, 2, 128, 2, 64))
    if DO_SAMPLE:
        def sst(name, shape):
            return np.concatenate([res[c][name] for c in range(8)], axis=1).reshape(shape).astype(f32)
        dn_s = sst("dn_s", (2, 128, 4, 128, 128))
        dnc_s = sst("dnc_s", (2, 128, 3, 1536))
        ssm_s = sst("ssm_s", (2, 128, 4, 64, 128))
        ssmc_s = sst("ssmc_s", (2, 128, 3, 768))
        k_s = sst("k_s", (2, 128, 128, 2, 64))
        v_s = sst("v_s", (2, 128, 128, 2, 64))
    else:
        z = lambda *s: np.zeros(s, f32)
        dn_s, dnc_s, ssm_s, ssmc_s = z(2, 128, 4, 128, 128), z(2, 128, 3, 1536), z(2, 128, 4, 64, 128), z(2, 128, 3, 768)
        k_s, v_s = z(2, 128, 128, 2, 64), z(2, 128, 128, 2, 64)
    return (y_p, y_s, dn_p, dnc_p, ssm_p, ssmc_p, k_p, v_p, dn_s, dnc_s, ssm_s, ssmc_s, k_s, v_s)
```

```python
from contextlib import ExitStack
import types
import numpy as np
import concourse.bass as bass
import concourse.mybir as mybir
from concourse.bass_utils import run_bass_kernel_spmd

F32 = mybir.dt.float32
BF16 = mybir.dt.bfloat16
AF = mybir.ActivationFunctionType
ALU = mybir.AluOpType
AX = mybir.AxisListType


def _freeze(fn):
    if fn.__closure__ is None:
        return fn
    cells = []
    for c in fn.__closure__:
        try:
            cells.append(types.CellType(c.cell_contents))
        except ValueError:
            cells.append(c)
    return types.FunctionType(fn.__code__, fn.__globals__, fn.__name__, fn.__defaults__, tuple(cells))


class Sched:
    BLK = {'pe': 'tensor', 'act': 'scalar', 'dve': 'vector', 'pool': 'gpsimd', 'sp': 'sync'}
    NPOOL = 16

    def __init__(self, nc):
        self.nc = nc
        self.ops = {e: [] for e in self.BLK}
        self.cnt = {e: 0 for e in self.BLK}
        self.seen = {e: {} for e in self.BLK}
        self.lastw = {}
        self.readers = {}
        self.dcount = {e: 0 for e in self.BLK}
        self.dtot = {}
        self.nins = 0

    def _deps(self, reads, writes, eng=None):
        deps = []
        for k in reads:
            t = self.lastw.get(k)
            if t is not None:
                deps.append(t)
            if isinstance(k, str) and k.startswith('ps'):
                r = self.readers.get(k)
                if r:
                    deps.extend((sid, v) for sid, v in r.items() if sid != 'c:' + str(eng))
        for k in writes:
            t = self.lastw.get(k)
            if t is not None:
                deps.append(t)
            r = self.readers.get(k)
            if r:
                deps.extend(r.items())
        return deps

    def _waits(self, eng, deps, compute=False):
        need = {}
        seen = self.seen[eng]
        for sid, v in deps:
            if sid == 'c:pe' and eng == 'pe':
                continue
            if compute and sid == 'c:pool' and eng == 'pool':
                continue
            if seen.get(sid, 0) >= v:
                continue
            if need.get(sid, 0) < v:
                need[sid] = v
        for sid, v in need.items():
            seen[sid] = v
        return list(need.items())

    def _commit(self, tok, reads, writes):
        for k in reads:
            r = self.readers.setdefault(k, {})
            if r.get(tok[0], 0) < tok[1]:
                r[tok[0]] = tok[1]
        for k in writes:
            self.lastw[k] = tok
            self.readers[k] = {}

    def op(self, eng, fn, reads=(), writes=()):
        waits = self._waits(eng, self._deps(reads, writes, eng), compute=True)
        self.cnt[eng] += 1
        tok = ('c:' + eng, self.cnt[eng])
        self.ops[eng].append((waits, _freeze(fn), (tok[0], 1)))
        self._commit(tok, reads, writes)
        self.nins += 1

    def dma(self, q, out, in_, reads=(), writes=(), **kw):
        deps = self._deps(reads, writes)
        j = self.dcount[q]
        self.dcount[q] += 1
        sid = 'd:%s:%d' % (q, j % self.NPOOL)
        prev = self.dtot.get(sid, 0)
        if prev:
            deps.append((sid, prev))
        waits = self._waits(q, deps)
        self.dtot[sid] = prev + 16
        tok = (sid, prev + 16)
        self.ops[q].append((waits, lambda e: e.dma_start(out=out, in_=in_, **kw), (sid, 16)))
        self._commit(tok, reads, writes)
        self.nins += 1

    def barrier(self):
        alls = [('c:' + e, c) for e, c in self.cnt.items() if c] + list(self.dtot.items())
        for e in self.BLK:
            w = self._waits(e, alls)
            if w:
                self.ops[e].append((w, None, None))

    def finish(self):
        w = self._waits('sp', list(self.dtot.items()) + [('c:' + e, c) for e, c in self.cnt.items() if c])
        if w:
            self.ops['sp'].append((w, None, None))
        nc = self.nc
        sids = ['c:' + e for e in self.BLK] + sorted(self.dtot)
        with ExitStack() as st:
            semh = {}
            for sid in sids:
                semh[sid] = st.enter_context(nc.semaphore(sid.replace(':', '_')))
            block = st.enter_context(nc.Block())
            for ename, bname in self.BLK.items():
                ops = self.ops[ename]

                def body(e, ops=ops):
                    for waits, fn, inc in ops:
                        for sid, v in waits:
                            e.wait_ge(semh[sid], v)
                        if fn is not None:
                            ins = fn(e)
                            ins.then_inc(semh[inc[0]], inc[1])
                getattr(block, bname)(body)


SUB = 99
INTERLEAVE = True

class Mixers:
    NAMES = ('win', 'pp', 'pb', 'cf', 'psA', 'psT', 'psB', 'mixT', 'negA8', 'negsink', 'identF', 'identB', 'onesB',
             'mUiP', 'mLsP', 'Rm', 'swaMP', 'swaMP0', 'swaMP1', 'validc', 'roped', 'rot', 'NT', 'SMP', 'epsc',
             'dn_p', 'dnc_p', 'ssm_p', 'ssmc_p', 'k_p', 'v_p', 'stage', 'do_sample', 'smp_io', 'wout_off')

    def __init__(self, nc, S, ar, c):
        self.nc, self.S, self.ar = nc, S, ar
        for k in self.NAMES:
            setattr(self, k, c[k])
        a = ar.alloc
        self.onec = self.cf[:, 1026:1027]
        self.rawbuf = a([128, 18, 176])
        self.rawP = self.rawbuf[:, :, 0:131]
        self.rawS = self.rawbuf.rearrange("p c (s u) -> p c s u", u=11)
        self.cv = [a([128, 4, 128]) for _ in range(2)]
        self.qk32 = a([128, 8, 128])
        self.sqb = a([128, 8, 128], BF16)
        self.qnT = a([128, 4, 128], BF16)
        self.knT = a([128, 4, 128], BF16)
        self.QdT = a([128, 4, 128], BF16)
        self.vT = a([128, 4, 128], BF16)
        self.szT = a([128, 4, 128], BF16)
        self.xT32 = a([128, 2, 128])
        self.BT = a([128, 2, 128], BF16)
        self.CT = a([128, 2, 128], BF16)
        self.sw32 = a([128, 4, 128])
        self.sz = a([128, 256], BF16)
        self.sm = a([128, 80])
        self.gcT = [a([1, 128]) for _ in range(2)]
        self.Ui = a([128, 8, 128], BF16)
        self.E = a([128, 8, 128], BF16)
        self.Ls = a([128, 4, 128])
        self.Dt = [a([128, 128]) for _ in range(2)]
        self.N32 = a([128, 4, 128])
        self.Noffb = a([128, 4, 128], BF16)
        self.bd64 = self.cf[:, 2048:2176]
        self.offm = self.cf[:, 2176:2304]
        self.Pk = [a([128, 4, 128], BF16) for _ in range(2)]
        self.Ptk = [a([128, 4, 128], BF16) for _ in range(2)]
        self.Y = a([128, 4, 128])
        self.Yb = [a([128, 4, 128], BF16) for _ in range(2)]
        self.attnT = a([128, 4, 128], BF16)
        self.Kbg = a([128, 4, 128], BF16)
        self.Kd = a([128, 4, 128], BF16)
        self.Vb = a([128, 4, 128], BF16)
        self.negWT = a([128, 4, 128], BF16)
        self.vnew = a([128, 4, 128], BF16)
        self.S32 = a([128, 4, 128])
        self.Sb = a([128, 4, 128], BF16)
        self.sqo = self.sqb[:, 0:4, :]
        self.o1 = self.N32
        self.xtok = a([128, 256])
        self.xdt = a([128, 4, 64], BF16)
        self.Bd = a([128, 4, 128], BF16)
        self.sattnT = a([128, 4, 128], BF16)
        self.CdT = a([128, 4, 128], BF16)
        self.h32 = a([128, 256])
        self.hb = a([128, 256], BF16)
        self.y2 = a([128, 256])
        self.y4 = a([128, 256], BF16)
        self.hout = a([128, 2, 128])
        self.ropet = [a([128, 2, 128]) for _ in range(2)]
        self.r1 = a([128, 3, 128])
        self.r2 = a([128, 3, 128])
        self.qz = a([128, 4, 128], BF16)
        self.kring = a([128, 2, 128], BF16)
        self.vring = a([128, 2, 2, 128], BF16)
        self.v32 = a([128, 128])
        self.e32 = [a([128, 256])]
        self.em = a([128, 256])
        self.pn = a([128, 256], BF16)
        self.pnT = [a([128, 2, 128], BF16) for _ in range(2)]
        if self.do_sample:
            (self.sdn, self.sdnc, self.sssm, self.sssmc, self.ck, self.cvd,
             self.dn_s, self.dnc_s, self.ssm_s, self.ssmc_s, self.k_s, self.v_s) = self.smp_io
            cf = self.cf
            self.mUiS, self.mLsS, self.lastm = cf[:, 1152:1280], cf[:, 1280:1408], cf[:, 1408:1536]
            self.swaMS, self.rowm = cf[:, 1536:1792], cf[:, 1920:1936]
            self.cstg = a([128, 768])
            self.ZW = a([128, 16, 248], BF16)
            self.kdv = a([128, 8])
            save_top = ar.top
            ar.top = self.wout_off
            self.ctmp = a([128, 4, 48])
            self.glS = a([128, 8, 16])
            self.KdM = [a([128, 128], BF16) for _ in range(2)]
            base = ar.top
            self.Sall = a([128, 16, 128])
            self.Sball = a([128, 16, 128], BF16)
            top1 = ar.top
            ar.top = base
            self.hin = a([128, 16, 128])
            self.Hh = a([128, 16, 64])
            self.Hhb = a([128, 16, 64], BF16)
            top2 = ar.top
            ar.top = base
            self.KcT = a([128, 16, 128], BF16)
            self.VA = a([128, 16, 2, 128], BF16)
            self.kst = a([128, 2, 128])
            self.vst = a([128, 2, 128])
            assert max(top1, top2, ar.top) <= self.wout_off + 8 * 1024, "sample overlay exceeds w_out region"
            ar.top = save_top
            S.op('pool', lambda e: e.memset(self.ZW, 0.0), writes=['ZW'])
        z = S.op
        z('pool', lambda e: e.memset(self.rawP[:, :, 0:3], 0.0), writes=['raw%d' % g for g in range(5)])
        z('pool', lambda e: e.memset(self.S32, 0.0), writes=['S32'])
        z('pool', lambda e: e.memset(self.Sb, 0.0), writes=['Sb'])
        z('pool', lambda e: e.memset(self.h32, 0.0), writes=['h32'])
        z('pool', lambda e: e.memset(self.hb, 0.0), writes=['hb'])
        z('pool', lambda e: e.memset(self.kring, 0.0), writes=['kring0', 'kring1'])
        z('pool', lambda e: e.memset(self.qz, 0.0), writes=['qz'])
        z('pool', lambda e: e.memset(self.vring, 0.0), writes=['vring0', 'vring1'])

    def proj_fm(self, chunks, half, hT, hkey):
        S, psA, win = self.S, self.psA, self.win
        for j, c in enumerate(chunks):
            for kc in range(8):
                S.op('pe', lambda e, j=j, c=c, kc=kc: e.matmul(psA[:, half * 512 + j * 128:half * 512 + (j + 1) * 128],
                                                               lhsT=win[:, kc, c * 128:(c + 1) * 128], rhs=hT[:, kc, :],
                                                               start=(kc == 0), stop=(kc == 7)),
                     reads=['win', hkey], writes=['psA%d' % half])

    def tile(self, L, t, hT, hkey):
        S = self.S
        psA, psT, psB = self.psA, self.psT, self.psB
        pp, pb, sm = self.pp, self.pb, self.sm
        rot = self.rot
        last = (t == self.NT)
        kind = 'S' if t == self.SMP else 'P'
        self.kind = kind
        mUi, mLs = (self.mUiS, self.mLsS) if kind == 'S' else (self.mUiP, self.mLsP)
        if kind == 'S':
            self.conv_hist_load(L)
        beta, negbeta, spin, sp, g8, gc8, egc8, begc, glast = (sm[:, 0:4], sm[:, 4:8], sm[:, 8:16], sm[:, 16:24],
                                                               sm[:, 24:32], sm[:, 32:40], sm[:, 40:48], sm[:, 48:52],
                                                               sm[:, 52:60])
        sws = sm[:, 60:64]

        groups = [([0, 1, 2, 3], 0), ([4, 5, 6, 7], 4), ([8, 9, 10, 11], 8), ([16, 17, 18, 19], 12), ([20, 21], 16)]
        for gi, (chunks, ci0) in enumerate(groups):
            n = len(chunks)
            half = rot('psAh')
            self.proj_fm(chunks, half, hT, hkey)
            rk = 'raw%d' % gi
            if kind == 'P':
                S.op('act', lambda e, half=half, ci0=ci0, n=n: e.copy(
                    out=self.rawP[:, ci0:ci0 + n, 3:131],
                    in_=psA[:, half * 512:half * 512 + n * 128].rearrange("p (a b) -> p a b", a=n)),
                    reads=['psA%d' % half], writes=[rk])
            else:
                for j in range(n):
                    S.op('act', lambda e, half=half, ci0=ci0, j=j: e.copy(
                        out=self.rawS[:, ci0 + j, :, 3:11],
                        in_=psA[:, half * 512 + j * 128:half * 512 + (j + 1) * 128].rearrange("p (s u) -> p s u", u=8)),
                        reads=['psA%d' % half], writes=[rk])
            cb_ = rot('cv')
            cv = self.cv[cb_]
            ckl = ['cv%d_%d' % (cb_, j) for j in range(4)]
            srcs, dsts = [], []
            for j in range(n):
                ci = ci0 + j
                if kind == 'P':
                    srcs.append(lambda tap, ci=ci: self.rawP[:, ci, tap:tap + 128])
                    dsts.append(cv[:, j, :])
                else:
                    srcs.append(lambda tap, ci=ci: self.rawS[:, ci, :, tap:tap + 8])
                    dsts.append(cv[:, j, :].rearrange("p (s u) -> p s u", u=8))
            for tap in range(4):
                for j in range(n):
                    ci = ci0 + j
                    src, dstv = srcs[j], dsts[j]
                    if tap == 0:
                        S.op('dve', lambda e, ci=ci, src=src, dstv=dstv: e.tensor_scalar(out=dstv, in0=src(0),
                                                                                         scalar1=pp[:, ci * 4:ci * 4 + 1], scalar2=None,
                                                                                         op0=ALU.mult),
                             reads=[rk, 'pp'], writes=[ckl[j]])
                    else:
                        S.op('dve', lambda e, ci=ci, tap=tap, src=src, dstv=dstv: e.scalar_tensor_tensor(
                            out=dstv, in0=src(tap), scalar=pp[:, ci * 4 + tap:ci * 4 + tap + 1],
                            in1=dstv, op0=ALU.mult, op1=ALU.add), reads=[rk, 'pp', ckl[j]], writes=[ckl[j]])
            if kind == 'P':
                S.op('pool', lambda e, ci0=ci0, n=n: e.tensor_copy(out=self.rawP[:, ci0:ci0 + n, 0:3],
                                                                   in_=self.rawP[:, ci0:ci0 + n, 128:131]),
                     reads=[rk] + ckl, writes=[rk])
            if gi == 0:
                S.op('act', lambda e, cv=cv: e.activation(out=self.qk32[:, 0:4, :], in_=cv, func=AF.Silu), reads=ckl, writes=['q32'])
            elif gi == 1:
                S.op('act', lambda e, cv=cv: e.activation(out=self.qk32[:, 4:8, :], in_=cv, func=AF.Silu), reads=ckl, writes=['k32'])
            elif gi == 2:
                S.op('act', lambda e, cv=cv: e.activation(out=self.vT, in_=cv, func=AF.Silu), reads=ckl, writes=['vT'])
            else:
                dests = ([self.xT32[:, 0, :], self.xT32[:, 1, :], self.BT[:, 0, :], self.BT[:, 1, :]] if gi == 3
                         else [self.CT[:, 0, :], self.CT[:, 1, :]])
                for j in range(n):
                    ci = ci0 + j
                    S.op('act', lambda e, j=j, ci=ci, cv=cv, dests=dests: e.activation(
                        out=dests[j], in_=cv[:, j, :], func=AF.Silu, bias=pp[:, 72 + ci - 12:73 + ci - 12]),
                        reads=[ckl[j], 'pp'], writes=['ssmxbc%d' % gi])
        half = rot('psAh')
        self.proj_fm([12, 13, 14, 15], half, hT, hkey)
        S.op('act', lambda e, half=half: e.activation(out=self.szT, in_=psA[:, half * 512:(half + 1) * 512].rearrange(
            "p (a b) -> p a b", a=4), func=AF.Silu), reads=['psA%d' % half], writes=['szT'])
        half = rot('psAh')
        self.proj_fm([24, 25, 26, 27], half, hT, hkey)
        S.op('act', lambda e, half=half: e.copy(out=self.sw32, in_=psA[:, half * 512:(half + 1) * 512].rearrange(
            "p (a b) -> p a b", a=4)), reads=['psA%d' % half], writes=['sw32'])
        for kc in range(8):
            S.op('pe', lambda e, kc=kc: e.matmul(psB[2][:, 0:16], lhsT=hT[:, kc, :], rhs=self.win[:, kc, 3584:3600],
                                                 start=(kc == 0), stop=(kc == 7)), reads=['win', hkey], writes=['psB2'])
        for kc in range(8):
            S.op('pe', lambda e, kc=kc: e.matmul(psB[3][:, 0:256], lhsT=hT[:, kc, :], rhs=self.win[:, kc, 22 * 128:24 * 128],
                                                 start=(kc == 0), stop=(kc == 7)), reads=['win', hkey], writes=['psB3'])
        S.op('act', lambda e: e.activation(out=self.sz, in_=psB[3][:, 0:256], func=AF.Silu), reads=['psB3'], writes=['sz'])

        S.op('act', lambda e: e.activation(out=beta, in_=psB[2][:, 0:4], func=AF.Sigmoid), reads=['psB2'], writes=['beta'])
        S.op('dve', lambda e: e.tensor_tensor(out=spin, in0=psB[2][:, 4:12], in1=pb[:, 8:16], op=ALU.add),
             reads=['psB2', 'pb'], writes=['spin'])
        S.op('act', lambda e: e.activation(out=spin, in_=spin, func=AF.Exp), reads=['spin'], writes=['spin'])
        S.op('act', lambda e: e.activation(out=sp, in_=spin, func=AF.Ln, bias=self.onec), reads=['spin'], writes=['sp'])
        if t == 0:
            S.op('dve', lambda e: e.tensor_scalar(out=sp, in0=sp, scalar1=self.validc, scalar2=None, op0=ALU.mult),
                 reads=['sp', 'cf'], writes=['sp'])
        S.op('dve', lambda e: e.tensor_tensor(out=g8, in0=sp, in1=self.negA8, op=ALU.mult), reads=['sp', 'negA8'], writes=['g8'])
        S.op('dve', lambda e: e.tensor_scalar(out=negbeta, in0=beta, scalar1=-1.0, scalar2=None, op0=ALU.mult),
             reads=['beta'], writes=['negbeta'])
        S.op('pe', lambda e: e.matmul(psB[2][:, 16:24], lhsT=mUi, rhs=g8, start=True, stop=True),
             reads=['g8', 'cf'], writes=['psB2'])
        S.op('dve', lambda e: e.tensor_copy(out=gc8, in_=psB[2][:, 16:24]), reads=['psB2'], writes=['gc8'])
        S.op('act', lambda e: e.activation(out=egc8, in_=gc8, func=AF.Exp), reads=['gc8'], writes=['egc8'])
        S.op('dve', lambda e: e.tensor_tensor(out=begc, in0=beta, in1=egc8[:, 0:4], op=ALU.mult),
             reads=['beta', 'egc8'], writes=['begc'])
        for hh in range(8):
            bi = 3 + hh % 2
            bank = psB[bi]
            bk = 'psB%d' % bi
            gb_ = rot('gcT')
            S.op('pe', lambda e, hh=hh: e.transpose(out=psB[2][0:1, 32:160], in_=gc8[:, hh:hh + 1], identity=self.identF),
                 reads=['gc8', 'cf'], writes=['psB2'])
            S.op('act', lambda e, gb_=gb_: e.copy(out=self.gcT[gb_], in_=psB[2][0:1, 32:160]), reads=['psB2'], writes=['gcT%d' % gb_])
            S.op('pe', lambda e, hh=hh, bank=bank, gb_=gb_: e.matmul(bank[:, 0:128], lhsT=self.mUiP[0:1, :],
                                                                     rhs=self.gcT[gb_], start=True, stop=True),
                 reads=['gcT%d' % gb_, 'cf'], writes=[bk])
            db = rot('Dt')
            S.op('dve', lambda e, hh=hh, bank=bank, db=db: e.tensor_scalar(out=self.Dt[db], in0=bank[:, 0:128],
                                                                           scalar1=gc8[:, hh:hh + 1], scalar2=0.0,
                                                                           op0=ALU.subtract, op1=ALU.min),
                 reads=[bk, 'gc8'], writes=['Dt%d' % db])
            S.op('act', lambda e, hh=hh, db=db: e.activation(out=self.Ui[:, hh, :], in_=self.Dt[db], func=AF.Exp),
                 reads=['Dt%d' % db], writes=['Ui%d' % hh])
            S.op('pool', lambda e, hh=hh: e.tensor_tensor(out=self.Ui[:, hh, :], in0=self.Ui[:, hh, :], in1=mUi, op=ALU.mult),
                 reads=['Ui%d' % hh, 'cf'], writes=['Ui%d' % hh])
            S.op('act', lambda e, hh=hh, bank=bank: e.activation(out=self.E[:, hh, :], in_=bank[:, 0:128], func=AF.Exp),
                 reads=[bk], writes=['E%d' % hh])
            if kind == 'P':
                S.op('act', lambda e, hh=hh, bank=bank: e.activation(out=glast[:, hh:hh + 1], in_=bank[:, 127:128], func=AF.Exp),
                     reads=[bk], writes=['glast'])
            else:
                S.op('act', lambda e, hh=hh, bank=bank: e.activation(
                    out=self.glS[:, hh, :], in_=bank[:, 0:128].rearrange("p (s u) -> p s u", u=8)[:, :, 7], func=AF.Exp),
                    reads=[bk], writes=['glS'])
                db = rot('Dt')
                S.op('dve', lambda e, hh=hh, db=db: e.scalar_tensor_tensor(
                    out=self.Dt[db], in0=self.Ui[:, hh, :], scalar=1.0, in1=self.lastm, op0=ALU.mult, op1=ALU.mult,
                    accum_out=self.kdv[:, hh:hh + 1]), reads=['Ui%d' % hh, 'cf'], writes=['Dt%d' % db, 'kdv'])
            if hh < 4:
                db = rot('Dt')
                S.op('dve', lambda e, hh=hh, bank=bank, db=db: e.tensor_scalar(out=self.Dt[db], in0=bank[:, 0:128],
                                                                               scalar1=gc8[:, hh:hh + 1], scalar2=0.0,
                                                                               op0=ALU.subtract, op1=ALU.max),
                     reads=[bk, 'gc8'], writes=['Dt%d' % db])
                S.op('act', lambda e, hh=hh, db=db: e.activation(out=self.Ls[:, hh, :], in_=self.Dt[db], func=AF.Exp, scale=-1.0),
                     reads=['Dt%d' % db], writes=['Ls%d' % hh])
                S.op('pool', lambda e, hh=hh: e.tensor_tensor(out=self.Ls[:, hh, :], in0=self.Ls[:, hh, :], in1=mLs, op=ALU.mult),
                     reads=['Ls%d' % hh, 'cf'], writes=['Ls%d' % hh])

        self.kdS = self.kdv[:, 4:8] if kind == 'S' else self.Ui[:, 4:8, 127]
        if kind == 'S':
            self.conv_tail_out_S(L)
        gens = [self.deltanet(L, t, last), self.ssd(L, t, last), self.swa(L, t, last)]
        if kind == 'P' and INTERLEAVE:
            while gens:
                for g in list(gens):
                    try:
                        next(g)
                    except StopIteration:
                        gens.remove(g)
        else:
            for g in gens:
                for _ in g:
                    pass

    def deltanet(self, L, t, last):
        S = self.S
        psT, psB, sm = self.psT, self.psB, self.sm
        rot = self.rot
        beta, negbeta, begc, glast = sm[:, 0:4], sm[:, 4:8], sm[:, 48:52], sm[:, 52:60]
        v4 = lambda ap: ap.rearrange("p (a b) -> p a b", a=4)
        isP = (self.kind == 'P')
        S.op('pool', lambda e: e.tensor_tensor(out=self.sqb, in0=self.qk32, in1=self.qk32, op=ALU.mult),
             reads=['q32', 'k32'], writes=['sqb'])
        for j in range(8):
            S.op('pe', lambda e, j=j: e.matmul(psB[j // 4][:, (j % 4) * 128:(j % 4 + 1) * 128], lhsT=self.onesB,
                                               rhs=self.sqb[:, j, :], start=True, stop=True),
                 reads=['sqb', 'cb'], writes=['psB%d' % (j // 4)])
        for hf in range(2):
            S.op('act', lambda e, hf=hf: e.activation(out=self.cv[hf], in_=v4(psB[hf][:, :]),
                                                      func=AF.Sqrt, bias=self.epsc), reads=['psB%d' % hf], writes=['cv%d_%d' % (hf, j) for j in range(4)])
            S.op('dve', lambda e, hf=hf: e.reciprocal(out=self.cv[hf], in_=self.cv[hf]),
                 reads=['cv%d_%d' % (hf, j) for j in range(4)], writes=['cv%d_%d' % (hf, j) for j in range(4)])
        S.op('dve', lambda e: e.scalar_tensor_tensor(out=self.qnT, in0=self.qk32[:, 0:4, :], scalar=128.0 ** -0.5,
                                                     in1=self.cv[0], op0=ALU.mult, op1=ALU.mult),
             reads=['q32'] + ['cv0_%d' % j for j in range(4)], writes=['qnT'])
        S.op('dve', lambda e: e.tensor_tensor(out=self.knT, in0=self.qk32[:, 4:8, :], in1=self.cv[1], op=ALU.mult),
             reads=['k32'] + ['cv1_%d' % j for j in range(4)], writes=['knT'])
        S.op('pool', lambda e: e.tensor_tensor(out=self.QdT, in0=self.qnT, in1=self.E[:, 0:4, :], op=ALU.mult),
             reads=['qnT'] + ['E%d' % h for h in range(4)], writes=['QdT'])
        yield
        if SUB == 1:
            return self._dn_bail()
        for h in range(4):
            bi = h % 2
            S.op('pe', lambda e, h=h, bi=bi: e.matmul(psB[bi][:, 0:128], lhsT=self.knT[:, h, :], rhs=self.knT[:, h, :],
                                                      start=True, stop=True), reads=['knT'], writes=['psB%d' % bi])
            S.op('pe', lambda e, h=h, bi=bi: e.matmul(psB[bi][:, 128:256], lhsT=self.knT[:, h, :], rhs=self.qnT[:, h, :],
                                                      start=True, stop=True), reads=['knT', 'qnT'], writes=['psB%d' % bi])
            S.op('dve', lambda e, h=h, bi=bi: e.scalar_tensor_tensor(out=self.N32[:, h, :], in0=psB[bi][:, 0:128],
                                                                     scalar=negbeta[:, h:h + 1], in1=self.Ls[:, h, :],
                                                                     op0=ALU.mult, op1=ALU.mult),
                 reads=['psB%d' % bi, 'negbeta', 'Ls%d' % h], writes=['N32'])
            S.op('dve', lambda e, h=h, bi=bi: e.tensor_tensor(out=self.attnT[:, h, :], in0=psB[bi][:, 128:256],
                                                              in1=self.Ui[:, h, :], op=ALU.mult),
                 reads=['psB%d' % bi, 'Ui%d' % h], writes=['attnT'])
            yield
        if SUB == 2:
            return self._dn_bail()
        bd64b = self.bd64.unsqueeze(1).broadcast_to([128, 4, 128])
        offmb = self.offm.unsqueeze(1).broadcast_to([128, 4, 128])
        S.op('pool', lambda e: e.tensor_tensor(out=self.Noffb, in0=self.N32, in1=offmb, op=ALU.mult),
             reads=['N32', 'cf'], writes=['Noffb'])
        S.op('pool', lambda e: e.tensor_tensor(out=self.N32, in0=self.N32, in1=bd64b, op=ALU.mult),
             reads=['N32', 'cf'], writes=['N32'])
        pi = rot('Pk')
        S.op('pool', lambda e: e.tensor_copy(out=self.Pk[pi], in_=self.N32), reads=['N32'], writes=['Pk%d' % pi])
        for h in range(4):
            S.op('pe', lambda e, h=h: e.transpose(out=psB[2][:, h * 128:(h + 1) * 128], in_=self.N32[:, h, :], identity=self.identF),
                 reads=['N32', 'cf'], writes=['psB2'])
        ti = rot('Ptk')
        S.op('act', lambda e: e.copy(out=self.Ptk[ti], in_=v4(psB[2][:, :])), reads=['psB2'], writes=['Ptk%d' % ti])
        for h in range(4):
            S.op('dve', lambda e, h=h: e.tensor_tensor(out=self.Y[:, h, :], in0=psB[2][:, h * 128:(h + 1) * 128],
                                                       in1=self.identF, op=ALU.add), reads=['psB2', 'cf'], writes=['Y'])
        yi = rot('Yb')
        S.op('pool', lambda e: e.tensor_copy(out=self.Yb[yi], in_=self.Y), reads=['Y'], writes=['Yb%d' % yi])
        yield
        K = 5
        for k in range(1, K + 1):
            pn_, tn_, yn_ = rot('Pk'), None, rot('Yb')
            for h in range(4):
                S.op('pe', lambda e, h=h: e.matmul(psB[0][:, h * 128:(h + 1) * 128], lhsT=self.Ptk[ti][:, h, :],
                                                   rhs=self.Pk[pi][:, h, :], start=True, stop=True),
                     reads=['Ptk%d' % ti, 'Pk%d' % pi], writes=['psB0'])
            S.op('act', lambda e, pn_=pn_: e.copy(out=self.Pk[pn_], in_=v4(psB[0][:, :])), reads=['psB0'], writes=['Pk%d' % pn_])
            yield
            if k < K:
                tn_ = rot('Ptk')
                for h in range(4):
                    S.op('pe', lambda e, h=h: e.matmul(psB[1][:, h * 128:(h + 1) * 128], lhsT=self.Pk[pi][:, h, :],
                                                       rhs=self.Ptk[ti][:, h, :], start=True, stop=True),
                         reads=['Ptk%d' % ti, 'Pk%d' % pi], writes=['psB1'])
                S.op('dve', lambda e, tn_=tn_: e.tensor_copy(out=self.Ptk[tn_], in_=v4(psB[1][:, :])),
                     reads=['psB1'], writes=['Ptk%d' % tn_])
            for h in range(4):
                S.op('pe', lambda e, h=h, pn_=pn_: e.matmul(psB[2][:, h * 128:(h + 1) * 128], lhsT=self.Pk[pn_][:, h, :],
                                                            rhs=self.Yb[yi][:, h, :], start=True, stop=True),
                     reads=['Pk%d' % pn_, 'Yb%d' % yi], writes=['psB2'])
            S.op('dve', lambda e: e.tensor_tensor(out=self.Y, in0=v4(psB[2][:, :]), in1=self.Y, op=ALU.add),
                 reads=['psB2', 'Y'], writes=['Y'])
            S.op('pool', lambda e, yn_=yn_: e.tensor_copy(out=self.Yb[yn_], in_=self.Y), reads=['Y'], writes=['Yb%d' % yn_])
            yield
            pi, yi = pn_, yn_
            if tn_ is not None:
                ti = tn_
        tb = self.Pk[1 - pi]
        m1 = self.Ptk[ti]
        for h in range(4):
            S.op('pe', lambda e, h=h: e.transpose(out=psT[:, h * 128:(h + 1) * 128], in_=self.Yb[yi][:, h, :], identity=self.identB),
                 reads=['Yb%d' % yi, 'cb'], writes=['psT'])
        S.op('act', lambda e: e.copy(out=tb, in_=v4(psT[:, 0:512])), reads=['psT'], writes=['Pk%d' % (1 - pi)])
        yield
        for h in range(4):
            S.op('pe', lambda e, h=h: e.matmul(psB[0][:, h * 128:(h + 1) * 128], lhsT=self.Noffb[:, h, :], rhs=self.Yb[yi][:, h, :],
                                               start=True, stop=True), reads=['Noffb', 'Yb%d' % yi], writes=['psB0'])
        S.op('dve', lambda e: e.tensor_copy(out=m1, in_=v4(psB[0][:, :])), reads=['psB0'], writes=['Ptk%d' % ti])
        yield
        for h in range(4):
            S.op('pe', lambda e, h=h: e.matmul(psB[1][:, h * 128:(h + 1) * 128], lhsT=tb[:, h, :], rhs=m1[:, h, :],
                                               start=True, stop=True), reads=['Pk%d' % (1 - pi), 'Ptk%d' % ti], writes=['psB1'])
        S.op('dve', lambda e: e.tensor_tensor(out=self.Y, in0=v4(psB[1][:, :]), in1=self.Y, op=ALU.add),
             reads=['psB1', 'Y'], writes=['Y'])
        yn_ = rot('Yb')
        S.op('pool', lambda e: e.tensor_copy(out=self.Yb[yn_], in_=self.Y), reads=['Y'], writes=['Yb%d' % yn_])
        yi = yn_
        Yb = self.Yb[yi]
        ykey = 'Yb%d' % yi
        yield
        for h in range(4):
            S.op('pe', lambda e, h=h: e.transpose(out=psT[:, h * 128:(h + 1) * 128], in_=self.knT[:, h, :], identity=self.identB),
                 reads=['knT', 'cb'], writes=['psT'])
        for h in range(4):
            S.op('act', lambda e, h=h: e.mul(out=self.Kbg[:, h, :], in_=psT[:, h * 128:(h + 1) * 128], mul=begc[:, h:h + 1]),
                 reads=['psT', 'begc'], writes=['Kbg'])
            kdc = self.kdv[:, h:h + 1] if self.kind == 'S' else self.Ui[:, h, 127:128]
            S.op('dve', lambda e, h=h, kdc=kdc: e.tensor_scalar(out=self.Kd[:, h, :], in0=psT[:, h * 128:(h + 1) * 128],
                                                                scalar1=kdc, scalar2=None, op0=ALU.mult),
                 reads=['psT', 'Ui%d' % h, 'kdv'], writes=['Kd'])
        yield
        for h in range(4):
            S.op('pe', lambda e, h=h: e.transpose(out=psT[:, h * 128:(h + 1) * 128], in_=self.vT[:, h, :],
                                                  identity=self.identB), reads=['vT', 'cb'], writes=['psT'])
        for h in range(4):
            S.op('act', lambda e, h=h: e.mul(out=self.Vb[:, h, :], in_=psT[:, h * 128:(h + 1) * 128],
                                             mul=beta[:, h:h + 1]), reads=['psT', 'beta'], writes=['Vb'])
        yield
        if SUB == 4:
            return self._dn_bail()
        if self.kind == 'S':
            self.dn_state_S(L, Yb, ykey)
        else:
            for h in range(4):
                S.op('pe', lambda e, h=h: e.matmul(psB[0][:, h * 128:(h + 1) * 128], lhsT=self.Kbg[:, h, :], rhs=Yb[:, h, :],
                                                   start=True, stop=True), reads=['Kbg', ykey], writes=['psB0'])
            yield
            S.op('act', lambda e: e.mul(out=self.negWT, in_=v4(psB[0][:, :]), mul=-1.0), reads=['psB0'], writes=['negWT'])
            for h in range(4):
                S.op('pe', lambda e, h=h: e.matmul(psB[1][:, h * 128:(h + 1) * 128], lhsT=Yb[:, h, :], rhs=self.Vb[:, h, :],
                                                   start=True, stop=False), reads=[ykey, 'Vb'], writes=['psB1'])
                S.op('pe', lambda e, h=h: e.matmul(psB[1][:, h * 128:(h + 1) * 128], lhsT=self.negWT[:, h, :], rhs=self.Sb[:, h, :],
                                                   start=False, stop=True), reads=['negWT', 'Sb'], writes=['psB1'])
            yield
            S.op('act', lambda e: e.copy(out=self.vnew, in_=v4(psB[1][:, :])), reads=['psB1'], writes=['vnew'])
            for h in range(4):
                S.op('pe', lambda e, h=h: e.matmul(psB[2][:, h * 128:(h + 1) * 128], lhsT=self.Sb[:, h, :], rhs=self.QdT[:, h, :],
                                                   start=True, stop=False), reads=['Sb', 'QdT'], writes=['psB2'])
                S.op('pe', lambda e, h=h: e.matmul(psB[2][:, h * 128:(h + 1) * 128], lhsT=self.vnew[:, h, :], rhs=self.attnT[:, h, :],
                                                   start=False, stop=True), reads=['vnew', 'attnT'], writes=['psB2'])
            for h in range(4):
                S.op('pe', lambda e, h=h: e.matmul(psB[0][:, h * 128:(h + 1) * 128], lhsT=self.Kd[:, h, :], rhs=self.vnew[:, h, :],
                                                   start=True, stop=True), reads=['Kd', 'vnew'], writes=['psB0'])
            for h in range(4):
                S.op('dve', lambda e, h=h: e.scalar_tensor_tensor(out=self.S32[:, h, :], in0=self.S32[:, h, :],
                                                                  scalar=glast[:, h:h + 1], in1=psB[0][:, h * 128:(h + 1) * 128],
                                                                  op0=ALU.mult, op1=ALU.add),
                     reads=['S32', 'glast', 'psB0'], writes=['S32'])
            yield
            S.op('pool', lambda e: e.tensor_copy(out=self.Sb, in_=self.S32), reads=['S32'], writes=['Sb'])
            if SUB == 5:
                return self._dn_bail()
        ob, nb = (2, 1) if isP else (0, 2)
        yield
        S.op('act', lambda e: e.activation(out=self.sqo, in_=v4(psB[ob][:, :]), func=AF.Square), reads=['psB%d' % ob], writes=['sqb'])
        for h in range(4):
            S.op('pe', lambda e, h=h: e.matmul(psB[nb][:, h * 128:(h + 1) * 128], lhsT=self.onesB, rhs=self.sqo[:, h, :],
                                               start=True, stop=True), reads=['sqb', 'cb'], writes=['psB%d' % nb])
        S.op('act', lambda e: e.activation(out=self.o1, in_=v4(psB[nb][:, :]), func=AF.Sqrt, scale=1.0 / 128, bias=self.epsc),
             reads=['psB%d' % nb], writes=['N32'])
        yield
        S.op('dve', lambda e: e.reciprocal(out=self.o1, in_=self.o1), reads=['N32'], writes=['N32'])
        S.op('dve', lambda e: e.tensor_tensor(out=self.o1, in0=v4(psB[ob][:, :]), in1=self.o1, op=ALU.mult),
             reads=['psB%d' % ob, 'N32'], writes=['N32'])
        S.op('dve', lambda e: e.scalar_tensor_tensor(out=self.mixT[:, 0:4, :], in0=self.o1, scalar=self.pp[:, 78:79],
                                                     in1=self.szT, op0=ALU.mult, op1=ALU.mult),
             reads=['N32', 'pp', 'szT'], writes=['mixT'])
        if SUB == 6:
            return self._dn_bail()
        if last and self.kind == 'P':
            S.dma('sp', self.dn_p[L].rearrange("h k v -> k h v"), self.S32, reads=['S32'], writes=[('dn_p', L)])
            self.conv_tail_out(L, 0, 12, self.dnc_p)

    def _dn_bail(self):
        self.S.op('pool', lambda e: e.memset(self.mixT[:, 0:4, :], 0.0), writes=['mixT'])

    def conv_tail_out(self, L, ci0, n, dst):
        S = self.S
        for j in range(n):
            gi = {0: 0, 4: 1, 8: 2}.get(((ci0 + j) // 4) * 4, None) if ci0 + j < 12 else (3 if ci0 + j < 16 else 4)
            S.dma('sp', dst[L][:, j * 128:(j + 1) * 128].rearrange("r p -> p r"), self.rawP[:, ci0 + j, 0:3],
                  reads=['raw%d' % gi], writes=[('ctail', L, ci0 + j)], allow_slow_non_contiguous=True)

    def ssd(self, L, t, last):
        S = self.S
        psT, psB, sm, pb = self.psT, self.psB, self.sm, self.pb
        dt, glast = sm[:, 20:24], sm[:, 52:60]
        ss2, rstd = sm[:, 64:66], sm[:, 66:68]
        isP = (self.kind == 'P')
        yb = 3 if isP else 0
        for c in range(2):
            S.op('pe', lambda e, c=c: e.transpose(out=psB[3][:, c * 128:(c + 1) * 128], in_=self.xT32[:, c, :], identity=self.identF),
                 reads=['ssmxbc3', 'cf'], writes=['psB3'])
        S.op('act', lambda e: e.copy(out=self.xtok, in_=psB[3][:, 0:256]), reads=['psB3'], writes=['xtok'])
        S.op('dve', lambda e: e.tensor_tensor(out=self.xdt, in0=self.xtok.rearrange("p (a b) -> p a b", a=4),
                                              in1=dt.unsqueeze(2).broadcast_to([128, 4, 64]), op=ALU.mult),
             reads=['xtok', 'sp'], writes=['xdt'])
        for g in range(2):
            S.op('pe', lambda e, g=g: e.transpose(out=psT[:, 512 + g * 128:512 + (g + 1) * 128], in_=self.BT[:, g, :], identity=self.identB),
                 reads=['ssmxbc3', 'cb'], writes=['psT'])
        yield
        for h in range(4):
            g = h // 2
            kdc = self.kdS[:, h:h + 1]
            S.op('dve', lambda e, h=h, g=g, kdc=kdc: e.tensor_scalar(out=self.Bd[:, h, :], in0=psT[:, 512 + g * 128:512 + (g + 1) * 128],
                                                                     scalar1=kdc, scalar2=None, op0=ALU.mult),
                 reads=['psT', 'kdv'] + ['Ui%d' % (4 + h)], writes=['Bd'])
        for g in range(2):
            S.op('pe', lambda e, g=g: e.matmul(psB[3][:, 256 + g * 128:256 + (g + 1) * 128], lhsT=self.BT[:, g, :], rhs=self.CT[:, g, :],
                                               start=True, stop=True), reads=['ssmxbc3', 'ssmxbc4'], writes=['psB3'])
        yield
        for h in range(4):
            g = h // 2
            S.op('dve', lambda e, h=h, g=g: e.tensor_tensor(out=self.sattnT[:, h, :], in0=psB[3][:, 256 + g * 128:256 + (g + 1) * 128],
                                                            in1=self.Ui[:, 4 + h, :], op=ALU.mult),
                 reads=['psB3', 'Ui%d' % (4 + h)], writes=['sattnT'])
            S.op('pool', lambda e, h=h, g=g: e.tensor_tensor(out=self.CdT[:, h, :], in0=self.CT[:, g, :], in1=self.E[:, 4 + h, :],
                                                             op=ALU.mult), reads=['ssmxbc4', 'E%d' % (4 + h)], writes=['CdT'])
        yield
        if self.kind == 'S':
            self.ssd_core_S(L)
        else:
            yield from self.ssd_core(L, t)
        yield
        for h in range(4):
            S.op('dve', lambda e, h=h: e.scalar_tensor_tensor(out=self.y2[:, h * 64:(h + 1) * 64], in0=self.xtok[:, h * 64:(h + 1) * 64],
                                                              scalar=pb[:, 16 + h:17 + h], in1=psB[yb][:, h * 64:(h + 1) * 64],
                                                              op0=ALU.mult, op1=ALU.add),
                 reads=['xtok', 'pb', 'psB%d' % yb], writes=['y2'])
        S.op('pool', lambda e: e.tensor_tensor(out=self.y2, in0=self.y2, in1=self.sz, op=ALU.mult), reads=['y2', 'sz'], writes=['y2'])
        yield
        for g in range(2):
            S.op('act', lambda e, g=g: e.activation(out=self.y4[:, g * 128:(g + 1) * 128], in_=self.y2[:, g * 128:(g + 1) * 128],
                                                    func=AF.Square, accum_out=ss2[:, g:g + 1]), reads=['y2'], writes=['y4', 'ss2'])
        S.op('act', lambda e: e.activation(out=rstd, in_=ss2, func=AF.Sqrt, scale=1.0 / 128, bias=self.epsc),
             reads=['ss2'], writes=['rstd2'])
        S.op('dve', lambda e: e.reciprocal(out=rstd, in_=rstd), reads=['rstd2'], writes=['rstd2'])
        yield
        for g in range(2):
            S.op('dve', lambda e, g=g: e.scalar_tensor_tensor(out=self.y4[:, g * 128:(g + 1) * 128], in0=self.y2[:, g * 128:(g + 1) * 128],
                                                              scalar=rstd[:, g:g + 1], in1=pb[:, 32 + g * 128:32 + (g + 1) * 128],
                                                              op0=ALU.mult, op1=ALU.mult),
                 reads=['y2', 'rstd2', 'pb', 'y4'], writes=['y4'])
        for g in range(2):
            S.op('pe', lambda e, g=g: e.transpose(out=psT[:, 512 + g * 128:512 + (g + 1) * 128], in_=self.y4[:, g * 128:(g + 1) * 128],
                                                  identity=self.identB), reads=['y4', 'cb'], writes=['psT'])
        S.op('act', lambda e: e.copy(out=self.mixT[:, 4:6, :], in_=psT[:, 512:768].rearrange("p (a b) -> p a b", a=2)),
             reads=['psT'], writes=['mixT'])
        if last:
            self.ssm_state_out(self.ssm_p[L], self.h32, 'h32')
            self.conv_tail_out(L, 12, 6, self.ssmc_p)

    def ssm_state_out(self, dst, h32, hkey):
        S, psB = self.S, self.psB
        for c in range(2):
            S.op('pe', lambda e, c=c: e.transpose(out=psB[3][:, c * 128:(c + 1) * 128], in_=h32[:, c * 128:(c + 1) * 128],
                                                  identity=self.identF), reads=[hkey, 'cf'], writes=['psB3'])
        S.op('act', lambda e: e.copy(out=self.hout, in_=psB[3][:, 0:256].rearrange("p (a b) -> p a b", a=2)),
             reads=['psB3'], writes=['hout'])
        S.dma('sp', dst.rearrange("(c h2) p n -> (h2 p) c n", c=2), self.hout, reads=['hout'], writes=[('ssmout', id(dst))])

    def ssd_core(self, L, t):
        S, psB, sm = self.S, self.psB, self.sm
        glast = sm[:, 52:60]
        for h in range(4):
            S.op('pe', lambda e, h=h: e.matmul(psB[3][:, h * 64:(h + 1) * 64], lhsT=self.sattnT[:, h, :], rhs=self.xdt[:, h, :],
                                               start=True, stop=False), reads=['sattnT', 'xdt'], writes=['psB3'])
            S.op('pe', lambda e, h=h: e.matmul(psB[3][:, h * 64:(h + 1) * 64], lhsT=self.CdT[:, h, :], rhs=self.hb[:, h * 64:(h + 1) * 64],
                                               start=False, stop=True), reads=['CdT', 'hb'], writes=['psB3'])
        yield
        for h in range(4):
            S.op('pe', lambda e, h=h: e.matmul(psB[3][:, 256 + h * 64:256 + (h + 1) * 64], lhsT=self.Bd[:, h, :], rhs=self.xdt[:, h, :],
                                               start=True, stop=True), reads=['Bd', 'xdt'], writes=['psB3'])
        for h in range(4):
            S.op('dve', lambda e, h=h: e.scalar_tensor_tensor(out=self.h32[:, h * 64:(h + 1) * 64], in0=self.h32[:, h * 64:(h + 1) * 64],
                                                              scalar=glast[:, 4 + h:5 + h], in1=psB[3][:, 256 + h * 64:256 + (h + 1) * 64],
                                                              op0=ALU.mult, op1=ALU.add),
                 reads=['h32', 'glast', 'psB3'], writes=['h32'])
        yield
        S.op('pool', lambda e: e.tensor_copy(out=self.hb, in_=self.h32), reads=['h32'], writes=['hb'])

    def swa(self, L, t, last):
        S = self.S
        psT, psB, sm, pb = self.psT, self.psB, self.sm, self.pb
        rot = self.rot
        negm, rsum, es, rden = sm[:, 60:61], sm[:, 61:62], sm[:, 62:63], sm[:, 63:64]
        cur, prev = t % 2, 1 - (t % 2)
        kind = self.kind
        mask = self.swaMP0 if t == 0 else (self.swaMP1 if t == 1 else self.swaMP)
        if kind == 'S':
            mask = self.swaMS
        isP = (kind == 'P')
        psA = self.psA
        rotb, rotk = (psA[:, 0:384], 'psA0') if isP else (psB[3][:, 0:384], 'psB3')
        vtb, vtk = (psA[:, 384:512], 'psA0') if isP else (psB[4][:, 0:128], 'psB4')
        rb = rot('ropet')
        rt = self.ropet[rb]
        S.dma('pool', rt, self.roped[:, :, t * 128:(t + 1) * 128].rearrange("a p n -> p a n"), writes=['ropet%d' % rb])
        for j in range(3):
            S.op('pe', lambda e, j=j: e.matmul(rotb[:, j * 128:(j + 1) * 128], lhsT=self.Rm, rhs=self.sw32[:, j, :],
                                               start=True, stop=True), reads=['sw32', 'cf'], writes=[rotk])
        S.op('dve', lambda e: e.tensor_tensor(out=self.r1, in0=self.sw32[:, 0:3, :],
                                              in1=rt[:, 0, :].unsqueeze(1).broadcast_to([128, 3, 128]), op=ALU.mult),
             reads=['sw32', 'ropet%d' % rb], writes=['r1'])
        yield
        S.op('dve', lambda e: e.tensor_tensor(out=self.r2, in0=rotb.rearrange("p (a b) -> p a b", a=3),
                                              in1=rt[:, 1, :].unsqueeze(1).broadcast_to([128, 3, 128]), op=ALU.mult),
             reads=[rotk, 'ropet%d' % rb], writes=['r2'])
        S.op('pool', lambda e: e.tensor_tensor(out=self.r1, in0=self.r1, in1=self.r2, op=ALU.add), reads=['r1', 'r2'], writes=['r1'])
        for j in range(2):
            for r in range(2):
                S.op('act', lambda e, j=j, r=r: e.copy(out=self.qz[r * 64:(r + 1) * 64, r * 2 + j, :], in_=self.r1[r * 64:(r + 1) * 64, j, :]),
                     reads=['r1'], writes=['qz'])
        S.op('pool', lambda e: e.tensor_copy(out=self.kring[:, cur, :], in_=self.r1[:, 2, :]), reads=['r1'], writes=['kring%d' % cur])
        yield
        S.op('pe', lambda e: e.transpose(out=vtb, in_=self.sw32[:, 3, :], identity=self.identF),
             reads=['sw32', 'cf'], writes=[vtk])
        S.op('act', lambda e: e.copy(out=self.v32, in_=vtb), reads=[vtk], writes=['v32'])
        S.op('dve', lambda e: e.tensor_copy(out=self.vring[:, cur, 0, 0:64], in_=vtb[:, 0:64]), reads=[vtk], writes=['vring%d' % cur])
        S.op('dve', lambda e: e.tensor_copy(out=self.vring[:, cur, 1, 64:128], in_=vtb[:, 64:128]), reads=[vtk], writes=['vring%d' % cur])
        yield
        if kind == 'S':
            self.swa_cache_load(L)
        yield from self.swa_attend(L, t, mask, lambda r: [(self.kring[:, prev, :], 'kring%d' % prev)],
                                   lambda r: [(self.vring[:, prev, r, :], 'vring%d' % prev)], cur)
        if kind == 'S':
            S.op('pe', lambda e: e.transpose(out=psB[4][:, 128:256], in_=self.r1[:, 2, :], identity=self.identF),
                 reads=['r1', 'cf'], writes=['psB4'])
            S.op('act', lambda e: e.copy(out=self.r2[:, 0, :], in_=psB[4][:, 128:256]), reads=['psB4'], writes=['r2'])
            for s in range(16):
                S.dma('sp', self.k_s[L][s, 120:128, :], self.r2[8 * s:8 * s + 8, 0, :], reads=['r2'], writes=[('k_s', L, 1, s)])
                S.dma('pool', self.v_s[L][s, 120:128, :], self.v32[8 * s:8 * s + 8, :], reads=['v32'], writes=[('v_s', L, 1, s)])
            S.dma('pool', self.k_s[L][:, 0:120, :], self.ck[L][:, 8:128, :], writes=[('k_s', L, 0)])
            S.dma('pool', self.v_s[L][:, 0:120, :], self.cvd[L][:, 8:128, :], writes=[('v_s', L, 0)])
        if last and kind == 'P':
            S.op('pe', lambda e: e.transpose(out=vtb, in_=self.r1[:, 2, :], identity=self.identF),
                 reads=['r1', 'cf'], writes=[vtk])
            S.op('act', lambda e: e.copy(out=self.r2[:, 0, :], in_=vtb), reads=[vtk], writes=['r2'])
            S.dma('sp', self.k_p[L], self.r2[:, 0, :], reads=['r2'], writes=[('k_p', L)])
            S.dma('sp', self.v_p[L], self.v32, reads=['v32'], writes=[('v_p', L)])

    def swa_attend(self, L, t, mask, kprev, vprev, cur):
        S = self.S
        psT, psB, sm, pb = self.psT, self.psB, self.sm, self.pb
        rot = self.rot
        negm, rsum, es, rden = sm[:, 60:61], sm[:, 61:62], sm[:, 62:63], sm[:, 63:64]
        e32 = self.e32[0]
        isP = (self.kind == 'P')
        pvb, pvk = (psB[4], 'psB4') if isP else (psB[2], 'psB2')
        for j in range(2):
            for r in range(2):
                hq = r * 2 + j
                bi = hq % 2
                bank, bk = (self.psA[:, 512:1024], 'psA1') if isP else (psB[bi], 'psB%d' % bi)
                (kp, kpk), = kprev(r)
                if self.kind == 'S':
                    S.op('pool', lambda e, hq=hq: e.tensor_copy(out=self.ZW[:, :, 120:128],
                                                                in_=self.qz[:, hq, :].rearrange("p (s u) -> p s u", u=8)),
                         reads=['qz'], writes=['ZW'])
                    for s in range(16):
                        S.op('pe', lambda e, s=s, bank=bank: e.matmul(bank[:, 0:128], lhsT=self.ZW[:, s, 120 - 8 * s:248 - 8 * s],
                                                                      rhs=self.KcT[:, s, :], start=(s == 0), stop=(s == 15)),
                             reads=['ZW', 'KcT'], writes=[bk])
                else:
                    S.op('pe', lambda e, hq=hq, bank=bank, kp=kp: e.matmul(bank[:, 0:128], lhsT=self.qz[:, hq, :],
                                                                           rhs=kp, start=True, stop=True),
                         reads=['qz', kpk], writes=[bk])
                S.op('pe', lambda e, hq=hq, bank=bank: e.matmul(bank[:, 128:256], lhsT=self.qz[:, hq, :],
                                                                rhs=self.kring[:, cur, :], start=True, stop=True),
                     reads=['qz', 'kring%d' % cur], writes=[bk])
                yield
                S.op('dve', lambda e, bank=bank: e.reduce_max(out=rden, in_=bank[:, 0:256], axis=AX.X), reads=[bk], writes=['rden'])
                S.op('dve', lambda e, hq=hq: e.tensor_scalar(out=negm, in0=rden, scalar1=-0.125, scalar2=self.negsink[:, hq:hq + 1],
                                                             op0=ALU.mult, op1=ALU.min), reads=['rden', 'negsink'], writes=['negm'])
                S.op('act', lambda e, bank=bank: e.activation(out=e32, in_=bank[:, 0:256], func=AF.Exp, bias=negm, scale=0.125),
                     reads=[bk, 'negm'], writes=['e32'])
                S.op('dve', lambda e: e.scalar_tensor_tensor(out=self.em, in0=e32, scalar=1.0, in1=mask, op0=ALU.mult, op1=ALU.mult,
                                                             accum_out=rsum), reads=['e32', 'cf'], writes=['em', 'rsum'])
                yield
                S.op('act', lambda e, hq=hq: e.activation(out=es, in_=negm, func=AF.Exp, bias=pb[:, 20 + hq:21 + hq]),
                     reads=['negm', 'pb'], writes=['es'])
                S.op('dve', lambda e: e.tensor_tensor(out=rden, in0=rsum, in1=es, op=ALU.add), reads=['rsum', 'es'], writes=['rden'])
                S.op('dve', lambda e: e.reciprocal(out=rden, in_=rden), reads=['rden'], writes=['rden'])
                S.op('dve', lambda e: e.tensor_scalar(out=self.pn, in0=self.em, scalar1=rden, scalar2=None, op0=ALU.mult),
                     reads=['em', 'rden'], writes=['pn'])
                yield
                for q in range(2):
                    S.op('pe', lambda e, q=q: e.transpose(out=psT[:, 768 + q * 128:768 + (q + 1) * 128], in_=self.pn[:, q * 128:(q + 1) * 128],
                                                          identity=self.identB), reads=['pn', 'cb'], writes=['psT'])
                pb_ = rot('pnT')
                pnT = self.pnT[pb_]
                S.op('act', lambda e, pnT=pnT: e.copy(out=pnT, in_=psT[:, 768:1024].rearrange("p (a b) -> p a b", a=2)),
                     reads=['psT'], writes=['pnT%d' % pb_])
                yield
                out = pvb[:, j * 128:(j + 1) * 128]
                (vp, vpk), = vprev(r)
                if self.kind == 'S':
                    S.op('pe', lambda e, out=out, pnT=pnT, r=r: e.matmul(out, lhsT=self.vring[:, cur, r, :], rhs=pnT[:, 1, :],
                                                                         start=(r == 0), stop=False),
                         reads=['vring%d' % cur, 'pnT%d' % pb_], writes=[pvk])
                    for s in range(16):
                        S.op('pe', lambda e, s=s, j=j, pnT=pnT, r=r: e.matmul(
                            pvb[:, j * 128 + 8 * s:j * 128 + 8 * s + 8], lhsT=self.VA[:, s, r, :], rhs=pnT[:, 0, 8 * s:8 * s + 8],
                            start=False, stop=(r == 1 and s == 15)), reads=['VA', 'pnT%d' % pb_], writes=[pvk])
                else:
                    S.op('pe', lambda e, out=out, vp=vp, pnT=pnT, r=r: e.matmul(out, lhsT=vp, rhs=pnT[:, 0, :], start=(r == 0), stop=False),
                         reads=[vpk, 'pnT%d' % pb_], writes=[pvk])
                    S.op('pe', lambda e, out=out, pnT=pnT, r=r: e.matmul(out, lhsT=self.vring[:, cur, r, :], rhs=pnT[:, 1, :],
                                                                         start=False, stop=(r == 1)),
                         reads=['vring%d' % cur, 'pnT%d' % pb_], writes=[pvk])
            S.op('act', lambda e, j=j: e.copy(out=self.mixT[:, 6 + j, :], in_=pvb[:, j * 128:(j + 1) * 128]),
                 reads=[pvk], writes=['mixT'])
            yield

    def finish(self, L):
        pass

    RAWK = {0: 'raw0', 1: 'raw1', 2: 'raw2', 3: 'raw3', 4: 'raw4'}

    def _rawkeys(self, ci0, n):
        ks = []
        for ci in range(ci0, ci0 + n):
            k = 'raw%d' % (ci // 4 if ci < 16 else 4)
            if k not in ks:
                ks.append(k)
        return ks

    def conv_hist_load(self, L):
        S, psB = self.S, self.psB
        for (src, c0, ci0) in [(self.sdnc[L], 0, 0), (self.sdnc[L], 768, 6), (self.sssmc[L], 0, 12)]:
            S.dma('sp', self.cstg[0:48, :], src[:, :, c0:c0 + 768].rearrange("s r c -> (s r) c"), writes=['cstg'])
            for k in range(6):
                S.op('pe', lambda e, k=k: e.transpose(out=psB[2][:, k * 48:(k + 1) * 48], in_=self.cstg[0:48, k * 128:(k + 1) * 128],
                                                      identity=self.identF[0:48, 0:48]), reads=['cstg', 'cf'], writes=['psB2'])
            for k in range(6):
                S.op('act', lambda e, k=k, ci0=ci0: e.copy(out=self.rawS[:, ci0 + k, :, 0:3],
                                                           in_=psB[2][:, k * 48:(k + 1) * 48].rearrange("p (s r) -> p s r", r=3)),
                     reads=['psB2'], writes=self._rawkeys(ci0 + k, 1))

    def conv_tail_out_S(self, L):
        S, psB = self.S, self.psB
        for (dst, col0, ci0, n) in [(self.dnc_s[L], 0, 0, 4), (self.dnc_s[L], 512, 4, 4), (self.dnc_s[L], 1024, 8, 4),
                                    (self.ssmc_s[L], 0, 12, 4), (self.ssmc_s[L], 512, 16, 2)]:
            for k in range(n):
                S.op('pool', lambda e, k=k, ci0=ci0: e.tensor_copy(out=self.ctmp[:, k, :].rearrange("p (s r) -> p s r", r=3),
                                                                   in_=self.rawS[:, ci0 + k, :, 8:11]),
                     reads=self._rawkeys(ci0 + k, 1), writes=['ctmp'])
            for k in range(n):
                S.op('pe', lambda e, k=k: e.transpose(out=psB[2][0:48, k * 128:(k + 1) * 128], in_=self.ctmp[:, k, :],
                                                      identity=self.identF), reads=['ctmp', 'cf'], writes=['psB2'])
            S.op('act', lambda e, n=n: e.copy(out=self.cstg[0:48, 0:n * 128], in_=psB[2][0:48, 0:n * 128]),
                 reads=['psB2'], writes=['cstg'])
            S.dma('sp', dst[:, :, col0:col0 + n * 128].rearrange("s r c -> (s r) c"), self.cstg[0:48, 0:n * 128],
                  reads=['cstg'], writes=[('ctailS', L, ci0)])

    def dn_state_S(self, L, Yb, ykey):
        S, psB, rot = self.S, self.psB, self.rot
        for h in range(4):
            S.dma('sp', self.Sall, self.sdn[L][:, h].rearrange("s k v -> k s v"), writes=['Sall'])
            S.op('pool', lambda e: e.tensor_copy(out=self.Sball, in_=self.Sall), reads=['Sall'], writes=['Sball'])
            S.op('pe', lambda e, h=h: e.matmul(psB[3][:, 0:128], lhsT=self.Kbg[:, h, :], rhs=Yb[:, h, :], start=True, stop=True),
                 reads=['Kbg', ykey], writes=['psB3'])
            S.op('act', lambda e: e.mul(out=self.ZW[:, :, 120:128], in_=psB[3][:, 0:128].rearrange("p (s u) -> p s u", u=8), mul=-1.0),
                 reads=['psB3'], writes=['ZW'])
            S.op('pe', lambda e, h=h: e.matmul(psB[4][:, 0:128], lhsT=Yb[:, h, :], rhs=self.Vb[:, h, :], start=True, stop=False),
                 reads=[ykey, 'Vb'], writes=['psB4'])
            for s in range(16):
                S.op('pe', lambda e, s=s: e.matmul(psB[4][:, 0:128], lhsT=self.ZW[:, s, 120 - 8 * s:248 - 8 * s], rhs=self.Sball[:, s, :],
                                                   start=False, stop=(s == 15)), reads=['ZW', 'Sball'], writes=['psB4'])
            S.op('act', lambda e, h=h: e.copy(out=self.vnew[:, h, :], in_=psB[4][:, 0:128]), reads=['psB4'], writes=['vnew'])
            S.op('pe', lambda e, h=h: e.matmul(psB[0][:, h * 128:(h + 1) * 128], lhsT=self.vnew[:, h, :], rhs=self.attnT[:, h, :],
                                               start=True, stop=False), reads=['vnew', 'attnT'], writes=['psB0'])
            for s in range(16):
                S.op('pe', lambda e, h=h, s=s: e.matmul(psB[0][:, h * 128 + 8 * s:h * 128 + 8 * s + 8], lhsT=self.Sball[:, s, :],
                                                        rhs=self.QdT[:, h, 8 * s:8 * s + 8], start=False, stop=(s == 15)),
                     reads=['Sball', 'QdT'], writes=['psB0'])
            for s in range(16):
                kb = rot('KdM')
                bi = 1 if (s // 4) % 2 == 0 else 3
                col = (s % 4) * 128
                S.op('dve', lambda e, h=h, s=s, kb=kb: e.tensor_scalar(out=self.KdM[kb], in0=self.Kd[:, h, :], scalar1=self.rowm[:, s:s + 1],
                                                                       scalar2=None, op0=ALU.mult), reads=['Kd', 'cf'], writes=['KdM%d' % kb])
                S.op('pe', lambda e, h=h, kb=kb, bi=bi, col=col: e.matmul(psB[bi][:, col:col + 128], lhsT=self.KdM[kb], rhs=self.vnew[:, h, :],
                                                                          start=True, stop=True),
                     reads=['KdM%d' % kb, 'vnew'], writes=['psB%d' % bi])
                S.op('dve', lambda e, h=h, s=s, bi=bi, col=col: e.scalar_tensor_tensor(
                    out=self.Sall[:, s, :], in0=self.Sall[:, s, :], scalar=self.glS[:, h, s:s + 1], in1=psB[bi][:, col:col + 128],
                    op0=ALU.mult, op1=ALU.add), reads=['Sall', 'glS', 'psB%d' % bi], writes=['Sall'])
            S.dma('sp', self.dn_s[L][:, h].rearrange("s k v -> k s v"), self.Sall, reads=['Sall'], writes=[('dn_s', L, h)])

    def ssd_core_S(self, L):
        S, psA, psB, rot = self.S, self.psA, self.psB, self.rot
        for h in range(4):
            S.dma('sp', self.hin[0:64, :, :], self.sssm[L][:, h].rearrange("s p n -> p s n"), writes=['hin'])
            for s in range(16):
                S.op('pe', lambda e, s=s: e.transpose(out=psA[:, s * 64:(s + 1) * 64], in_=self.hin[0:64, s, :],
                                                      identity=self.identF[0:64, 0:64]), reads=['hin', 'cf'], writes=['psA0', 'psA1'])
            S.op('act', lambda e: e.copy(out=self.Hh, in_=psA[:, :].rearrange("p (s q) -> p s q", q=64)),
                 reads=['psA0', 'psA1'], writes=['Hh'])
            S.op('pool', lambda e: e.tensor_copy(out=self.Hhb, in_=self.Hh), reads=['Hh'], writes=['Hhb'])
            S.op('pool', lambda e, h=h: e.tensor_copy(out=self.ZW[:, :, 120:128], in_=self.CdT[:, h, :].rearrange("p (s u) -> p s u", u=8)),
                 reads=['CdT'], writes=['ZW'])
            S.op('pe', lambda e, h=h: e.matmul(psB[0][:, h * 64:(h + 1) * 64], lhsT=self.sattnT[:, h, :], rhs=self.xdt[:, h, :],
                                               start=True, stop=False), reads=['sattnT', 'xdt'], writes=['psB0'])
            for s in range(16):
                S.op('pe', lambda e, h=h, s=s: e.matmul(psB[0][:, h * 64:(h + 1) * 64], lhsT=self.ZW[:, s, 120 - 8 * s:248 - 8 * s],
                                                        rhs=self.Hhb[:, s, :], start=False, stop=(s == 15)),
                     reads=['ZW', 'Hhb'], writes=['psB0'])
            for s in range(16):
                kb = rot('KdM')
                bi = 1 if (s // 8) % 2 == 0 else 4
                col = (s % 8) * 64
                S.op('dve', lambda e, h=h, s=s, kb=kb: e.tensor_scalar(out=self.KdM[kb], in0=self.Bd[:, h, :], scalar1=self.rowm[:, s:s + 1],
                                                                       scalar2=None, op0=ALU.mult), reads=['Bd', 'cf'], writes=['KdM%d' % kb])
                S.op('pe', lambda e, h=h, kb=kb, bi=bi, col=col: e.matmul(psB[bi][:, col:col + 64], lhsT=self.KdM[kb], rhs=self.xdt[:, h, :],
                                                                          start=True, stop=True),
                     reads=['KdM%d' % kb, 'xdt'], writes=['psB%d' % bi])
                S.op('dve', lambda e, h=h, s=s, bi=bi, col=col: e.scalar_tensor_tensor(
                    out=self.Hh[:, s, :], in0=self.Hh[:, s, :], scalar=self.glS[:, 4 + h, s:s + 1], in1=psB[bi][:, col:col + 64],
                    op0=ALU.mult, op1=ALU.add), reads=['Hh', 'glS', 'psB%d' % bi], writes=['Hh'])
            for half in range(2):
                for s8 in range(8):
                    S.op('pe', lambda e, half=half, s8=s8: e.transpose(out=psA[0:64, s8 * 128:(s8 + 1) * 128],
                                                                       in_=self.Hh[:, half * 8 + s8, :], identity=self.identF),
                         reads=['Hh', 'cf'], writes=['psA0', 'psA1'])
                S.op('act', lambda e, half=half: e.copy(out=self.hin[0:64, half * 8:(half + 1) * 8, :],
                                                        in_=psA[0:64, :].rearrange("p (s n) -> p s n", n=128)),
                     reads=['psA0', 'psA1'], writes=['hin'])
            S.dma('sp', self.ssm_s[L][:, h].rearrange("s p n -> p s n"), self.hin[0:64, :, :], reads=['hin'], writes=[('ssm_s', L, h)])

    def swa_cache_load(self, L):
        S, psB, rot = self.S, self.psB, self.rot
        S.op('pool', lambda e: e.memset(self.VA, 0.0), writes=['VA'])
        for s in range(16):
            kb = rot('kst')
            S.dma('sp', self.kst[:, kb, :], self.ck[L][s], writes=['kst%d' % kb])
            S.op('pe', lambda e, s=s, kb=kb: e.transpose(out=psB[3][:, (s % 4) * 128:(s % 4 + 1) * 128], in_=self.kst[:, kb, :],
                                                         identity=self.identF), reads=['kst%d' % kb, 'cf'], writes=['psB3'])
            if s % 4 == 3:
                S.op('act', lambda e, s=s: e.copy(out=self.KcT[:, s - 3:s + 1, :], in_=psB[3][:, :].rearrange("p (a b) -> p a b", a=4)),
                     reads=['psB3'], writes=['KcT'])
            vb = rot('vst')
            S.dma('pool', self.vst[:, vb, :], self.cvd[L][s], writes=['vst%d' % vb])
            S.op('dve', lambda e, s=s, vb=vb: e.tensor_copy(out=self.VA[:, s, 0, 0:64], in_=self.vst[:, vb, 0:64]),
                 reads=['vst%d' % vb], writes=['VA'])
            S.op('dve', lambda e, s=s, vb=vb: e.tensor_copy(out=self.VA[:, s, 1, 64:128], in_=self.vst[:, vb, 64:128]),
                 reads=['vst%d' % vb], writes=['VA'])

D = 1024
DFF = 2816
NCI = 3600
EPS = 1e-6
PAST_LEN = 8192
NSEQ = 16


class Arena:
    def __init__(self, nc, nbytes):
        self.t = nc.alloc_sbuf_tensor('arena', [128, nbytes // 2], BF16)
        self.n = nbytes // 2
        self.top = 0
        self.marks = []

    def alloc(self, shape, dt=F32):
        per = 1
        for s in shape[1:]:
            per *= s
        ne = per * (2 if dt == F32 else 1)
        ne = (ne + 15) // 16 * 16
        assert self.top + ne <= self.n, "SBUF arena overflow %d" % ((self.top + ne) * 2)
        v = self.t[0:shape[0], self.top:self.top + (per * (2 if dt == F32 else 1))]
        self.top += ne
        if dt == F32:
            v = v.bitcast(F32)
        if len(shape) == 3:
            v = v.rearrange("p (a b) -> p a b", a=shape[1])
        elif len(shape) == 4:
            v = v.rearrange("p (a b c) -> p a b c", a=shape[1], b=shape[2])
        return v

    def mark(self):
        self.marks.append(self.top)

    def release(self):
        self.top = self.marks.pop()


DBG = []


def build_program(NT, do_sample=True, stage=99):
    nc = bass.Bass("TRN2", target_bir_lowering=False)
    S = Sched(nc)
    NTILE = NT + 2
    SMP = NT + 1

    def din(name, shape, dt=F32):
        return nc.dram_tensor(name, list(shape), dt, kind="ExternalInput").ap()

    def dout(name, shape):
        return nc.dram_tensor(name, list(shape), F32, kind="ExternalOutput").ap()

    def dscr(name, shape):
        return nc.dram_tensor(name, list(shape), F32, kind="Internal").ap()

    xin = din("xin", [NTILE * 128, D])
    w_in = din("w_in", [2, D, NCI])
    w_out = din("w_out", [2, D, D])
    w_fi = din("w_fi", [2, D, 2 * DFF])
    w_fo = din("w_fo", [2, DFF, D])
    gbd = din("gb", [2, 4, 128, D])
    ppd = din("pp", [2, 128, 80])
    pbd = din("pb", [2, 128, 288])
    cfd = din("cf", [128, 2560])
    cbd = din("cb", [128, 256], BF16)
    roped = din("rope", [2, 128, NTILE * 128])
    y_o = dout("y", [(NT + 1) * 128, D])
    dn_p = dout("dn_p", [2, 4, 128, 128])
    dnc_p = dout("dnc_p", [2, 3, 1536])
    ssm_p = dout("ssm_p", [2, 4, 64, 128])
    ssmc_p = dout("ssmc_p", [2, 3, 768])
    k_p = dout("k_p", [2, 128, 128])
    v_p = dout("v_p", [2, 128, 128])
    smp_io = None
    if do_sample:
        smp_io = (din("sdn", [2, 16, 4, 128, 128]), din("sdnc", [2, 16, 3, 1536]), din("sssm", [2, 16, 4, 64, 128]),
                  din("sssmc", [2, 16, 3, 768]), din("ck", [2, 16, 128, 128]), din("cv", [2, 16, 128, 128]),
                  dout("dn_s", [2, 16, 4, 128, 128]), dout("dnc_s", [2, 16, 3, 1536]), dout("ssm_s", [2, 16, 4, 64, 128]),
                  dout("ssmc_s", [2, 16, 3, 768]), dout("k_s", [2, 16, 128, 128]), dout("v_s", [2, 16, 128, 128]))
    xa = dscr("xa", [NTILE * 128, D])
    xb = dscr("xb", [NTILE * 128, D])

    psA = nc.alloc_psum_tensor("psA", [128, 1024], F32)
    psT = nc.alloc_psum_tensor("psT", [128, 1024], BF16)
    psB = [nc.alloc_psum_tensor("psB%d" % i, [128, 512], F32) for i in range(5)]

    ar = Arena(nc, 208000)
    cf = ar.alloc([128, 2560])
    cb = ar.alloc([128, 256], BF16)
    S.dma('sp', cf, cfd, writes=['cf'])
    S.dma('sp', cb, cbd, writes=['cb'])
    identF = cf[:, 0:128]
    mUiP = cf[:, 128:256]
    mLsP = cf[:, 256:384]
    Rm = cf[:, 384:512]
    swaMP = cf[:, 512:768]
    swaMP0 = cf[:, 768:1024]
    swaMP1 = cf[:, 2304:2560]
    validc = cf[:, 1024:1025]
    identB = cb[:, 0:128]
    onesB = cb[:, 128:256]

    dbg_list = DBG
    def dbg_out(name, ap, keys, dt=F32):
        if name not in dbg_list:
            return
        shp = list(ap.shape)
        d = nc.dram_tensor("dbg_" + name, shp, dt, kind="ExternalOutput").ap()
        S.dma('sp', d, ap, reads=keys, writes=[('dbg', name)])

    rr = {}

    def rot(name, n=2):
        i = rr.get(name, 0)
        rr[name] = i + 1
        return i % n

    def engcycle(name, engs):
        return engs[rot('ec_' + name, len(engs))]

    def load_weights_bf16(dst, src, nrow_chunks, ncols, stg, tag, colblk):
        i = 0
        for rc in range(nrow_chunks):
            for c0 in range(0, ncols, colblk):
                cw = min(colblk, ncols - c0)
                b = rot('stg')
                S.dma('sp' if i % 2 == 0 else 'pool', stg[b][:, 0:cw], src[rc * 128:(rc + 1) * 128, c0:c0 + cw],
                      writes=['stg%d' % b])
                eng = ('act', 'dve', 'pool')[i % 3]
                if eng == 'act':
                    S.op('act', lambda e, b=b, rc=rc, c0=c0, cw=cw: e.copy(out=dst[:, rc, c0:c0 + cw], in_=stg[b][:, 0:cw]),
                         reads=['stg%d' % b], writes=[tag])
                else:
                    S.op(eng, lambda e, b=b, rc=rc, c0=c0, cw=cw: e.tensor_copy(out=dst[:, rc, c0:c0 + cw], in_=stg[b][:, 0:cw]),
                         reads=['stg%d' % b], writes=[tag])
                i += 1

    def rms_rstd(src_ap, src_keys, n, junk, ss, rs, tag):
        S.op('act', lambda e: e.activation(out=junk, in_=src_ap, func=AF.Square, accum_out=ss),
             reads=src_keys, writes=['xn', tag + 'ss'])
        S.op('act', lambda e: e.activation(out=rs, in_=ss, func=AF.Sqrt, scale=1.0 / n, bias=epsc),
             reads=[tag + 'ss'], writes=[tag + 'rs'])
        S.op('dve', lambda e: e.reciprocal(out=rs, in_=rs), reads=[tag + 'rs'], writes=[tag + 'rs'])

    def norm_to_T(xt, xkey, gB, hTdst, hkey, junk, xn, ss, rs):
        rms_rstd(xt, [xkey], D, junk, ss, rs, 'nt')
        S.op('dve', lambda e: e.scalar_tensor_tensor(out=xn, in0=xt, scalar=rs, in1=gB, op0=ALU.mult, op1=ALU.mult),
             reads=[xkey, 'ntrs', 'gB'], writes=['xn'])
        for kc in range(8):
            S.op('pe', lambda e, kc=kc: e.transpose(out=psT[:, kc * 128:(kc + 1) * 128], in_=xn[:, kc * 128:(kc + 1) * 128],
                                                     identity=identB),
                 reads=['xn', 'cb'], writes=['psT'])
        S.op('act', lambda e: e.copy(out=hTdst, in_=psT[:, :].rearrange("p (a b) -> p a b", a=8)),
             reads=['psT'], writes=[hkey])

    epsc = cf[:, 1025:1026]

    for L in range(2):
        src_x = xin if L == 0 else xb
        ar.mark()
        win = ar.alloc([128, 8, NCI], BF16)
        wout_off = ar.top
        wout = ar.alloc([128, 8, D], BF16)
        gB = ar.alloc([128, 2, D])
        pp = ar.alloc([128, 80])
        pb = ar.alloc([128, 288])
        ar.mark()
        stg = [ar.alloc([128, 900]) for _ in range(2)]
        S.dma('sp', gB, gbd[L, 0:2].rearrange("a p d -> p a d"), writes=['gB'])
        S.dma('sp', pp, ppd[L], writes=['pp'])
        S.dma('sp', pb, pbd[L], writes=['pb'])
        load_weights_bf16(win, w_in[L], 8, NCI, stg, 'win', 900)
        load_weights_bf16(wout, w_out[L], 8, D, stg, 'wout', 512)
        S.barrier()
        ar.release()
        X = [ar.alloc([128, D]) for _ in range(2)]
        xn = ar.alloc([128, D], BF16)
        junk = xn
        hT = [ar.alloc([128, 8, 128], BF16) for _ in range(2)]
        ss = ar.alloc([128, 8])
        rs = ar.alloc([128, 8])
        mixT = ar.alloc([128, 8, 128], BF16)
        negA8 = ar.alloc([128, 8])
        S.op('act', lambda e: e.activation(out=negA8, in_=pb[:, 0:8], func=AF.Exp), reads=['pb'], writes=['negA8'])
        S.op('dve', lambda e: e.tensor_scalar(out=negA8, in0=negA8, scalar1=-1.0, scalar2=None, op0=ALU.mult),
             reads=['negA8'], writes=['negA8'])
        negsink = ar.alloc([128, 4])
        S.op('dve', lambda e: e.tensor_scalar(out=negsink, in0=pb[:, 20:24], scalar1=-1.0, scalar2=None, op0=ALU.mult),
             reads=['pb'], writes=['negsink'])

        mix = Mixers(nc, S, ar, locals())
        tiles = list(range(0, NT + 1)) + ([SMP] if do_sample else [])
        for ti, t in enumerate(tiles):
            b = ti % 2
            xt = X[b]
            if ti == 0:
                S.dma('sp', xt, src_x[t * 128:(t + 1) * 128, :], reads=[('x%d' % L, t)], writes=['X%d' % b])
            hb = rot('hT')
            norm_to_T(xt, 'X%d' % b, gB[:, 0, :], hT[hb], 'hT%d' % hb, junk, xn, ss[:, 0:1], rs[:, 0:1])
            if ti + 1 < len(tiles):
                t2 = tiles[ti + 1]
                S.dma('sp', X[1 - b], src_x[t2 * 128:(t2 + 1) * 128, :], reads=[('x%d' % L, t2)], writes=['X%d' % (1 - b)])
            if t == SMP:
                S.barrier()
            mix.tile(L, t, hT[hb], 'hT%d' % hb)
            if t == SMP:
                S.barrier()
                for rc in range(8):
                    for c0 in (0, 512):
                        S.dma('sp', mix.cstg[:, 0:512], w_out[L][rc * 128:(rc + 1) * 128, c0:c0 + 512], writes=['cstg'])
                        S.op('act', lambda e, rc=rc, c0=c0: e.copy(out=wout[:, rc, c0:c0 + 512], in_=mix.cstg[:, 0:512]),
                             reads=['cstg'], writes=['wout'])
            for half in range(2):
                for mc in range(8):
                    S.op('pe', lambda e, half=half, mc=mc: e.matmul(psA[:, half * 512:(half + 1) * 512], lhsT=mixT[:, mc, :],
                                                                    rhs=wout[:, mc, half * 512:(half + 1) * 512],
                                                                    start=(mc == 0), stop=(mc == 7)),
                         reads=['mixT', 'wout'], writes=['psA%d' % half])
            rms_rstd(psA[:, :], ['psA0', 'psA1'], D, junk, ss[:, 1:2], rs[:, 1:2], 'po')
            S.op('dve', lambda e: e.tensor_tensor(out=psA[:, :], in0=psA[:, :], in1=gB[:, 1, :], op=ALU.mult),
                 reads=['psA0', 'psA1', 'gB'], writes=['psA0', 'psA1'])
            S.op('dve', lambda e, xt=xt: e.scalar_tensor_tensor(out=xt, in0=psA[:, :], scalar=rs[:, 1:2], in1=xt,
                                                                op0=ALU.mult, op1=ALU.add),
                 reads=['psA0', 'psA1', 'pors', 'X%d' % b], writes=['X%d' % b])
            S.dma('sp', xa[t * 128:(t + 1) * 128, :], xt, reads=['X%d' % b], writes=[('xa%d' % L, t)])
        mix.finish(L)
        S.barrier()
        ar.release()

        ar.mark()
        wfi = ar.alloc([128, 8, 2 * DFF], BF16)
        wfo = ar.alloc([128, 22, D], BF16)
        gB = ar.alloc([128, 2, D])
        S.dma('sp', gB, gbd[L, 2:4].rearrange("a p d -> p a d"), writes=['gB'])
        ar.mark()
        stg = [ar.alloc([128, 1408]) for _ in range(2)]
        load_weights_bf16(wfi, w_fi[L], 8, 2 * DFF, stg, 'wfi', 1408)
        load_weights_bf16(wfo, w_fo[L], 22, D, stg, 'wfo', 1024)
        S.barrier()
        ar.release()
        NSUB = 2
        XS = [ar.alloc([128, D]) for _ in range(NSUB)]
        xn = ar.alloc([128, D], BF16)
        junk = xn
        hTm = [ar.alloc([128, 8, NSUB * 128], BF16) for _ in range(2)]
        h2T = ar.alloc([128, 22, NSUB * 128], BF16)
        sg = [ar.alloc([128, NSUB * 128]) for _ in range(2)]
        ss = ar.alloc([128, 8])
        rs = ar.alloc([128, 8])
        groups = [[0]]
        t = 1
        while t <= NT:
            groups.append(list(range(t, min(t + NSUB, NT + 1))))
            t += NSUB
        if do_sample:
            groups.append([SMP])
        if L == 1:
            groups = groups[1:]
        for grp in groups:
            n = len(grp) * 128
            hb = rot('hTm')
            for si, t in enumerate(grp):
                S.dma('sp', XS[si], xa[t * 128:(t + 1) * 128, :], reads=[('xa%d' % L, t)], writes=['XS%d' % si])
                norm_to_T(XS[si], 'XS%d' % si, gB[:, 0, :], hTm[hb][:, :, si * 128:(si + 1) * 128], 'hTm%d' % hb,
                          junk, xn, ss[:, 0:1], rs[:, 0:1])
            if L == 0 and grp[0] == 1:
                dbg_out('B_xs0', XS[0], ['XS0'])
                dbg_out('B_hT', hTm[hb], ['hTm%d' % hb], BF16)
            for fc in range(22):
                pg = rot('ffn_ps', 2)
                for which in range(2):
                    bank = psB[pg * 2 + which]
                    c0 = which * DFF + fc * 128
                    for kc in range(8):
                        S.op('pe', lambda e, bank=bank, c0=c0, kc=kc: e.matmul(bank[:, 0:n], lhsT=wfi[:, kc, c0:c0 + 128],
                                                                               rhs=hTm[hb][:, kc, 0:n],
                                                                               start=(kc == 0), stop=(kc == 7)),
                             reads=['wfi', 'hTm%d' % hb], writes=['psB%d' % (pg * 2 + which)])
                S.op('act', lambda e, pg=pg: e.activation(out=sg[pg][:, 0:n], in_=psB[pg * 2][:, 0:n], func=AF.Silu),
                     reads=['psB%d' % (pg * 2)], writes=['sg%d' % pg])
                S.op('dve', lambda e, pg=pg, fc=fc: e.tensor_tensor(out=h2T[:, fc, 0:n], in0=psB[pg * 2 + 1][:, 0:n],
                                                                     in1=sg[pg][:, 0:n], op=ALU.mult),
                     reads=['psB%d' % (pg * 2 + 1), 'sg%d' % pg], writes=['h2T'])
            for si, t in enumerate(grp):
                for half in range(2):
                    for fc in range(22):
                        S.op('pe', lambda e, half=half, fc=fc, si=si: e.matmul(
                            psA[:, half * 512:(half + 1) * 512], lhsT=h2T[:, fc, si * 128:(si + 1) * 128],
                            rhs=wfo[:, fc, half * 512:(half + 1) * 512], start=(fc == 0), stop=(fc == 21)),
                            reads=['h2T', 'wfo'], writes=['psA%d' % half])
                rms_rstd(psA[:, :], ['psA0', 'psA1'], D, junk, ss[:, 1:2], rs[:, 1:2], 'po')
                S.op('dve', lambda e: e.tensor_tensor(out=psA[:, :], in0=psA[:, :], in1=gB[:, 1, :], op=ALU.mult),
                     reads=['psA0', 'psA1', 'gB'], writes=['psA0', 'psA1'])
                S.op('dve', lambda e, si=si: e.scalar_tensor_tensor(out=XS[si], in0=psA[:, :], scalar=rs[:, 1:2], in1=XS[si],
                                                                    op0=ALU.mult, op1=ALU.add),
                     reads=['psA0', 'psA1', 'pors', 'XS%d' % si], writes=['XS%d' % si])
                if L == 0:
                    S.dma('sp', xb[t * 128:(t + 1) * 128, :], XS[si], reads=['XS%d' % si], writes=[('x1', t)])
                else:
                    S.dma('sp', y_o[(t - 1) * 128:t * 128, :], XS[si], reads=['XS%d' % si], writes=[('y', t)])
        S.barrier()
        ar.release()
    S.finish()
    return nc

def _bf16():
    import ml_dtypes
    return ml_dtypes.bfloat16


def _consts(NT):
    NTILE = NT + 2
    cf = np.zeros((128, 2560), np.float32)
    p = np.arange(128)[:, None]
    f = np.arange(128)[None, :]
    cf[:, 0:128] = np.eye(128)
    cf[:, 128:256] = (f >= p)
    cf[:, 256:384] = (f < p)
    Rm = np.zeros((128, 128), np.float32)
    for hb in (0, 64):
        for j in range(32):
            Rm[hb + 32 + j, hb + j] = -1.0
            Rm[hb + j, hb + 32 + j] = 1.0
    cf[:, 384:512] = Rm
    f2 = np.arange(256)[None, :]
    band = (f2 > p) & (f2 <= p + 128)
    cf[:, 512:768] = band
    cf[:, 768:1024] = band & (f2 >= 240)
    cf[:, 2304:2560] = band & (f2 >= 112)
    cf[:, 1024] = (np.arange(128) >= 112)
    cf[:, 1025] = 1e-6
    cf[:, 1026] = 1.0
    sb_ = (p // 8) == (f // 8)
    cf[:, 1152:1280] = (f >= p) & sb_
    cf[:, 1280:1408] = (f < p) & sb_
    cf[:, 1408:1536] = (f == (p // 8) * 8 + 7)
    m = np.zeros((128, 256), np.float32)
    m[:, 0:128] = (f > (p % 8))
    m[:, 128:256] = (f <= p) & sb_
    cf[:, 1536:1792] = m
    cf[:, 1792:1920] = sb_
    for s in range(16):
        cf[:, 1920 + s] = (np.arange(128) // 8 == s)
    cf[:, 2048:2176] = (p // 64) == (f // 64)
    cf[:, 2176:2304] = (p >= 64) & (f < 64)
    bf = _bf16()
    cb = np.zeros((128, 256), np.float32)
    cb[:, 0:128] = np.eye(128)
    cb[:, 128:256] = 1.0
    cb = cb.astype(bf)
    inv = (10000.0 ** (-np.arange(32, dtype=np.float32) / np.float32(32))).astype(np.float32)
    pos = np.zeros(NTILE * 128, np.float32)
    for t in range(NT + 1):
        pos[t * 128:(t + 1) * 128] = t * 128 + np.arange(128) - 112
    pos[(NT + 1) * 128:] = PAST_LEN + (np.arange(128) % 8)
    ang = (pos[None, :] * inv[:, None]).astype(np.float32)
    ang = np.tile(ang, (4, 1)).astype(np.float64)
    rope = np.stack([np.cos(ang), np.sin(ang)]).astype(np.float32)
    return cf, cb, rope


def _prep_shared(inp, NT):
    f32 = np.float32
    idx = list(range(0, 2048)) + list(range(2056, 3080))
    idx += [3084 + i for i in list(range(0, 64)) + list(range(128, 192)) + list(range(64, 128)) + list(range(192, 256))]
    idx += list(range(3340, 3596)) + list(range(2048, 2056)) + list(range(3080, 3084))
    w_in = np.zeros((2, D, NCI), f32)
    w_in[:, :, :3596] = inp["w_in"][:, :, idx]
    ridx = list(range(0, 768)) + [768 + i for i in list(range(0, 64)) + list(range(128, 192)) + list(range(64, 128)) + list(range(192, 256))]
    w_out = np.ascontiguousarray(inp["w_out"][:, ridx, :])
    gb = np.stack([inp["g_pre_mix"], inp["g_post_mix"], inp["g_pre_ffn"], inp["g_post_ffn"]], axis=1)
    gb = np.ascontiguousarray(np.broadcast_to(gb[:, :, None, :], (2, 4, 128, D))).astype(f32)
    pp = np.zeros((2, 128, 80), f32)
    for l in range(2):
        for ci in range(12):
            pp[l, :, ci * 4:(ci + 1) * 4] = inp["dn_conv_w"][l][:, ci * 128:(ci + 1) * 128].T
        for ci in range(12, 18):
            pp[l, :, ci * 4:(ci + 1) * 4] = inp["ssm_conv_w"][l][:, (ci - 12) * 128:(ci - 11) * 128].T
        for j in range(6):
            pp[l, :, 72 + j] = inp["ssm_conv_b"][l][j * 128:(j + 1) * 128]
        pp[l, :, 78] = inp["dn_norm_w"][l]
    pb = np.zeros((2, 128, 288), f32)
    for l in range(2):
        row = np.zeros(288, f32)
        row[0:4] = inp["dn_a_log"][l]
        row[4:8] = inp["ssm_a_log"][l]
        row[8:12] = inp["dn_dt_bias"][l]
        row[12:16] = inp["ssm_dt_bias"][l]
        row[16:20] = inp["ssm_d"][l]
        row[20:24] = inp["swa_sinks"][l]
        row[32:288] = inp["ssm_norm_w"][l]
        pb[l] = row[None, :]
    cf, cb, rope = _consts(NT)
    return dict(w_in=w_in, w_out=w_out, w_fi=np.ascontiguousarray(inp["w_ffn_in"], f32),
                w_fo=np.ascontiguousarray(inp["w_ffn_out"], f32), gb=gb, pp=pp, pb=pb, cf=cf, cb=cb, rope=rope)


_CACHE = {}
LAST_RES = None
STAGE = 99
DO_SAMPLE = True


def kernel(**inputs):
    inp = {k: np.asarray(v) for k, v in inputs.items()}
    NT = inp["x_prompt"].shape[1] // 128
    shared = _prep_shared(inp, NT)
    key = (NT, STAGE, DO_SAMPLE)
    if key not in _CACHE:
        _CACHE[key] = build_program(NT, do_sample=DO_SAMPLE, stage=STAGE)
    nc = _CACHE[key]
    in_maps = []
    for c in range(8):
        b = c // 4
        xin = np.zeros(((NT + 2) * 128, D), np.float32)
        xin[112:128] = inp["meta_tokens"]
        xin[128:(NT + 1) * 128] = inp["x_prompt"][b]
        xin[(NT + 1) * 128:] = inp["x_sample"][16 * c:16 * c + 16].reshape(128, D)
        m = dict(shared)
        m["xin"] = xin
        if DO_SAMPLE:
            sl = slice(16 * c, 16 * c + 16)
            m["sdn"] = np.ascontiguousarray(inp["state_dn"][:, sl])
            m["sdnc"] = np.ascontiguousarray(inp["state_dn_conv"][:, sl])
            m["sssm"] = np.ascontiguousarray(inp["state_ssm"][:, sl])
            m["sssmc"] = np.ascontiguousarray(inp["state_ssm_conv"][:, sl])
            m["ck"] = np.ascontiguousarray(inp["cache_swa_k"][:, sl].reshape(2, 16, 128, 128))
            m["cv"] = np.ascontiguousarray(inp["cache_swa_v"][:, sl].reshape(2, 16, 128, 128))
        in_maps.append(m)
    res = run_bass_kernel_spmd(nc, in_maps, core_ids=list(range(8))).results
    global LAST_RES
    LAST_RES = res
    f32 = np.float32
    y_p = np.stack([res[4 * b]["y"][:NT * 128] for b in range(2)]).astype(f32)
    y_s = np.concatenate([res[c]["y"][NT * 128:].reshape(16, 8, D) for c in range(8)]).astype(f32)

    def pst(name, shape):
        return np.stack([res[4 * b][name] for b in range(2)], axis=1).reshape(shape).astype(f32)

    dn_p = pst("dn_p", (2, 2, 4, 128, 128))
    dnc_p = pst("dnc_p", (2, 2, 3, 1536))
    ssm_p = pst("ssm_p", (2, 2, 4, 64, 128))
    ssmc_p = pst("ssmc_p", (2, 2, 3, 768))
    k_p = pst("k_p", (2, 2, 128, 2, 64))
    v_p = pst("v_p", (2, 2, 128, 2, 64))
    if DO_SAMPLE:
        def sst(name, shape):
            return np.concatenate([res[c][name] for c in range(8)], axis=1).reshape(shape).astype(f32)
        dn_s = sst("dn_s", (2, 128, 4, 128, 128))
        dnc_s = sst("dnc_s", (2, 128, 3, 1536))
        ssm_s = sst("ssm_s", (2, 128, 4, 64, 128))
        ssmc_s = sst("ssmc_s", (2, 128, 3, 768))
        k_s = sst("k_s", (2, 128, 128, 2, 64))
        v_s = sst("v_s", (2, 128, 128, 2, 64))
    else:
        z = lambda *s: np.zeros(s, f32)
        dn_s, dnc_s, ssm_s, ssmc_s = z(2, 128, 4, 128, 128), z(2, 128, 3, 1536), z(2, 128, 4, 64, 128), z(2, 128, 3, 768)
        k_s, v_s = z(2, 128, 128, 2, 64), z(2, 128, 128, 2, 64)
    return (y_p, y_s, dn_p, dnc_p, ssm_p, ssmc_p, k_p, v_p, dn_s, dnc_s, ssm_s, ssmc_s, k_s, v_s)
```
